# Optimizing a Trainium2 kernel written in Bass

```python
import jax, jax.numpy as jnp
from jax import lax
import numpy as np

D_MODEL = 2048
BATCH = 2
SEQ = 4096
DEPTH = 2

CHUNK = 64
EPS = 1e-6
N_EVEN = (DEPTH + 1) // 2
N_ODD = DEPTH // 2

GMLP_BLOCK = 128
A_HEADS = 8
A_HEAD_DIM = D_MODEL // 2 // A_HEADS
D_A = A_HEADS * A_HEAD_DIM

POOL_WINDOWS = (2, 4, 8, 16)
B_GROUPS = len(POOL_WINDOWS)
B_GROUP_DIM = D_MODEL // 2 // B_GROUPS
D_B = B_GROUPS * B_GROUP_DIM

C_HEADS = 16
Q_LORA = 512
KV_LORA = 512
NOPE_DIM = 128
ROPE_DIM = 64
V_DIM = 128
ROPE_THETA = 10000.0
Q_BLOCK = 128

D_FF = -(-8 * D_MODEL // (3 * 256)) * 256

kernel_name = "hybrid_gmlp_pool_mla_stream_trunk"


def rmsnorm(x, g):
    xf = x.astype(jnp.float32)
    y = xf * lax.rsqrt(jnp.mean(xf * xf, axis=-1, keepdims=True) + EPS)
    return (y * g.astype(jnp.float32)).astype(x.dtype)


def apply_rope(x, cos, sin):
    xf = x.astype(jnp.float32)
    half = x.shape[-1] // 2
    x1, x2 = xf[..., :half], xf[..., half:]
    return jnp.concatenate([x1 * cos - x2 * sin, x2 * cos + x1 * sin], axis=-1).astype(x.dtype)


def gmlp_mixer(uv, g_v, w_s, b_s):
    B_, S_, _ = uv.shape
    uv = jax.nn.gelu(uv)
    u, v = uv[..., :D_A], uv[..., D_A:]
    v = rmsnorm(v, g_v)
    nb = S_ // GMLP_BLOCK
    v = v.reshape(B_, nb, GMLP_BLOCK, A_HEADS, A_HEAD_DIM)
    pos_chunk = jnp.arange(GMLP_BLOCK) // CHUNK
    mask = (pos_chunk[None, :] <= pos_chunk[:, None]).astype(w_s.dtype)
    w = w_s * mask[None]
    mix = jnp.einsum('hts,bnshd->bnthd', w, v) + jnp.swapaxes(b_s, 0, 1)[None, None, :, :, None]
    return u * mix.reshape(B_, S_, D_A)


def multi_scale_pool(z, w_pool, scale):
    B_, S_, _ = z.shape
    zg = z.reshape(B_, S_, B_GROUPS, B_GROUP_DIM)
    cs = jnp.cumsum(zg.astype(jnp.float32), axis=1)
    cs = jnp.pad(cs, ((0, 0), (1, 0), (0, 0), (0, 0)))
    t1 = jnp.arange(1, S_ + 1)
    outs = []
    for g, win in enumerate(POOL_WINDOWS):
        c = cs[:, :, g]
        lo = jnp.pad(c, ((0, 0), (win - 1, 0), (0, 0)))[:, :S_]
        cnt = jnp.minimum(t1, win).astype(jnp.float32)[None, :, None]
        outs.append((c[:, 1:] - lo) / cnt)
    pooled = jnp.stack(outs, axis=2)
    d = (pooled - zg.astype(jnp.float32)).astype(z.dtype)
    y = jnp.einsum('bsgc,gcd->bsgd', d, w_pool).reshape(B_, S_, D_B)
    return y * scale


def mla(h, w_in_c, g_cq, g_ckv, w_uq, w_ukv, w_out_c, cos, sin):
    B_, S_, _ = h.shape
    p = h @ w_in_c
    c_q = p[..., :Q_LORA]
    c_kv = p[..., Q_LORA:Q_LORA + KV_LORA]
    k_rope = apply_rope(p[..., Q_LORA + KV_LORA:], cos, sin)
    q = (rmsnorm(c_q, g_cq) @ w_uq).reshape(B_, S_, C_HEADS, NOPE_DIM + ROPE_DIM)
    q_nope = q[..., :NOPE_DIM]
    q_rope = apply_rope(q[..., NOPE_DIM:], cos[:, None, :], sin[:, None, :])
    kv = (rmsnorm(c_kv, g_ckv) @ w_ukv).reshape(B_, S_, C_HEADS, NOPE_DIM + V_DIM)
    k_nope, v = kv[..., :NOPE_DIM], kv[..., NOPE_DIM:]
    nq = S_ // Q_BLOCK
    key_chunk = jnp.arange(S_) // CHUNK
    sm_scale = (NOPE_DIM + ROPE_DIM) ** -0.5

    def to_blocks(t):
        return jnp.moveaxis(t.reshape(B_, nq, Q_BLOCK, *t.shape[2:]), 1, 0)

    def attend(args):
        qn, qr, blk = args
        s = (jnp.einsum('bqhd,bkhd->bhqk', qn, k_nope)
             + jnp.einsum('bqhr,bkr->bhqk', qr, k_rope)).astype(jnp.float32) * sm_scale
        q_chunk = (blk * Q_BLOCK + jnp.arange(Q_BLOCK)) // CHUNK
        mask = key_chunk[None, :] <= q_chunk[:, None]
        s = jnp.where(mask[None, None], s, -jnp.inf)
        pr = jax.nn.softmax(s, axis=-1).astype(v.dtype)
        return jnp.einsum('bhqk,bkhd->bqhd', pr, v)

    o = lax.map(attend, (to_blocks(q_nope), to_blocks(q_rope), jnp.arange(nq)))
    o = jnp.moveaxis(o, 0, 1).reshape(B_, S_, C_HEADS * V_DIM)
    return o @ w_out_c


def swiglu(h, w_gate, w_up, w_down):
    return (jax.nn.silu(h @ w_gate) * (h @ w_up)) @ w_down


def setup_inputs(seed: int = 0) -> dict:
    key = jax.random.key(seed)
    ks = jax.random.split(key, 24)
    f32 = jnp.float32

    def w(k, shape, fan_in):
        return jax.random.normal(k, shape, f32) * (fan_in ** -0.5)

    def gain(k, shape):
        return 1.0 + 0.05 * jax.random.normal(k, shape, f32)

    return {
        'x': jax.random.normal(ks[0], (BATCH, SEQ, D_MODEL), f32),
        'g_mix': gain(ks[1], (DEPTH, D_MODEL)),
        'g_ffn': gain(ks[2], (DEPTH, D_MODEL)),
        'g_final': gain(ks[3], (D_MODEL,)),
        'w_in_ab': w(ks[4], (N_EVEN, D_MODEL, 2 * D_A + D_B), D_MODEL),
        'g_v': gain(ks[5], (N_EVEN, D_A)),
        'w_s': w(ks[6], (N_EVEN, A_HEADS, GMLP_BLOCK, GMLP_BLOCK), GMLP_BLOCK),
        'b_s': 1.0 + 0.1 * jax.random.normal(ks[7], (N_EVEN, A_HEADS, GMLP_BLOCK), f32),
        'w_pool': w(ks[8], (N_EVEN, B_GROUPS, B_GROUP_DIM, B_GROUP_DIM), B_GROUP_DIM),
        'pool_scale': 1.0 + 0.1 * jax.random.normal(ks[9], (N_EVEN, D_B), f32),
        'w_out_ab': w(ks[10], (N_EVEN, D_A + D_B, D_MODEL), D_A + D_B),
        'w_in_c': w(ks[11], (N_ODD, D_MODEL, Q_LORA + KV_LORA + ROPE_DIM), D_MODEL),
        'g_cq': gain(ks[12], (N_ODD, Q_LORA)),
        'g_ckv': gain(ks[13], (N_ODD, KV_LORA)),
        'w_uq': w(ks[14], (N_ODD, Q_LORA, C_HEADS * (NOPE_DIM + ROPE_DIM)), Q_LORA),
        'w_ukv': w(ks[15], (N_ODD, KV_LORA, C_HEADS * (NOPE_DIM + V_DIM)), KV_LORA),
        'w_out_c': w(ks[16], (N_ODD, C_HEADS * V_DIM, D_MODEL), C_HEADS * V_DIM),
        'w_gate': w(ks[17], (DEPTH, D_MODEL, D_FF), D_MODEL),
        'w_up': w(ks[18], (DEPTH, D_MODEL, D_FF), D_MODEL),
        'w_down': w(ks[19], (DEPTH, D_FF, D_MODEL), D_FF),
    }


def reference(x, g_mix, g_ffn, g_final, w_in_ab, g_v, w_s, b_s, w_pool, pool_scale, w_out_ab,
              w_in_c, g_cq, g_ckv, w_uq, w_ukv, w_out_c, w_gate, w_up, w_down):
    S_ = x.shape[1]
    pos = jnp.arange(S_, dtype=jnp.float32)
    inv_freq = ROPE_THETA ** (-jnp.arange(0, ROPE_DIM, 2, dtype=jnp.float32) / ROPE_DIM)
    ang = pos[:, None] * inv_freq[None, :]
    cos, sin = jnp.cos(ang), jnp.sin(ang)
    for layer in range(DEPTH):
        i = layer // 2
        h = rmsnorm(x, g_mix[layer])
        if layer % 2 == 0:
            p = h @ w_in_ab[i]
            a = gmlp_mixer(p[..., :2 * D_A], g_v[i], w_s[i], b_s[i])
            b = multi_scale_pool(p[..., 2 * D_A:], w_pool[i], pool_scale[i])
            mix = jnp.concatenate([a, b], axis=-1) @ w_out_ab[i]
        else:
            mix = mla(h, w_in_c[i], g_cq[i], g_ckv[i], w_uq[i], w_ukv[i], w_out_c[i], cos, sin)
        x = x + mix
        x = x + swiglu(rmsnorm(x, g_ffn[layer]), w_gate[layer], w_up[layer], w_down[layer])
    return rmsnorm(x, g_final)
```

```python
import numpy as np
from contextlib import ExitStack
import concourse.bass as bass
import concourse.mybir as mybir
from concourse.bass_utils import run_bass_kernel_spmd

F32 = mybir.dt.float32
BF16 = mybir.dt.bfloat16
AF = mybir.ActivationFunctionType
ALU = mybir.AluOpType

D = 2048
KC = 16
T = 1024
NT = 8
DFF = 5632
EPS = 1e-6
NCORES = 8
SM_SCALE = 192.0 ** -0.5


class Reg:
    __slots__ = ("name", "w", "rd", "cnt")

    def __init__(self, name):
        self.name = name
        self.w = None
        self.rd = {}
        self.cnt = 0


class Op:
    __slots__ = ("eng", "fn", "waits", "signal", "idx", "dma", "ticket", "inc")


class Prog:
    ENG = ("pe", "act", "dve", "pool", "sp")

    def __init__(self):
        self.streams = {e: [] for e in self.ENG}
        self.known = {e: {} for e in self.ENG}
        self.bar = None
        self.alltok = {}
        self.dma_regs = []
        self.nreg = 0

    def R(self, name="r"):
        self.nreg += 1
        return Reg(f"{name}{self.nreg}")

    def Rs(self, name, *dims):
        if len(dims) == 1:
            return [self.R(name) for _ in range(dims[0])]
        return [self.Rs(name, *dims[1:]) for _ in range(dims[0])]

    def add(self, eng, fn, rd=(), wr=(), dma=None, inc=16):
        st = self.streams[eng]
        op = Op()
        op.inc = inc
        op.idx = len(st)
        op.eng = eng
        op.fn = fn
        op.dma = dma
        op.signal = False
        op.ticket = 0
        deps = {}

        def need(tok):
            key = tok[0]
            if key in deps and deps[key][1] >= tok[1]:
                return
            deps[key] = tok

        if self.bar is not None:
            need(self.bar)
        for r in rd:
            if r.w is not None:
                need(r.w)
        for r in wr:
            if r.w is not None:
                need(r.w)
            for tok in r.rd.values():
                need(tok)
        known = self.known[eng]
        waits = []
        for key, tok in deps.items():
            if key == ("c", "pe") and eng == "pe" and dma is None:
                continue
            if known.get(key, -1) >= tok[1]:
                continue
            known[key] = tok[1]
            waits.append(tok)
            if tok[2] is not None:
                tok[2].signal = True
        op.waits = waits
        if dma is not None:
            if dma.cnt == 0:
                self.dma_regs.append(dma)
            dma.cnt += inc
            mytok = (("d", dma), dma.cnt, None)
        else:
            mytok = (("c", eng), op.idx, op)
        for r in rd:
            r.rd[mytok[0]] = mytok
        for r in wr:
            r.w = mytok
            r.rd = {}
        self.alltok[mytok[0]] = mytok
        st.append(op)
        return op

    def barrier(self, fn):
        st = self.streams["dve"]
        op = Op()
        op.idx = len(st)
        op.eng = "dve"
        op.fn = fn
        op.dma = None
        op.signal = True
        op.ticket = 0
        op.inc = 16
        known = self.known["dve"]
        waits = []
        for key, tok in self.alltok.items():
            if known.get(key, -1) >= tok[1]:
                continue
            known[key] = tok[1]
            waits.append(tok)
            if tok[2] is not None:
                tok[2].signal = True
        op.waits = waits
        st.append(op)
        mytok = (("c", "dve"), op.idx, op)
        self.bar = mytok
        self.alltok[mytok[0]] = mytok
        for e in self.ENG:
            if e != "dve":
                for key, tok in self.alltok.items():
                    if key != ("c", "dve"):
                        self.known[e][key] = max(self.known[e].get(key, -1), tok[1])

    def finish(self, eng, regs):
        op = Op()
        op.idx = len(self.streams[eng])
        op.eng = eng
        op.fn = None
        op.dma = None
        op.signal = False
        op.ticket = 0
        op.inc = 16
        op.waits = [(("d", r), r.cnt, None) for r in regs]
        self.streams[eng].append(op)

    def emit(self, nc, es):
        sems = {}
        for e in ("pe", "act", "dve", "pool"):
            sems[("c", e)] = es.enter_context(nc.semaphore("s_" + e))
        for r in self.dma_regs:
            sems[("d", r)] = es.enter_context(nc.semaphore("d_" + r.name))
        for e in ("pe", "act", "dve", "pool"):
            n = 0
            for op in self.streams[e]:
                if op.dma is None and op.signal:
                    n += 1
                    op.ticket = n
        streams = self.streams

        def run(ename, e):
            for op in streams[ename]:
                for key, val, p in op.waits:
                    v = p.ticket if p is not None else val
                    e.wait_ge(sems[key], v)
                if op.fn is None:
                    continue
                ins = op.fn(e)
                if op.dma is not None:
                    if op.inc == 1:
                        ins.then_inc(sems[("d", op.dma)])
                    else:
                        ins.then_inc(sems[("d", op.dma)], op.inc)
                elif op.signal:
                    ins.then_inc(sems[("c", ename)], 1)

        with nc.Block() as block:
            block.tensor(lambda e: run("pe", e))
            block.scalar(lambda e: run("act", e))
            block.vector(lambda e: run("dve", e))
            block.gpsimd(lambda e: run("pool", e))
            block.sync(lambda e: run("sp", e))


class KB:
    def __init__(self):
        self.nc = bass.Bass("TRN2", target_bir_lowering=False)
        self.P = Prog()
        self.es = ExitStack()
        self.bankrr = 0

    def din(self, name, shape, dt=F32):
        return self.nc.dram_tensor(name, list(shape), dt, kind="ExternalInput").ap()

    def dout(self, name, shape, dt=F32):
        return self.nc.dram_tensor(name, list(shape), dt, kind="ExternalOutput").ap()

    def sb(self, name, shape, dt):
        return self.es.enter_context(self.nc.sbuf_tensor(name, list(shape), dt))

    def pst(self, name):
        return self.es.enter_context(self.nc.psum_tensor(name, [128, 512], F32))

    def common(self, nrstd=2):
        P = self.P
        self.ps = [self.pst(f"ps{i}") for i in range(8)]
        self.psr = [P.R("psr") for _ in range(8)]
        self.slots = [self.sb(f"slot{i}", [128, 8192], BF16) for i in range(3)]
        self.slotr = [P.R("slot") for _ in range(3)]
        self.slot_rr = 0
        self.slot_n = 3
        self.ones = self.sb("ones", [128, 3, 128], BF16)
        self.onesr = P.R("ones")
        self.bscr = self.sb("bscr", [128, 8], F32)
        ones = self.ones
        P.add("dve", lambda e: e.memset(ones[:, 0, :], 1.0 / 2048), wr=[self.onesr])
        P.add("dve", lambda e: e.memset(ones[:, 1, :], 1.0 / 512), wr=[self.onesr])
        P.add("dve", lambda e: e.memset(ones[:, 2, :], 1.0), wr=[self.onesr])
        self.sq = [self.sb(f"sq{i}", [128, 512], BF16) for i in range(2)]
        self.sqr = [P.R("sq") for _ in range(2)]
        self.sq_rr = 0
        self.rstd = [self.sb(f"rstd{i}", [128, 512], F32) for i in range(nrstd)]
        self.rstdr = [P.R("rstd") for _ in range(nrstd)]
        self.nrstd = nrstd
        self.rstd_rr = 0

    def bank(self, lo=0, hi=8):
        b = lo + (self.bankrr % (hi - lo))
        self.bankrr += 1
        return b

    def barrier(self):
        bscr = self.bscr
        self.P.barrier(lambda e: e.memset(bscr[:, 0:1], 0.0))

    def next_slot(self):
        s = self.slot_rr % self.slot_n
        self.slot_rr += 1
        return s

    def load_w(self, src_ap, view):
        s = self.next_slot()
        slot = self.slots[s]
        self.P.add("pool", lambda e: e.dma_start(out=view(slot), in_=src_ap),
                   wr=[self.slotr[s]], dma=self.slotr[s])
        return s

    def norm_fm(self, src, src_regs, nchunk, pieces, gcols, dst, dst_regs, ones_idx, dst_is_f32=False,
                act_square=True):
        P = self.P
        ones = self.ones
        for (c0, n, pi) in pieces:
            b = self.bank()
            ps = self.ps[b]
            for kc in range(nchunk):
                qi = self.sq_rr % 2
                self.sq_rr += 1
                sq = self.sq[qi]
                P.add("act", lambda e, sq=sq, kc=kc, c0=c0, n=n: e.activation(
                    sq[:, 0:n], src[:, kc, c0:c0 + n], AF.Square),
                    rd=[src_regs[kc][pi]], wr=[self.sqr[qi]])
                P.add("pe", lambda e, sq=sq, kc=kc, n=n, ps=ps: e.matmul(
                    ps[:, 0:n], ones[:, ones_idx, :], sq[:, 0:n], start=(kc == 0), stop=(kc == nchunk - 1)),
                    rd=[self.sqr[qi], self.onesr], wr=[self.psr[b]])
            ri = self.rstd_rr % self.nrstd
            self.rstd_rr += 1
            rstd = self.rstd[ri]
            P.add("act", lambda e, rstd=rstd, ps=ps, n=n: e.activation(
                rstd[:, 0:n], ps[:, 0:n], AF.Sqrt, bias=EPS, scale=1.0),
                rd=[self.psr[b]], wr=[self.rstdr[ri]])
            P.add("dve", lambda e, rstd=rstd, n=n: e.reciprocal(rstd[:, 0:n], rstd[:, 0:n]),
                  rd=[self.rstdr[ri]], wr=[self.rstdr[ri]])
            for kc in range(nchunk):
                P.add("dve", lambda e, rstd=rstd, kc=kc, c0=c0, n=n: e.scalar_tensor_tensor(
                    dst[:, kc, c0:c0 + n], src[:, kc, c0:c0 + n], gcols[:, kc:kc + 1], rstd[:, 0:n],
                    ALU.mult, ALU.mult),
                    rd=[src_regs[kc][pi], self.rstdr[ri], self.constr], wr=[dst_regs[kc][pi]])

    def ffn(self, w_gate, w_up, w_down, layer):
        P = self.P
        X, Xr, H, Hr = self.X, self.Xr, self.H, self.Hr
        ACTB, ACTBr = self.ACTB, self.ACTBr
        SG, SGr = self.SG, self.SGr
        NG = DFF // 512
        wg = w_gate[layer].rearrange("(kc p) n -> p kc n", p=128)
        wu = w_up[layer].rearrange("(kc p) n -> p kc n", p=128)
        wd = w_down[layer].rearrange("(kc p) n -> p kc n", p=128)
        v16 = lambda s: s[:, :].rearrange("p (a b) -> p a b", a=16)
        v4 = lambda s: s[:, :].rearrange("p (a b) -> p a b", a=4)
        sg_rr = 0

        def gate_up(fg):
            nonlocal sg_rr
            sgl = self.load_w(wg[:, :, fg * 512:(fg + 1) * 512], v16)
            sul = self.load_w(wu[:, :, fg * 512:(fg + 1) * 512], v16)
            ab = fg % 2
            for fc in range(4):
                bg = [self.bank(0, 4), self.bank(0, 4)]
                for half in range(2):
                    for kc in range(KC):
                        P.add("pe", lambda e, kc=kc, half=half, fc=fc, b=bg[half], sl=self.slots[sgl]: e.matmul(
                            self.ps[b][:, :], v16(sl)[:, kc, fc * 128:(fc + 1) * 128],
                            H[:, kc, half * 512:(half + 1) * 512], start=(kc == 0), stop=(kc == KC - 1)),
                            rd=[self.slotr[sgl], Hr[kc][half]], wr=[self.psr[bg[half]]])
                sgi = []
                for half in range(2):
                    si = sg_rr % 2
                    sg_rr += 1
                    sgi.append(si)
                    P.add("act", lambda e, b=bg[half], si=si: e.activation(SG[si][:, :], self.ps[b][:, :], AF.Silu),
                          rd=[self.psr[bg[half]]], wr=[SGr[si]])
                bu = [self.bank(0, 4), self.bank(0, 4)]
                for half in range(2):
                    for kc in range(KC):
                        P.add("pe", lambda e, kc=kc, half=half, fc=fc, b=bu[half], sl=self.slots[sul]: e.matmul(
                            self.ps[b][:, :], v16(sl)[:, kc, fc * 128:(fc + 1) * 128],
                            H[:, kc, half * 512:(half + 1) * 512], start=(kc == 0), stop=(kc == KC - 1)),
                            rd=[self.slotr[sul], Hr[kc][half]], wr=[self.psr[bu[half]]])
                for half in range(2):
                    P.add("dve", lambda e, b=bu[half], si=sgi[half], fc=fc, half=half, ab=ab: e.tensor_tensor(
                        ACTB[ab][:, fc, half * 512:(half + 1) * 512], SG[si][:, :], self.ps[b][:, :], ALU.mult),
                        rd=[self.psr[bu[half]], SGr[sgi[half]]], wr=[ACTBr[ab][fc][half]])

        def down(fg):
            sdl = self.load_w(wd[:, fg * 4:(fg + 1) * 4, :], v4)
            ab = fg % 2
            for oc in range(KC):
                for half in range(2):
                    b = self.bank(4, 8)
                    for kc in range(4):
                        P.add("pe", lambda e, kc=kc, half=half, oc=oc, b=b, sl=self.slots[sdl], ab=ab: e.matmul(
                            self.ps[b][:, :], v4(sl)[:, kc, oc * 128:(oc + 1) * 128],
                            ACTB[ab][:, kc, half * 512:(half + 1) * 512], start=(kc == 0), stop=(kc == 3)),
                            rd=[self.slotr[sdl], ACTBr[ab][kc][half]], wr=[self.psr[b]])
                    P.add("dve", lambda e, b=b, oc=oc, half=half: e.tensor_tensor(
                        X[:, oc, half * 512:(half + 1) * 512], X[:, oc, half * 512:(half + 1) * 512],
                        self.ps[b][:, :], ALU.add),
                        rd=[self.psr[b], Xr[oc][half]], wr=[Xr[oc][half]])

        gate_up(0)
        for fg in range(NG):
            if fg + 1 < NG:
                gate_up(fg + 1)
            down(fg)


CA = dict(GM0=0, GF0=16, GM1=32, PSC=48, GCKV=56, INVC=64, N=128)


def build_A(debug=False):
    k = KB()
    nc, P = k.nc, k.P
    xT = k.din("xT", [D, T])
    xhT = k.din("xhT", [D, 128])
    consts = k.din("consts", [128, CA["N"]])
    gvbs = k.din("gvbs", [128, 2048])
    cossin = k.din("cossin", [128, 2048])
    w_in_ab = k.din("w_in_ab", [D, 3072])
    w_sT = k.din("w_sT", [8, 128, 128])
    w_pool = k.din("w_pool", [4, 256, 256])
    w_out_ab = k.din("w_out_ab", [D, D])
    w_gate = k.din("w_gate", [1, D, DFF])
    w_up = k.din("w_up", [1, D, DFF])
    w_down = k.din("w_down", [1, DFF, D])
    w_ckv = k.din("w_ckv", [D, 512])
    w_kr = k.din("w_kr", [D, 128])
    x1T = k.dout("x1T", [D, T])
    latc = k.dout("latc", [512, T])
    latr = k.dout("latr", [64, T])

    k.common(1)
    X = k.X = k.sb("X", [128, KC, T], F32)
    Xr = k.Xr = P.Rs("X", KC, 2)
    H = k.H = k.sb("H", [128, KC, T], BF16)
    Hr = k.Hr = P.Rs("H", KC, 2)
    HH = k.sb("HH", [128, KC, 128], BF16)
    HHr = P.Rs("HH", KC, 1)
    C = k.sb("C", [128, CA["N"]], F32)
    k.constr = P.R("const")
    U = k.sb("U", [128, 8, T], BF16)
    Ur = P.Rs("U", 8, NT)
    BIG = k.sb("BIG", [128, 8192], BF16)
    bigr = P.R("BIG")
    BT = U
    BTr = P.Rs("BT", 8, 2)
    S2F = k.slots[2][:, :].bitcast(F32)
    GVBS = S2F[:, 0:2048]
    gvbsr = P.R("GVBS")
    Z = S2F[:, 2048:2048 + 576].rearrange("p (i t) -> p i t", i=4)
    Zr = P.R("Z")
    ZS = [S2F[:, 2624 + 576 * i:2624 + 576 * (i + 1)].rearrange("p (i t) -> p i t", i=4) for i in range(2)]
    ZSr = [P.R("ZS") for _ in range(2)]
    WSWP = k.sb("WSWP", [128, 2048], BF16)
    WST = WSWP[:, 0:1024].rearrange("p (h t) -> p h t", h=8)
    WP = WSWP[:, :].rearrange("p (g c d) -> p g c d", g=4, c=2)
    WSTr = P.R("WSWP")
    WPr = WSTr
    SSQ = k.sb("SSQ", [128, 16], F32)
    SSQr = P.Rs("SSQ", 16)
    RV = k.sb("RV", [128, 8], F32)
    RVr = P.R("RV")
    TG = [k.sb(f"TG{i}", [128, 512], F32) for i in range(2)]
    TGr = [P.R("TG") for _ in range(2)]
    k.SG = TG
    k.SGr = TGr
    TMP, TMPr = TG, TGr

    xr = xT.rearrange("(kc p) t -> p kc t", p=128)
    Xin = P.Rs("Xin", 4)
    for q in range(4):
        P.add("sp", lambda e, q=q: e.dma_start(out=X[:, 4 * q:4 * q + 4, :], in_=xr[:, 4 * q:4 * q + 4, :]),
              wr=[Xr[kc][h] for kc in range(4 * q, 4 * q + 4) for h in range(2)] + [Xin[q]], dma=Xin[q])
    P.add("sp", lambda e: e.dma_start(out=C[:, :], in_=consts), wr=[k.constr], dma=k.constr)
    XHt = BIG[:, 0:4096].bitcast(F32).rearrange("p (a b) -> p a b", a=KC)
    XHr = [[bigr] for _ in range(KC)]
    XHd = P.R("XHd")
    P.add("sp", lambda e: e.dma_start(out=XHt, in_=xhT.rearrange("(kc p) t -> p kc t", p=128)),
          wr=[bigr, XHd], dma=XHd)
    P.add("sp", lambda e: e.dma_start(out=GVBS, in_=gvbs), wr=[gvbsr], dma=gvbsr)
    P.add("pool", lambda e: e.dma_start(out=WST, in_=w_sT.rearrange("h s t -> s h t")),
          wr=[WSTr], dma=WSTr)
    P.add("dve", lambda e: e.memset(WST[64:128, :, 0:64], 0.0), wr=[WSTr])

    gm0 = C[:, CA["GM0"]:CA["GM0"] + 16]
    k.norm_fm(X, Xr, KC, [(0, 512, 0), (512, 512, 1)], gm0, H, Hr, 0)
    k.norm_fm(XHt, XHr, KC, [(0, 128, 0)], gm0, HH, HHr, 0)

    v16 = lambda s: s[:, :].rearrange("p (a b) -> p a b", a=16)
    v8 = lambda s: s[:, 0:4096].rearrange("p (a b) -> p a b", a=8)
    wab = w_in_ab.rearrange("(kc p) n -> p kc n", p=128)
    VG = BIG[:, :].rearrange("p (i c) -> p i c", i=NT)
    VGr = P.Rs("VG", NT)
    k.slot_n = 2

    for sblk in range(2):
        sl = k.load_w(wab[:, :, sblk * 512:(sblk + 1) * 512], v16)
        for oc in range(4):
            hd = sblk * 4 + oc
            for half in range(2):
                b = k.bank()
                for kc in range(KC):
                    P.add("pe", lambda e, kc=kc, half=half, oc=oc, b=b, sl=sl: e.matmul(
                        k.ps[b][:, :], v16(k.slots[sl])[:, kc, oc * 128:(oc + 1) * 128],
                        H[:, kc, half * 512:(half + 1) * 512], start=(kc == 0), stop=(kc == KC - 1)),
                        rd=[k.slotr[sl], Hr[kc][half]], wr=[k.psr[b]])
                P.add("act", lambda e, b=b, hd=hd, half=half: e.activation(
                    U[:, hd, half * 512:(half + 1) * 512], k.ps[b][:, :], AF.Gelu_apprx_tanh),
                    rd=[k.psr[b]], wr=[Ur[hd][i] for i in range(4 * half, 4 * half + 4)])
    tg_rr = 0
    for sblk in range(2):
        sl = k.load_w(wab[:, :, 1024 + sblk * 512:1024 + (sblk + 1) * 512], v16)
        for i in range(NT):
            b = k.bank()
            for kc in range(KC):
                P.add("pe", lambda e, kc=kc, i=i, b=b, sl=sl: e.matmul(
                    k.ps[b][:, :], H[:, kc, i * 128:(i + 1) * 128], v16(k.slots[sl])[:, kc, :],
                    start=(kc == 0), stop=(kc == KC - 1)),
                    rd=[k.slotr[sl], Hr[kc][i // 4]], wr=[k.psr[b]])
            ti = tg_rr % 2
            tg_rr += 1
            P.add("act", lambda e, b=b, ti=ti: e.activation(TG[ti][:, :], k.ps[b][:, :], AF.Gelu_apprx_tanh),
                  rd=[k.psr[b]], wr=[TGr[ti]])
            qi = k.sq_rr % 2
            k.sq_rr += 1
            P.add("act", lambda e, ti=ti, i=i, sblk=sblk, qi=qi: e.activation(
                k.sq[qi][:, :], TG[ti][:, :], AF.Square,
                accum_out=SSQ[:, 2 * i + sblk:2 * i + sblk + 1]),
                rd=[TGr[ti]], wr=[k.sqr[qi], SSQr[2 * i + sblk]])
            P.add("dve", lambda e, ti=ti, i=i, sblk=sblk: e.tensor_copy(
                VG[:, i, sblk * 512:(sblk + 1) * 512], TG[ti][:, :]),
                rd=[TGr[ti]], wr=[VGr[i], bigr])
    SS3 = SSQ[:, :].rearrange("p (i s) -> p i s", s=2)
    P.add("dve", lambda e: e.tensor_tensor(RV[:, :], SS3[:, :, 0], SS3[:, :, 1], ALU.add),
          rd=SSQr, wr=[RVr])
    P.add("act", lambda e: e.activation(RV[:, :], RV[:, :], AF.Sqrt, bias=EPS, scale=1.0 / 1024),
          rd=[RVr], wr=[RVr])
    P.add("dve", lambda e: e.reciprocal(RV[:, :], RV[:, :]), rd=[RVr], wr=[RVr])
    GV = GVBS[:, 0:1024]
    BS = GVBS[:, 1024:2048]
    for i in range(NT):
        P.add("dve", lambda e, i=i: e.scalar_tensor_tensor(
            VG[:, i, :], VG[:, i, :], RV[:, i:i + 1], GV, ALU.mult, ALU.mult),
            rd=[VGr[i], RVr, gvbsr], wr=[VGr[i]])
    tmp_rr = 0
    for i in range(NT):
        for hg in range(2):
            b = k.bank()
            for h4 in range(4):
                hd = hg * 4 + h4
                P.add("pe", lambda e, i=i, hd=hd, h4=h4, b=b: e.matmul(
                    k.ps[b][:, h4 * 128:(h4 + 1) * 128], VG[:, i, hd * 128:(hd + 1) * 128], WST[:, hd, :],
                    start=True, stop=True),
                    rd=[VGr[i], WSTr], wr=[k.psr[b]])
            ti = tmp_rr % 2
            tmp_rr += 1
            P.add("dve", lambda e, b=b, ti=ti, hg=hg: e.tensor_tensor(
                TMP[ti][:, :], k.ps[b][:, :], BS[:, hg * 512:(hg + 1) * 512], ALU.add),
                rd=[k.psr[b], gvbsr], wr=[TMPr[ti]])
            P.add("dve", lambda e, ti=ti, hg=hg, i=i: e.tensor_tensor(
                U[:, hg * 4:(hg + 1) * 4, i * 128:(i + 1) * 128],
                U[:, hg * 4:(hg + 1) * 4, i * 128:(i + 1) * 128],
                TMP[ti][:, :].rearrange("p (h t) -> p h t", h=4), ALU.mult),
                rd=[TMPr[ti]] + [Ur[hg * 4 + h][i] for h in range(4)],
                wr=[Ur[hg * 4 + h][i] for h in range(4)])

    wo = w_out_ab.rearrange("(kc p) n -> p kc n", p=128)

    def out_proj(part):
        for sblk in range(2):
            sl = k.load_w(wo[:, part * 8:(part + 1) * 8, sblk * 1024:(sblk + 1) * 1024],
                          lambda s: s[:, :].rearrange("p (a b) -> p a b", a=8))
            W8 = k.slots[sl][:, :].rearrange("p (a b) -> p a b", a=8)
            for oc in range(8):
                ocg = sblk * 8 + oc
                for half in range(2):
                    b = k.bank()
                    for kc in range(8):
                        if part == 0:
                            rr = [Ur[kc][i] for i in range(4 * half, 4 * half + 4)]
                        else:
                            rr = [BTr[kc][half]]
                        P.add("pe", lambda e, kc=kc, oc=oc, b=b, W8=W8, half=half: e.matmul(
                            k.ps[b][:, :], W8[:, kc, oc * 128:(oc + 1) * 128],
                            U[:, kc, half * 512:(half + 1) * 512],
                            start=(kc == 0), stop=(kc == 7)),
                            rd=[k.slotr[sl]] + rr, wr=[k.psr[b]])
                    P.add("dve", lambda e, b=b, ocg=ocg, half=half: e.tensor_tensor(
                        X[:, ocg, half * 512:(half + 1) * 512], X[:, ocg, half * 512:(half + 1) * 512],
                        k.ps[b][:, :], ALU.add),
                        rd=[k.psr[b], Xr[ocg][half]], wr=[Xr[ocg][half]])

    out_proj(0)
    P.add("pool", lambda e: e.dma_start(out=WP, in_=w_pool.rearrange("g (cc p) d -> p g cc d", p=128)),
          wr=[WPr], dma=WPr)
    UALL = [Ur[h][i] for h in range(8) for i in range(NT)]

    DT = BIG[:, :].rearrange("p (c t) -> p c t", c=8)
    DTr = VGr
    INVC = C[:, CA["INVC"]:CA["INVC"] + 64]
    PSC = C[:, CA["PSC"]:CA["PSC"] + 8]
    wins = (2, 4, 8, 16)
    for sblk in range(2):
        sl = k.load_w(wab[:, :, 2048 + sblk * 512:2048 + (sblk + 1) * 512], v16)
        for oc in range(4):
            c = sblk * 4 + oc
            g = c // 2
            bm = [k.bank(), k.bank()]
            bh = k.bank()
            for half in range(2):
                for kc in range(KC):
                    P.add("pe", lambda e, kc=kc, half=half, oc=oc, b=bm[half], sl=sl: e.matmul(
                        k.ps[b][:, :], v16(k.slots[sl])[:, kc, oc * 128:(oc + 1) * 128],
                        H[:, kc, half * 512:(half + 1) * 512], start=(kc == 0), stop=(kc == KC - 1)),
                        rd=[k.slotr[sl], Hr[kc][half]], wr=[k.psr[bm[half]]])
            for kc in range(KC):
                P.add("pe", lambda e, kc=kc, oc=oc, b=bh, sl=sl: e.matmul(
                    k.ps[b][:, 0:128], v16(k.slots[sl])[:, kc, oc * 128:(oc + 1) * 128],
                    HH[:, kc, :], start=(kc == 0), stop=(kc == KC - 1)),
                    rd=[k.slotr[sl], HHr[kc][0]], wr=[k.psr[bh]])
            for th in range(2):
                P.add("act", lambda e, th=th, b=bm[th]: e.activation(
                    Z[:, :, 16:144], k.ps[b][:, :].rearrange("p (i t) -> p i t", i=4), AF.Copy),
                    rd=[k.psr[bm[th]]], wr=[Zr])
                P.add("act", lambda e, b=bh, th=th: e.activation(
                    Z[:, :, 0:16], k.ps[b][:, th * 64:(th + 1) * 64].rearrange("p (i t) -> p i t", i=4), AF.Copy),
                    rd=[k.psr[bh]], wr=[Zr])
                cur, curr = Z, Zr
                sh = 1
                zi = 0
                while sh < wins[g]:
                    nxt, nxtr = ZS[zi % 2], ZSr[zi % 2]
                    zi += 1
                    P.add("dve", lambda e, cur=cur, nxt=nxt, sh=sh: e.tensor_tensor(
                        nxt[:, :, 2 * sh - 1:144], cur[:, :, 2 * sh - 1:144], cur[:, :, sh - 1:144 - sh], ALU.add),
                        rd=[curr], wr=[nxtr])
                    cur, curr = nxt, nxtr
                    sh *= 2
                P.add("dve", lambda e, cur=cur, c=c, g=g, th=th: e.scalar_tensor_tensor(
                    DT[:, c, th * 512:(th + 1) * 512].rearrange("p (i t) -> p i t", i=4), cur[:, :, 16:144],
                    1.0 / wins[g], Z[:, :, 16:144], ALU.mult, ALU.subtract),
                    rd=[curr, Zr], wr=DTr[4 * th:4 * th + 4] + [bigr])
                if th == 0:
                    ti = tmp_rr % 2
                    tmp_rr += 1
                    P.add("dve", lambda e, cur=cur, ti=ti, g=g: e.tensor_tensor(
                        TMP[ti][:, 0:16], cur[:, 0, 16:32], INVC[:, g * 16:(g + 1) * 16], ALU.mult),
                        rd=[curr, k.constr], wr=[TMPr[ti]])
                    P.add("dve", lambda e, ti=ti, c=c: e.tensor_tensor(
                        DT[:, c, 0:16], TMP[ti][:, 0:16], Z[:, 0, 16:32], ALU.subtract),
                        rd=[TMPr[ti], Zr], wr=[DTr[0], bigr])
            if c % 2 == 1:
                for dc in range(2):
                    for half in range(2):
                        b = k.bank()
                        for cc in range(2):
                            P.add("pe", lambda e, g=g, cc=cc, dc=dc, half=half, b=b: e.matmul(
                                k.ps[b][:, :], WP[:, g, cc, dc * 128:(dc + 1) * 128],
                                DT[:, 2 * g + cc, half * 512:(half + 1) * 512], start=(cc == 0), stop=(cc == 1)),
                                rd=[WPr] + DTr[4 * half:4 * half + 4], wr=[k.psr[b]])
                        P.add("act", lambda e, g=g, dc=dc, half=half, b=b: e.activation(
                            BT[:, 2 * g + dc, half * 512:(half + 1) * 512], k.ps[b][:, :], AF.Copy,
                            scale=PSC[:, 2 * g + dc:2 * g + dc + 1]),
                            rd=[k.psr[b], k.constr], wr=[BTr[2 * g + dc][half]] + (UALL if (g == 0 and dc == 0 and half == 0) else []))
    out_proj(1)
    k.barrier()
    k.slot_n = 3
    k.ACTB = [BIG[:, 0:4096].rearrange("p (a b) -> p a b", a=4), BIG[:, 4096:8192].rearrange("p (a b) -> p a b", a=4)]
    k.ACTBr = P.Rs("ACTB", 2, 4, 2)
    gf0 = C[:, CA["GF0"]:CA["GF0"] + 16]
    k.norm_fm(X, Xr, KC, [(0, 512, 0), (512, 512, 1)], gf0, H, Hr, 0)
    k.ffn(w_gate, w_up, w_down, 0)
    k.barrier()
    k.slot_n = 2
    k.slot_rr = 0
    CSr = P.R("CS")
    P.add("sp", lambda e: e.dma_start(out=S2F[:, 0:2048], in_=cossin), wr=[CSr], dma=CSr)
    x1r = x1T.rearrange("(kc p) t -> p kc t", p=128)
    outr = P.R("out")
    for q in range(4):
        P.add("sp", lambda e, q=q: e.dma_start(out=x1r[:, 4 * q:4 * q + 4, :], in_=X[:, 4 * q:4 * q + 4, :]),
              rd=[Xr[kc][h] for kc in range(4 * q, 4 * q + 4) for h in range(2)], wr=[outr], dma=outr)
    gm1 = C[:, CA["GM1"]:CA["GM1"] + 16]
    k.norm_fm(X, Xr, KC, [(0, 512, 0), (512, 512, 1)], gm1, H, Hr, 0)
    wckv = w_ckv.rearrange("(kc p) n -> p kc n", p=128)
    sl = k.load_w(wckv, v16)
    CKV = BIG[:, :].bitcast(F32).rearrange("p (a b) -> p a b", a=4)
    CKVr = P.Rs("CKV", 4, 2)
    for oc in range(4):
        for half in range(2):
            b = k.bank()
            for kc in range(KC):
                P.add("pe", lambda e, kc=kc, half=half, oc=oc, b=b, sl=sl: e.matmul(
                    k.ps[b][:, :], v16(k.slots[sl])[:, kc, oc * 128:(oc + 1) * 128],
                    H[:, kc, half * 512:(half + 1) * 512], start=(kc == 0), stop=(kc == KC - 1)),
                    rd=[k.slotr[sl], Hr[kc][half]], wr=[k.psr[b]])
            P.add("act", lambda e, b=b, oc=oc, half=half: e.activation(
                CKV[:, oc, half * 512:(half + 1) * 512], k.ps[b][:, :], AF.Copy),
                rd=[k.psr[b]], wr=[CKVr[oc][half]])
    CKN = U[:, :, :].rearrange("p a b -> p (a b)").bitcast(F32).rearrange("p (a b) -> p a b", a=4)
    CKNr = P.Rs("CKN", 4, 2)
    gckv = C[:, CA["GCKV"]:CA["GCKV"] + 4]
    k.norm_fm(CKV, CKVr, 4, [(0, 512, 0), (512, 512, 1)], gckv, CKN, CKNr, 1)
    lcr = latc.rearrange("(kc p) t -> p kc t", p=128)
    P.add("sp", lambda e: e.dma_start(out=lcr, in_=CKN),
          rd=[CKNr[c][h] for c in range(4) for h in range(2)], wr=[outr], dma=outr)
    v128 = lambda s: s[:, 0:2048].rearrange("p (a b) -> p a b", a=16)
    sl = k.load_w(w_kr.rearrange("(kc p) n -> p kc n", p=128), v128)
    P.add("dve", lambda e, sl=sl: e.tensor_scalar(
        v128(k.slots[sl])[:, :, 64:96], v128(k.slots[sl])[:, :, 64:96], -1.0, None, ALU.mult),
        rd=[k.slotr[sl]], wr=[k.slotr[sl]])
    KTMP = k.rstd[0][0:64, :]
    KTMPr = k.rstdr[0]
    COS = S2F[:, 0:T]
    SIN = S2F[:, T:2 * T]
    for half in range(2):
        ba, bb = k.bank(), k.bank()
        for (b, c0) in ((ba, 0), (bb, 64)):
            for kc in range(KC):
                P.add("pe", lambda e, kc=kc, half=half, b=b, c0=c0, sl=sl: e.matmul(
                    k.ps[b][0:64, :], v128(k.slots[sl])[:, kc, c0:c0 + 64],
                    H[:, kc, half * 512:(half + 1) * 512], start=(kc == 0), stop=(kc == KC - 1)),
                    rd=[k.slotr[sl], Hr[kc][half]], wr=[k.psr[b]])
        P.add("dve", lambda e, ba=ba, half=half: e.tensor_tensor(
            TG[half][0:64, :], k.ps[ba][0:64, :], COS[0:64, half * 512:(half + 1) * 512], ALU.mult),
            rd=[k.psr[ba], CSr], wr=[TGr[half]])
        P.add("dve", lambda e, bb=bb, half=half: e.tensor_tensor(
            KTMP[:, :], k.ps[bb][0:64, :], SIN[0:64, half * 512:(half + 1) * 512], ALU.mult),
            rd=[k.psr[bb], CSr], wr=[KTMPr])
        P.add("dve", lambda e, half=half: e.tensor_tensor(
            TG[half][0:64, :], TG[half][0:64, :], KTMP[:, :], ALU.add),
            rd=[KTMPr, TGr[half]], wr=[TGr[half]])
        P.add("sp", lambda e, half=half: e.dma_start(out=latr[:, half * 512:(half + 1) * 512], in_=TG[half][0:64, :]),
              rd=[TGr[half]], wr=[outr], dma=outr)
    P.finish("sp", [outr])
    P.emit(nc, k.es)
    k.es.close()
    return nc


CB = dict(GM1=0, GF1=16, GFIN=32, GCQ=48, COS=64, SIN=1088, MASK=2112, N=2624)


def build_B():
    k = KB()
    nc, P = k.nc, k.P
    x1T = k.din("x1T", [D, T])
    consts = k.din("consts", [128, CB["N"]])
    ckv_all = k.din("ckv_all", [512, 4 * T])
    kr_all = k.din("kr_all", [64, 4 * T])
    w_cq = k.din("w_cq", [D, 512])
    w_att = k.din("w_att", [512, 16, 512])
    w_out_c = k.din("w_out_c", [D, D])
    w_gate = k.din("w_gate", [1, D, DFF])
    w_up = k.din("w_up", [1, D, DFF])
    w_down = k.din("w_down", [1, DFF, D])
    yT = k.dout("yT", [D, T])

    k.common()
    X = k.X = k.sb("X", [128, KC, T], F32)
    Xr = k.Xr = P.Rs("X", KC, 2)
    H = k.H = k.sb("H", [128, KC, T], BF16)
    Hr = k.Hr = P.Rs("H", KC, 2)
    C = k.sb("C", [128, CB["N"]], F32)
    k.constr = P.R("const")
    TMP = [k.sb(f"TMP{i}", [128, 512], F32) for i in range(2)]
    TMPr = [P.R("TMP") for _ in range(2)]
    tmp_rr = 0
    k.SG = TMP
    k.SGr = TMPr
    CQ = k.slots[2][:, :].bitcast(F32).rearrange("p (a b) -> p a b", a=4)
    CQr = P.Rs("CQ", 4, 2)
    CQN = k.sb("CQN", [128, 4, T], BF16)
    CQNr = P.Rs("CQN", 4, 2)
    KRA = k.sb("KRA", [64, 4 * T], BF16)
    KRAr = P.R("KRA")
    MASKB = k.sb("MASKB", [128, 512], BF16)
    MASKBr = P.R("MASKB")
    OT = k.sb("OT", [128, 4, T], BF16)
    OTr = P.Rs("OT", 4, 2)
    S2 = k.slots[2]
    QN = [S2[:, 4096 + 1024 * i:4096 + 1024 * (i + 1)] for i in range(2)]
    QNr = [P.Rs("QN", 2) for _ in range(2)]
    QR = [S2[0:64, 6144 + 1024 * i:6144 + 1024 * (i + 1)] for i in range(2)]
    QRr = [P.Rs("QR", 2) for _ in range(2)]
    VVt = k.sb("VV", [128, 32, 128], BF16)
    VV = [VVt, VVt]
    _vvr = P.R("VV")
    VVr = [_vvr, _vvr]
    PT = [k.sb(f"PT{i}", [128, 512], BF16) for i in range(2)]
    PTr = [P.R("PT") for _ in range(2)]
    RCP = [k.sb(f"RCP{i}", [128, 512], F32) for i in range(1)]
    RCPr = [P.R("RCP") for _ in range(1)]

    xr = x1T.rearrange("(kc p) t -> p kc t", p=128)
    Xin = P.Rs("Xin", 4)
    for q in range(4):
        P.add("sp", lambda e, q=q: e.dma_start(out=X[:, 4 * q:4 * q + 4, :], in_=xr[:, 4 * q:4 * q + 4, :]),
              wr=[Xr[kc][h] for kc in range(4 * q, 4 * q + 4) for h in range(2)] + [Xin[q]], dma=Xin[q])
    P.add("sp", lambda e: e.dma_start(out=C[:, :], in_=consts), wr=[k.constr], dma=k.constr)
    for q in range(4):
        P.add("pool", lambda e, q=q: e.dma_start(out=KRA[:, q * 1024:(q + 1) * 1024], in_=kr_all[:, q * 1024:(q + 1) * 1024]),
              wr=[KRAr], dma=KRAr)
    P.add("dve", lambda e: e.tensor_copy(MASKB[:, :], C[:, CB["MASK"]:CB["MASK"] + 512]),
          rd=[k.constr], wr=[MASKBr])
    COS = C[:, CB["COS"]:CB["COS"] + T]
    SIN = C[:, CB["SIN"]:CB["SIN"] + T]

    gm1 = C[:, CB["GM1"]:CB["GM1"] + 16]
    k.norm_fm(X, Xr, KC, [(0, 512, 0), (512, 512, 1)], gm1, H, Hr, 0)
    v16 = lambda s: s[:, :].rearrange("p (a b) -> p a b", a=16)
    v4 = lambda s: s[:, :].rearrange("p (a b) -> p a b", a=4)
    sl = k.load_w(w_cq.rearrange("(kc p) n -> p kc n", p=128), v16)
    for oc in range(4):
        for half in range(2):
            b = k.bank()
            for kc in range(KC):
                P.add("pe", lambda e, kc=kc, half=half, oc=oc, b=b, sl=sl: e.matmul(
                    k.ps[b][:, :], v16(k.slots[sl])[:, kc, oc * 128:(oc + 1) * 128],
                    H[:, kc, half * 512:(half + 1) * 512], start=(kc == 0), stop=(kc == KC - 1)),
                    rd=[k.slotr[sl], Hr[kc][half]], wr=[k.psr[b]])
            P.add("act", lambda e, b=b, oc=oc, half=half: e.activation(
                CQ[:, oc, half * 512:(half + 1) * 512], k.ps[b][:, :], AF.Copy),
                rd=[k.psr[b]], wr=[CQr[oc][half]])
    gcq = C[:, CB["GCQ"]:CB["GCQ"] + 4]
    k.norm_fm(CQ, CQr, 4, [(0, 512, 0), (512, 512, 1)], gcq, CQN, CQNr, 1)
    k.barrier()
    CKA = H[:, :, :].rearrange("p a b -> p (a b)").rearrange("p (c n) -> p c n", c=4)
    CKAr = P.R("CKA")
    ckr = ckv_all.rearrange("(c p) n -> p c n", p=128)
    for q in range(4):
        P.add("pool", lambda e, q=q: e.dma_start(out=CKA[:, :, q * 1024:(q + 1) * 1024], in_=ckr[:, :, q * 1024:(q + 1) * 1024]),
              wr=[CKAr], dma=CKAr)
    KNt = k.slots[2][:, :].rearrange("p (a n) -> p a n", a=2)
    _knr = P.R("KN")
    KNr = [_knr, _knr]
    va = lambda s: s[:, :].rearrange("p (h kc c) -> p h kc c", h=4, kc=4)
    watt = w_att.rearrange("(kc p) h c -> p h kc c", p=128)
    woc = w_out_c.rearrange("(kc p) n -> p kc n", p=128)
    slot_seq = [0, 1]
    ps_rr = 0
    for hgp in range(4):
        s = slot_seq[hgp % 2]
        for hh in range(4):
            P.add("pool", lambda e, s=s, hh=hh, hgp=hgp: e.dma_start(
                out=va(k.slots[s])[:, hh, :, :], in_=watt[:, hgp * 4 + hh, :, :]),
                wr=[k.slotr[s]], dma=k.slotr[s])
        P.add("dve", lambda e, s=s: e.tensor_scalar(
            va(k.slots[s])[:, :, :, 192:224], va(k.slots[s])[:, :, :, 192:224], -1.0, None, ALU.mult),
            rd=[k.slotr[s]], wr=[k.slotr[s]])
        W = va(k.slots[s])
        Wr = k.slotr[s]
        for hh in range(4):
            h = hgp * 4 + hh
            pb = h % 2
            for half in range(2):
                b = 6 + (ps_rr % 2)
                ps_rr += 1
                for kc in range(4):
                    P.add("pe", lambda e, kc=kc, half=half, b=b, hh=hh, W=W: e.matmul(
                        k.ps[b][:, :], W[:, hh, kc, 0:128], CQN[:, kc, half * 512:(half + 1) * 512],
                        start=(kc == 0), stop=(kc == 3)),
                        rd=[Wr, CQNr[kc][half]], wr=[k.psr[b]])
                P.add("act", lambda e, b=b, half=half, pb=pb: e.activation(
                    QN[pb][:, half * 512:(half + 1) * 512], k.ps[b][:, :], AF.Copy),
                    rd=[k.psr[b]], wr=[QNr[pb][half]])
            for half in range(2):
                ba = 6 + (ps_rr % 2)
                ps_rr += 1
                bb = 6 + (ps_rr % 2)
                ps_rr += 1
                for (b, c0) in ((ba, 128), (bb, 192)):
                    for kc in range(4):
                        P.add("pe", lambda e, kc=kc, half=half, b=b, hh=hh, W=W, c0=c0: e.matmul(
                            k.ps[b][0:64, :], W[:, hh, kc, c0:c0 + 64], CQN[:, kc, half * 512:(half + 1) * 512],
                            start=(kc == 0), stop=(kc == 3)),
                            rd=[Wr, CQNr[kc][half]], wr=[k.psr[b]])
                t0 = tmp_rr % 2
                tmp_rr += 1
                t1 = tmp_rr % 2
                tmp_rr += 1
                P.add("dve", lambda e, ba=ba, half=half, t0=t0: e.tensor_tensor(
                    TMP[t0][0:64, :], k.ps[ba][0:64, :], COS[0:64, half * 512:(half + 1) * 512], ALU.mult),
                    rd=[k.psr[ba], k.constr], wr=[TMPr[t0]])
                P.add("dve", lambda e, bb=bb, half=half, t1=t1: e.tensor_tensor(
                    TMP[t1][0:64, :], k.ps[bb][0:64, :], SIN[0:64, half * 512:(half + 1) * 512], ALU.mult),
                    rd=[k.psr[bb], k.constr], wr=[TMPr[t1]])
                P.add("dve", lambda e, half=half, t0=t0, t1=t1, pb=pb: e.tensor_tensor(
                    QR[pb][:, half * 512:(half + 1) * 512], TMP[t0][0:64, :], TMP[t1][0:64, :], ALU.add),
                    rd=[TMPr[t0], TMPr[t1]], wr=[QRr[pb][half]])
            for kb in range(8):
                b = 6 + (ps_rr % 2)
                ps_rr += 1
                for kc in range(4):
                    P.add("pe", lambda e, kc=kc, kb=kb, b=b, hh=hh, W=W: e.matmul(
                        k.ps[b][:, :], W[:, hh, kc, 256:384], CKA[:, kc, kb * 512:(kb + 1) * 512],
                        start=(kc == 0), stop=(kc == 3)),
                        rd=[Wr, CKAr], wr=[k.psr[b]])
                P.add("act", lambda e, b=b, kb=kb, pb=pb: e.activation(
                    KNt[:, 0, kb * 512:(kb + 1) * 512], k.ps[b][:, :], AF.Copy),
                    rd=[k.psr[b]], wr=[KNr[pb]])
            for jb in range(8):
                b = 6 + (ps_rr % 2)
                ps_rr += 1
                for j4 in range(4):
                    j = jb * 4 + j4
                    for kc in range(4):
                        P.add("pe", lambda e, kc=kc, j=j, j4=j4, b=b, hh=hh, W=W: e.matmul(
                            k.ps[b][:, j4 * 128:(j4 + 1) * 128], CKA[:, kc, j * 128:(j + 1) * 128],
                            W[:, hh, kc, 384:512], start=(kc == 0), stop=(kc == 3)),
                            rd=[Wr, CKAr], wr=[k.psr[b]])
                P.add("dve", lambda e, b=b, jb=jb, pb=pb: e.tensor_copy(
                    VV[pb][:, jb * 4:(jb + 1) * 4, :], k.ps[b][:, :].rearrange("p (j d) -> p j d", j=4)),
                    rd=[k.psr[b]], wr=[VVr[pb]])
            for G in range(2):
                bo = 2 + (h * 2 + G) % 2
                bl = 4 + (h * 2 + G) % 2
                nblk = 16 * G + 16
                def blk(j, G=G):
                    ip, rp = j // 4, j % 4
                    kcol = rp * 1024 + ip * 128
                    imin = max(ip, 4 * G)
                    c0 = (imin - 4 * G) * 128
                    return ip, rp, kcol, c0, 512 - c0, G * 512 + c0

                def emit_S(j, pb=pb, G=G):
                    ip, rp, kcol, c0, n, q0 = blk(j)
                    bs = j % 2
                    P.add("pe", lambda e, bs=bs, pb=pb, kcol=kcol, q0=q0, n=n: e.matmul(
                        k.ps[bs][:, 0:n], KNt[:, 0, kcol:kcol + 128], QN[pb][:, q0:q0 + n],
                        start=True, stop=False),
                        rd=[KNr[pb], QNr[pb][G]], wr=[k.psr[bs]])
                    P.add("pe", lambda e, bs=bs, pb=pb, kcol=kcol, q0=q0, n=n: e.matmul(
                        k.ps[bs][:, 0:n], KRA[:, kcol:kcol + 128], QR[pb][:, q0:q0 + n],
                        start=False, stop=True),
                        rd=[KRAr, QRr[pb][G]], wr=[k.psr[bs]])

                emit_S(0)
                for j in range(nblk):
                    if j + 1 < nblk:
                        emit_S(j + 1)
                    ip, rp, kcol, c0, n, q0 = blk(j)
                    bs = j % 2
                    pi = j % 2
                    P.add("act", lambda e, bs=bs, pi=pi, n=n: e.activation(
                        PT[pi][:, 0:n], k.ps[bs][:, 0:n], AF.Exp, scale=SM_SCALE),
                        rd=[k.psr[bs]], wr=[PTr[pi]])
                    if ip >= 4 * G:
                        P.add("dve", lambda e, pi=pi, rp=rp: e.tensor_tensor(
                            PT[pi][:, 0:128], PT[pi][:, 0:128], MASKB[:, rp * 128:(rp + 1) * 128], ALU.mult),
                            rd=[PTr[pi], MASKBr], wr=[PTr[pi]])
                    P.add("pe", lambda e, bo=bo, pb=pb, j=j, pi=pi, c0=c0, n=n, nblk=nblk: e.matmul(
                        k.ps[bo][:, c0:c0 + n], VV[pb][:, (j % 4) * 8 + j // 4, :], PT[pi][:, 0:n],
                        start=(j == 0), stop=(j == nblk - 1)),
                        rd=[VVr[pb], PTr[pi]], wr=[k.psr[bo]])
                    P.add("pe", lambda e, bl=bl, pi=pi, c0=c0, n=n, j=j, nblk=nblk: e.matmul(
                        k.ps[bl][:, c0:c0 + n], k.ones[:, 2, :], PT[pi][:, 0:n],
                        start=(j == 0), stop=(j == nblk - 1)),
                        rd=[k.onesr, PTr[pi]], wr=[k.psr[bl]])
                ri = 0
                P.add("dve", lambda e, bl=bl, ri=ri: e.reciprocal(RCP[ri][:, :], k.ps[bl][:, :]),
                      rd=[k.psr[bl]], wr=[RCPr[ri]])
                ob = hh
                P.add("dve", lambda e, bo=bo, ri=ri, ob=ob, G=G: e.tensor_tensor(
                    OT[:, ob, G * 512:(G + 1) * 512], k.ps[bo][:, :], RCP[ri][:, :], ALU.mult),
                    rd=[k.psr[bo], RCPr[ri]], wr=[OTr[ob][G]])
        ws = slot_seq[(hgp + 1) % 2]
        P.add("pool", lambda e, ws=ws, hgp=hgp: e.dma_start(
            out=v4(k.slots[ws]), in_=woc[:, hgp * 4:(hgp + 1) * 4, :]),
            wr=[k.slotr[ws]], dma=k.slotr[ws])
        for oc in range(KC):
            for half in range(2):
                b = 6 + (ps_rr % 2)
                ps_rr += 1
                for kc in range(4):
                    ob = kc
                    P.add("pe", lambda e, kc=kc, half=half, oc=oc, b=b, ws=ws, ob=ob: e.matmul(
                        k.ps[b][:, :], v4(k.slots[ws])[:, kc, oc * 128:(oc + 1) * 128],
                        OT[:, ob, half * 512:(half + 1) * 512], start=(kc == 0), stop=(kc == 3)),
                        rd=[k.slotr[ws], OTr[ob][half]], wr=[k.psr[b]])
                P.add("dve", lambda e, b=b, oc=oc, half=half: e.tensor_tensor(
                    X[:, oc, half * 512:(half + 1) * 512], X[:, oc, half * 512:(half + 1) * 512],
                    k.ps[b][:, :], ALU.add),
                    rd=[k.psr[b], Xr[oc][half]], wr=[Xr[oc][half]])
    k.barrier()
    k.slot_rr = 0
    k.ACTB = [OT, VVt[:, :, :].rearrange("p a b -> p (a b)").rearrange("p (a b) -> p a b", a=4)]
    k.ACTBr = P.Rs("ACTB", 2, 4, 2)
    gf1 = C[:, CB["GF1"]:CB["GF1"] + 16]
    k.norm_fm(X, Xr, KC, [(0, 512, 0), (512, 512, 1)], gf1, H, Hr, 0)
    k.ffn(w_gate, w_up, w_down, 0)
    k.barrier()
    gfin = C[:, CB["GFIN"]:CB["GFIN"] + 16]
    YF = H[:, :, :].rearrange("p a b -> p (a b)").bitcast(F32).rearrange("p (a b) -> p a b", a=8)
    YFr = P.Rs("YF", 8, 2)
    yr = yT.rearrange("(kc p) t -> p kc t", p=128)
    outr = P.R("out")
    ones = k.ones
    rst = []
    for (c0, n, pi) in [(0, 512, 0), (512, 512, 1)]:
        b = k.bank()
        for kc in range(KC):
            qi = k.sq_rr % 2
            k.sq_rr += 1
            P.add("act", lambda e, qi=qi, kc=kc, c0=c0, n=n: e.activation(
                k.sq[qi][:, 0:n], X[:, kc, c0:c0 + n], AF.Square),
                rd=[Xr[kc][pi]], wr=[k.sqr[qi]])
            P.add("pe", lambda e, qi=qi, kc=kc, n=n, b=b: e.matmul(
                k.ps[b][:, 0:n], ones[:, 0, :], k.sq[qi][:, 0:n], start=(kc == 0), stop=(kc == KC - 1)),
                rd=[k.sqr[qi], k.onesr], wr=[k.psr[b]])
        P.add("act", lambda e, b=b, pi=pi: e.activation(k.rstd[pi][:, :], k.ps[b][:, :], AF.Sqrt, bias=EPS, scale=1.0),
              rd=[k.psr[b]], wr=[k.rstdr[pi]])
        P.add("dve", lambda e, pi=pi: e.reciprocal(k.rstd[pi][:, :], k.rstd[pi][:, :]),
              rd=[k.rstdr[pi]], wr=[k.rstdr[pi]])
    for part in range(2):
        for kc8 in range(8):
            kc = part * 8 + kc8
            for pi in range(2):
                P.add("dve", lambda e, kc=kc, kc8=kc8, pi=pi: e.scalar_tensor_tensor(
                    YF[:, kc8, pi * 512:(pi + 1) * 512], X[:, kc, pi * 512:(pi + 1) * 512], gfin[:, kc:kc + 1],
                    k.rstd[pi][:, :], ALU.mult, ALU.mult),
                    rd=[Xr[kc][pi], k.rstdr[pi], k.constr], wr=[YFr[kc8][pi]])
        P.add("sp", lambda e, part=part: e.dma_start(out=yr[:, part * 8:(part + 1) * 8, :], in_=YF),
              rd=[YFr[c][h] for c in range(8) for h in range(2)], wr=[outr], dma=outr)
    P.finish("sp", [outr])
    P.emit(nc, k.es)
    k.es.close()
    return nc


def build_F():
    k = KB()
    nc, P = k.nc, k.P
    xT = k.din("xT", [4, D, T])
    xhT = k.din("xhT", [4, D, 128])
    constsA = k.din("constsA", [4, 128, CA["N"]])
    gvbs = k.din("gvbs", [128, 2048])
    cossin = k.din("cossin", [4, 128, 2048])
    constsB = k.din("constsB", [128, CB["N"]])
    w_in_ab = k.din("w_in_ab", [D, 3072])
    w_sT = k.din("w_sT", [8, 128, 128])
    w_pool = k.din("w_pool", [4, 256, 256])
    w_out_ab = k.din("w_out_ab", [D, D])
    w_gate2 = k.din("w_gate", [2, D, DFF])
    w_up2 = k.din("w_up", [2, D, DFF])
    w_down2 = k.din("w_down", [2, DFF, D])
    w_ckv = k.din("w_ckv", [D, 512])
    w_kr = k.din("w_kr", [D, 128])
    w_cq = k.din("w_cq", [D, 512])
    w_att = k.din("w_att", [512, 16, 512])
    w_out_c = k.din("w_out_c", [D, D])
    yT = k.dout("yT", [D, T])
    latc_d = nc.dram_tensor("latc_d", [512, 4 * T], F32, kind="Internal").ap()
    latr_d = nc.dram_tensor("latr_d", [64, 4 * T], F32, kind="Internal").ap()
    latdr = P.R("latd")

    k.common(2)
    X = k.X = k.sb("X", [128, KC, T], F32)
    Xr = k.Xr = P.Rs("X", KC, 2)
    H = k.H = k.sb("H", [128, KC, T], BF16)
    Hr = k.Hr = P.Rs("H", KC, 2)
    TG = [k.sb(f"TG{i}", [128, 512], F32) for i in range(2)]
    TGr = [P.R("TG") for _ in range(2)]
    k.SG = TG
    k.SGr = TGr
    TMP, TMPr = TG, TGr
    ARENA_N = 24320
    ARENA = k.sb("ARENA", [128, ARENA_N], BF16)
    apos = [0]

    def carve(nelem_bf16):
        a = apos[0]
        apos[0] = a + nelem_bf16
        assert apos[0] <= ARENA_N, apos[0]
        return ARENA[:, a:a + nelem_bf16]

    HH = carve(2048).rearrange("p (a b) -> p a b", a=KC)
    HHr = P.Rs("HH", KC, 1)
    C_A = carve(2 * CA["N"]).bitcast(F32)
    constr_A = P.R("const")
    U = carve(8192).rearrange("p (a b) -> p a b", a=8)
    Ur = P.Rs("U", 8, NT)
    BIG = carve(8192)
    bigr = P.R("BIG")
    BT = U
    BTr = P.Rs("BT", 8, 2)
    S2F = k.slots[2][:, :].bitcast(F32)
    GVBS = S2F[:, 0:2048]
    gvbsr = P.R("GVBS")
    Z = S2F[:, 2048:2048 + 576].rearrange("p (i t) -> p i t", i=4)
    Zr = P.R("Z")
    ZS = [S2F[:, 2624 + 576 * i:2624 + 576 * (i + 1)].rearrange("p (i t) -> p i t", i=4) for i in range(2)]
    ZSr = [P.R("ZS") for _ in range(2)]
    WSWP = carve(2048)
    WST = WSWP[:, 0:1024].rearrange("p (h t) -> p h t", h=8)
    WP = WSWP[:, :].rearrange("p (g c d) -> p g c d", g=4, c=2)
    WSTr = P.R("WSWP")
    WPr = WSTr
    SSQ = carve(32).bitcast(F32)
    SSQr = P.Rs("SSQ", 16)
    RV = carve(16).bitcast(F32)
    RVr = P.R("RV")
    w_gate, w_up, w_down = w_gate2, w_up2, w_down2

    def pass_A(s):
        C = C_A
        k.constr = constr_A
        xr = xT[s].rearrange("(kc p) t -> p kc t", p=128)
        Xin = P.Rs("Xin", 4)
        for q in range(4):
            P.add("sp", lambda e, q=q: e.dma_start(out=X[:, 4 * q:4 * q + 4, :], in_=xr[:, 4 * q:4 * q + 4, :]),
                  wr=[Xr[kc][h] for kc in range(4 * q, 4 * q + 4) for h in range(2)] + [Xin[q]], dma=Xin[q])
        P.add("sp", lambda e: e.dma_start(out=C[:, :], in_=constsA[s]), wr=[k.constr], dma=k.constr)
        XHt = BIG[:, 0:4096].bitcast(F32).rearrange("p (a b) -> p a b", a=KC)
        XHr = [[bigr] for _ in range(KC)]
        XHd = P.R("XHd")
        P.add("sp", lambda e: e.dma_start(out=XHt, in_=xhT[s].rearrange("(kc p) t -> p kc t", p=128)),
              wr=[bigr, XHd], dma=XHd)
        P.add("sp", lambda e: e.dma_start(out=GVBS, in_=gvbs), wr=[gvbsr], dma=gvbsr)
        P.add("pool", lambda e: e.dma_start(out=WST, in_=w_sT.rearrange("h s t -> s h t")),
              wr=[WSTr], dma=WSTr)
        P.add("dve", lambda e: e.memset(WST[64:128, :, 0:64], 0.0), wr=[WSTr])

        gm0 = C[:, CA["GM0"]:CA["GM0"] + 16]
        k.norm_fm(X, Xr, KC, [(0, 512, 0), (512, 512, 1)], gm0, H, Hr, 0)
        k.norm_fm(XHt, XHr, KC, [(0, 128, 0)], gm0, HH, HHr, 0)

        v16 = lambda s: s[:, :].rearrange("p (a b) -> p a b", a=16)
        v8 = lambda s: s[:, 0:4096].rearrange("p (a b) -> p a b", a=8)
        wab = w_in_ab.rearrange("(kc p) n -> p kc n", p=128)
        VG = BIG[:, :].rearrange("p (i c) -> p i c", i=NT)
        VGr = P.Rs("VG", NT)
        k.slot_n = 2

        for sblk in range(2):
            sl = k.load_w(wab[:, :, sblk * 512:(sblk + 1) * 512], v16)
            for oc in range(4):
                hd = sblk * 4 + oc
                for half in range(2):
                    b = k.bank()
                    for kc in range(KC):
                        P.add("pe", lambda e, kc=kc, half=half, oc=oc, b=b, sl=sl: e.matmul(
                            k.ps[b][:, :], v16(k.slots[sl])[:, kc, oc * 128:(oc + 1) * 128],
                            H[:, kc, half * 512:(half + 1) * 512], start=(kc == 0), stop=(kc == KC - 1)),
                            rd=[k.slotr[sl], Hr[kc][half]], wr=[k.psr[b]])
                    P.add("act", lambda e, b=b, hd=hd, half=half: e.activation(
                        U[:, hd, half * 512:(half + 1) * 512], k.ps[b][:, :], AF.Gelu_apprx_tanh),
                        rd=[k.psr[b]], wr=[Ur[hd][i] for i in range(4 * half, 4 * half + 4)])
        tg_rr = 0
        for sblk in range(2):
            sl = k.load_w(wab[:, :, 1024 + sblk * 512:1024 + (sblk + 1) * 512], v16)
            for i in range(NT):
                b = k.bank()
                for kc in range(KC):
                    P.add("pe", lambda e, kc=kc, i=i, b=b, sl=sl: e.matmul(
                        k.ps[b][:, :], H[:, kc, i * 128:(i + 1) * 128], v16(k.slots[sl])[:, kc, :],
                        start=(kc == 0), stop=(kc == KC - 1)),
                        rd=[k.slotr[sl], Hr[kc][i // 4]], wr=[k.psr[b]])
                ti = tg_rr % 2
                tg_rr += 1
                P.add("act", lambda e, b=b, ti=ti: e.activation(TG[ti][:, :], k.ps[b][:, :], AF.Gelu_apprx_tanh),
                      rd=[k.psr[b]], wr=[TGr[ti]])
                qi = k.sq_rr % 2
                k.sq_rr += 1
                P.add("act", lambda e, ti=ti, i=i, sblk=sblk, qi=qi: e.activation(
                    k.sq[qi][:, :], TG[ti][:, :], AF.Square,
                    accum_out=SSQ[:, 2 * i + sblk:2 * i + sblk + 1]),
                    rd=[TGr[ti]], wr=[k.sqr[qi], SSQr[2 * i + sblk]])
                P.add("dve", lambda e, ti=ti, i=i, sblk=sblk: e.tensor_copy(
                    VG[:, i, sblk * 512:(sblk + 1) * 512], TG[ti][:, :]),
                    rd=[TGr[ti]], wr=[VGr[i], bigr])
        SS3 = SSQ[:, :].rearrange("p (i s) -> p i s", s=2)
        P.add("dve", lambda e: e.tensor_tensor(RV[:, :], SS3[:, :, 0], SS3[:, :, 1], ALU.add),
              rd=SSQr, wr=[RVr])
        P.add("act", lambda e: e.activation(RV[:, :], RV[:, :], AF.Sqrt, bias=EPS, scale=1.0 / 1024),
              rd=[RVr], wr=[RVr])
        P.add("dve", lambda e: e.reciprocal(RV[:, :], RV[:, :]), rd=[RVr], wr=[RVr])
        GV = GVBS[:, 0:1024]
        BS = GVBS[:, 1024:2048]
        for i in range(NT):
            P.add("dve", lambda e, i=i: e.scalar_tensor_tensor(
                VG[:, i, :], VG[:, i, :], RV[:, i:i + 1], GV, ALU.mult, ALU.mult),
                rd=[VGr[i], RVr, gvbsr], wr=[VGr[i]])
        tmp_rr = 0
        for i in range(NT):
            for hg in range(2):
                b = k.bank()
                for h4 in range(4):
                    hd = hg * 4 + h4
                    P.add("pe", lambda e, i=i, hd=hd, h4=h4, b=b: e.matmul(
                        k.ps[b][:, h4 * 128:(h4 + 1) * 128], VG[:, i, hd * 128:(hd + 1) * 128], WST[:, hd, :],
                        start=True, stop=True),
                        rd=[VGr[i], WSTr], wr=[k.psr[b]])
                ti = tmp_rr % 2
                tmp_rr += 1
                P.add("dve", lambda e, b=b, ti=ti, hg=hg: e.tensor_tensor(
                    TMP[ti][:, :], k.ps[b][:, :], BS[:, hg * 512:(hg + 1) * 512], ALU.add),
                    rd=[k.psr[b], gvbsr], wr=[TMPr[ti]])
                P.add("dve", lambda e, ti=ti, hg=hg, i=i: e.tensor_tensor(
                    U[:, hg * 4:(hg + 1) * 4, i * 128:(i + 1) * 128],
                    U[:, hg * 4:(hg + 1) * 4, i * 128:(i + 1) * 128],
                    TMP[ti][:, :].rearrange("p (h t) -> p h t", h=4), ALU.mult),
                    rd=[TMPr[ti]] + [Ur[hg * 4 + h][i] for h in range(4)],
                    wr=[Ur[hg * 4 + h][i] for h in range(4)])

        wo = w_out_ab.rearrange("(kc p) n -> p kc n", p=128)

        def out_proj(part):
            for sblk in range(2):
                sl = k.load_w(wo[:, part * 8:(part + 1) * 8, sblk * 1024:(sblk + 1) * 1024],
                              lambda s: s[:, :].rearrange("p (a b) -> p a b", a=8))
                W8 = k.slots[sl][:, :].rearrange("p (a b) -> p a b", a=8)
                for oc in range(8):
                    ocg = sblk * 8 + oc
                    for half in range(2):
                        b = k.bank()
                        for kc in range(8):
                            if part == 0:
                                rr = [Ur[kc][i] for i in range(4 * half, 4 * half + 4)]
                            else:
                                rr = [BTr[kc][half]]
                            P.add("pe", lambda e, kc=kc, oc=oc, b=b, W8=W8, half=half: e.matmul(
                                k.ps[b][:, :], W8[:, kc, oc * 128:(oc + 1) * 128],
                                U[:, kc, half * 512:(half + 1) * 512],
                                start=(kc == 0), stop=(kc == 7)),
                                rd=[k.slotr[sl]] + rr, wr=[k.psr[b]])
                        P.add("dve", lambda e, b=b, ocg=ocg, half=half: e.tensor_tensor(
                            X[:, ocg, half * 512:(half + 1) * 512], X[:, ocg, half * 512:(half + 1) * 512],
                            k.ps[b][:, :], ALU.add),
                            rd=[k.psr[b], Xr[ocg][half]], wr=[Xr[ocg][half]])

        out_proj(0)
        P.add("pool", lambda e: e.dma_start(out=WP, in_=w_pool.rearrange("g (cc p) d -> p g cc d", p=128)),
              wr=[WPr], dma=WPr)
        UALL = [Ur[h][i] for h in range(8) for i in range(NT)]

        DT = BIG[:, :].rearrange("p (c t) -> p c t", c=8)
        DTr = VGr
        INVC = C[:, CA["INVC"]:CA["INVC"] + 64]
        PSC = C[:, CA["PSC"]:CA["PSC"] + 8]
        wins = (2, 4, 8, 16)
        for sblk in range(2):
            sl = k.load_w(wab[:, :, 2048 + sblk * 512:2048 + (sblk + 1) * 512], v16)
            for oc in range(4):
                c = sblk * 4 + oc
                g = c // 2
                bm = [k.bank(), k.bank()]
                bh = k.bank()
                for half in range(2):
                    for kc in range(KC):
                        P.add("pe", lambda e, kc=kc, half=half, oc=oc, b=bm[half], sl=sl: e.matmul(
                            k.ps[b][:, :], v16(k.slots[sl])[:, kc, oc * 128:(oc + 1) * 128],
                            H[:, kc, half * 512:(half + 1) * 512], start=(kc == 0), stop=(kc == KC - 1)),
                            rd=[k.slotr[sl], Hr[kc][half]], wr=[k.psr[bm[half]]])
                for kc in range(KC):
                    P.add("pe", lambda e, kc=kc, oc=oc, b=bh, sl=sl: e.matmul(
                        k.ps[b][:, 0:128], v16(k.slots[sl])[:, kc, oc * 128:(oc + 1) * 128],
                        HH[:, kc, :], start=(kc == 0), stop=(kc == KC - 1)),
                        rd=[k.slotr[sl], HHr[kc][0]], wr=[k.psr[bh]])
                for th in range(2):
                    P.add("act", lambda e, th=th, b=bm[th]: e.activation(
                        Z[:, :, 16:144], k.ps[b][:, :].rearrange("p (i t) -> p i t", i=4), AF.Copy),
                        rd=[k.psr[bm[th]]], wr=[Zr])
                    P.add("act", lambda e, b=bh, th=th: e.activation(
                        Z[:, :, 0:16], k.ps[b][:, th * 64:(th + 1) * 64].rearrange("p (i t) -> p i t", i=4), AF.Copy),
                        rd=[k.psr[bh]], wr=[Zr])
                    cur, curr = Z, Zr
                    sh = 1
                    zi = 0
                    while sh < wins[g]:
                        nxt, nxtr = ZS[zi % 2], ZSr[zi % 2]
                        zi += 1
                        P.add("dve", lambda e, cur=cur, nxt=nxt, sh=sh: e.tensor_tensor(
                            nxt[:, :, 2 * sh - 1:144], cur[:, :, 2 * sh - 1:144], cur[:, :, sh - 1:144 - sh], ALU.add),
                            rd=[curr], wr=[nxtr])
                        cur, curr = nxt, nxtr
                        sh *= 2
                    P.add("dve", lambda e, cur=cur, c=c, g=g, th=th: e.scalar_tensor_tensor(
                        DT[:, c, th * 512:(th + 1) * 512].rearrange("p (i t) -> p i t", i=4), cur[:, :, 16:144],
                        1.0 / wins[g], Z[:, :, 16:144], ALU.mult, ALU.subtract),
                        rd=[curr, Zr], wr=DTr[4 * th:4 * th + 4] + [bigr])
                    if th == 0:
                        ti = tmp_rr % 2
                        tmp_rr += 1
                        P.add("dve", lambda e, cur=cur, ti=ti, g=g: e.tensor_tensor(
                            TMP[ti][:, 0:16], cur[:, 0, 16:32], INVC[:, g * 16:(g + 1) * 16], ALU.mult),
                            rd=[curr, k.constr], wr=[TMPr[ti]])
                        P.add("dve", lambda e, ti=ti, c=c: e.tensor_tensor(
                            DT[:, c, 0:16], TMP[ti][:, 0:16], Z[:, 0, 16:32], ALU.subtract),
                            rd=[TMPr[ti], Zr], wr=[DTr[0], bigr])
                if c % 2 == 1:
                    for dc in range(2):
                        for half in range(2):
                            b = k.bank()
                            for cc in range(2):
                                P.add("pe", lambda e, g=g, cc=cc, dc=dc, half=half, b=b: e.matmul(
                                    k.ps[b][:, :], WP[:, g, cc, dc * 128:(dc + 1) * 128],
                                    DT[:, 2 * g + cc, half * 512:(half + 1) * 512], start=(cc == 0), stop=(cc == 1)),
                                    rd=[WPr] + DTr[4 * half:4 * half + 4], wr=[k.psr[b]])
                            P.add("act", lambda e, g=g, dc=dc, half=half, b=b: e.activation(
                                BT[:, 2 * g + dc, half * 512:(half + 1) * 512], k.ps[b][:, :], AF.Copy,
                                scale=PSC[:, 2 * g + dc:2 * g + dc + 1]),
                                rd=[k.psr[b], k.constr], wr=[BTr[2 * g + dc][half]] + (UALL if (g == 0 and dc == 0 and half == 0) else []))
        out_proj(1)
        k.barrier()
        k.slot_n = 3
        k.ACTB = [BIG[:, 0:4096].rearrange("p (a b) -> p a b", a=4), BIG[:, 4096:8192].rearrange("p (a b) -> p a b", a=4)]
        k.ACTBr = P.Rs("ACTB", 2, 4, 2)
        gf0 = C[:, CA["GF0"]:CA["GF0"] + 16]
        k.norm_fm(X, Xr, KC, [(0, 512, 0), (512, 512, 1)], gf0, H, Hr, 0)
        k.ffn(w_gate, w_up, w_down, 0)
        k.barrier()
        k.slot_n = 2
        k.slot_rr = 0
        CSr = P.R("CS")
        P.add("sp", lambda e: e.dma_start(out=S2F[:, 0:2048], in_=cossin[s]), wr=[CSr], dma=CSr)
        outr = latdr
        gm1 = C[:, CA["GM1"]:CA["GM1"] + 16]
        k.norm_fm(X, Xr, KC, [(0, 512, 0), (512, 512, 1)], gm1, H, Hr, 0)
        wckv = w_ckv.rearrange("(kc p) n -> p kc n", p=128)
        sl = k.load_w(wckv, v16)
        CKV = BIG[:, :].bitcast(F32).rearrange("p (a b) -> p a b", a=4)
        CKVr = P.Rs("CKV", 4, 2)
        for oc in range(4):
            for half in range(2):
                b = k.bank()
                for kc in range(KC):
                    P.add("pe", lambda e, kc=kc, half=half, oc=oc, b=b, sl=sl: e.matmul(
                        k.ps[b][:, :], v16(k.slots[sl])[:, kc, oc * 128:(oc + 1) * 128],
                        H[:, kc, half * 512:(half + 1) * 512], start=(kc == 0), stop=(kc == KC - 1)),
                        rd=[k.slotr[sl], Hr[kc][half]], wr=[k.psr[b]])
                P.add("act", lambda e, b=b, oc=oc, half=half: e.activation(
                    CKV[:, oc, half * 512:(half + 1) * 512], k.ps[b][:, :], AF.Copy),
                    rd=[k.psr[b]], wr=[CKVr[oc][half]])
        CKN = U[:, :, :].rearrange("p a b -> p (a b)").bitcast(F32).rearrange("p (a b) -> p a b", a=4)
        CKNr = P.Rs("CKN", 4, 2)
        gckv = C[:, CA["GCKV"]:CA["GCKV"] + 4]
        k.norm_fm(CKV, CKVr, 4, [(0, 512, 0), (512, 512, 1)], gckv, CKN, CKNr, 1)
        lcr = latc_d[:, s * T:(s + 1) * T].rearrange("(kc p) t -> p kc t", p=128)
        P.add("sp", lambda e: e.dma_start(out=lcr, in_=CKN),
              rd=[CKNr[c][h] for c in range(4) for h in range(2)], wr=[outr], dma=outr)
        v128 = lambda s: s[:, 0:2048].rearrange("p (a b) -> p a b", a=16)
        sl = k.load_w(w_kr.rearrange("(kc p) n -> p kc n", p=128), v128)
        P.add("dve", lambda e, sl=sl: e.tensor_scalar(
            v128(k.slots[sl])[:, :, 64:96], v128(k.slots[sl])[:, :, 64:96], -1.0, None, ALU.mult),
            rd=[k.slotr[sl]], wr=[k.slotr[sl]])
        KTMP = k.rstd[0][0:64, :]
        KTMPr = k.rstdr[0]
        COS = S2F[:, 0:T]
        SIN = S2F[:, T:2 * T]
        for half in range(2):
            ba, bb = k.bank(), k.bank()
            for (b, c0) in ((ba, 0), (bb, 64)):
                for kc in range(KC):
                    P.add("pe", lambda e, kc=kc, half=half, b=b, c0=c0, sl=sl: e.matmul(
                        k.ps[b][0:64, :], v128(k.slots[sl])[:, kc, c0:c0 + 64],
                        H[:, kc, half * 512:(half + 1) * 512], start=(kc == 0), stop=(kc == KC - 1)),
                        rd=[k.slotr[sl], Hr[kc][half]], wr=[k.psr[b]])
            P.add("dve", lambda e, ba=ba, half=half: e.tensor_tensor(
                TG[half][0:64, :], k.ps[ba][0:64, :], COS[0:64, half * 512:(half + 1) * 512], ALU.mult),
                rd=[k.psr[ba], CSr], wr=[TGr[half]])
            P.add("dve", lambda e, bb=bb, half=half: e.tensor_tensor(
                KTMP[:, :], k.ps[bb][0:64, :], SIN[0:64, half * 512:(half + 1) * 512], ALU.mult),
                rd=[k.psr[bb], CSr], wr=[KTMPr])
            P.add("dve", lambda e, half=half: e.tensor_tensor(
                TG[half][0:64, :], TG[half][0:64, :], KTMP[:, :], ALU.add),
                rd=[KTMPr, TGr[half]], wr=[TGr[half]])
            P.add("sp", lambda e, half=half: e.dma_start(out=latr_d[:, s * T + half * 512:s * T + (half + 1) * 512], in_=TG[half][0:64, :]),
                  rd=[TGr[half]], wr=[outr], dma=outr)

    for s_ in range(4):
        k.slot_n = 3
        k.slot_rr = 0
        pass_A(s_)
        k.barrier()

    apos[0] = 0
    k.slot_n = 3
    k.slot_rr = 0
    C = carve(2 * CB["N"]).bitcast(F32)
    k.constr = P.R("constB")
    tmp_rr = 0
    CQ = k.slots[2][:, :].bitcast(F32).rearrange("p (a b) -> p a b", a=4)
    CQr = P.Rs("CQ", 4, 2)
    CQN = carve(4096).rearrange("p (a b) -> p a b", a=4)
    CQNr = P.Rs("CQN", 4, 2)
    KRA = carve(4096)[0:64, :]
    KRAr = P.R("KRA")
    MASKB = carve(512)
    MASKBr = P.R("MASKB")
    OT = carve(4096).rearrange("p (a b) -> p a b", a=4)
    OTr = P.Rs("OT", 4, 2)
    S2 = k.slots[2]
    QN = [S2[:, 4096 + 1024 * i:4096 + 1024 * (i + 1)] for i in range(2)]
    QNr = [P.Rs("QN", 2) for _ in range(2)]
    QR = [S2[0:64, 6144 + 1024 * i:6144 + 1024 * (i + 1)] for i in range(2)]
    QRr = [P.Rs("QR", 2) for _ in range(2)]
    VVt = carve(4096).rearrange("p (a b) -> p a b", a=32)
    VV = [VVt, VVt]
    _vvr = P.R("VV")
    VVr = [_vvr, _vvr]
    PT = [carve(512) for i in range(2)]
    PTr = [P.R("PT") for _ in range(2)]
    RCP = [carve(1024).bitcast(F32) for i in range(1)]
    RCPr = [P.R("RCP") for _ in range(1)]
    P.add("sp", lambda e: e.dma_start(out=C[:, :], in_=constsB), wr=[k.constr], dma=k.constr)
    for q in range(4):
        P.add("pool", lambda e, q=q: e.dma_start(out=KRA[:, q * 1024:(q + 1) * 1024], in_=latr_d[:, q * 1024:(q + 1) * 1024]),
              rd=[latdr], wr=[KRAr], dma=KRAr)
    P.add("dve", lambda e: e.tensor_copy(MASKB[:, :], C[:, CB["MASK"]:CB["MASK"] + 512]),
          rd=[k.constr], wr=[MASKBr])
    COS = C[:, CB["COS"]:CB["COS"] + T]
    SIN = C[:, CB["SIN"]:CB["SIN"] + T]

    gm1 = C[:, CB["GM1"]:CB["GM1"] + 16]
    k.norm_fm(X, Xr, KC, [(0, 512, 0), (512, 512, 1)], gm1, H, Hr, 0)
    v16 = lambda s: s[:, :].rearrange("p (a b) -> p a b", a=16)
    v4 = lambda s: s[:, :].rearrange("p (a b) -> p a b", a=4)
    sl = k.load_w(w_cq.rearrange("(kc p) n -> p kc n", p=128), v16)
    for oc in range(4):
        for half in range(2):
            b = k.bank()
            for kc in range(KC):
                P.add("pe", lambda e, kc=kc, half=half, oc=oc, b=b, sl=sl: e.matmul(
                    k.ps[b][:, :], v16(k.slots[sl])[:, kc, oc * 128:(oc + 1) * 128],
                    H[:, kc, half * 512:(half + 1) * 512], start=(kc == 0), stop=(kc == KC - 1)),
                    rd=[k.slotr[sl], Hr[kc][half]], wr=[k.psr[b]])
            P.add("act", lambda e, b=b, oc=oc, half=half: e.activation(
                CQ[:, oc, half * 512:(half + 1) * 512], k.ps[b][:, :], AF.Copy),
                rd=[k.psr[b]], wr=[CQr[oc][half]])
    gcq = C[:, CB["GCQ"]:CB["GCQ"] + 4]
    k.norm_fm(CQ, CQr, 4, [(0, 512, 0), (512, 512, 1)], gcq, CQN, CQNr, 1)
    k.barrier()
    CKA = H[:, :, :].rearrange("p a b -> p (a b)").rearrange("p (c n) -> p c n", c=4)
    CKAr = P.R("CKA")
    ckr = latc_d.rearrange("(c p) n -> p c n", p=128)
    for q in range(4):
        P.add("pool", lambda e, q=q: e.dma_start(out=CKA[:, :, q * 1024:(q + 1) * 1024], in_=ckr[:, :, q * 1024:(q + 1) * 1024]),
              rd=[latdr], wr=[CKAr], dma=CKAr)
    KNt = k.slots[2][:, :].rearrange("p (a n) -> p a n", a=2)
    _knr = P.R("KN")
    KNr = [_knr, _knr]
    va = lambda s: s[:, :].rearrange("p (h kc c) -> p h kc c", h=4, kc=4)
    watt = w_att.rearrange("(kc p) h c -> p h kc c", p=128)
    woc = w_out_c.rearrange("(kc p) n -> p kc n", p=128)
    slot_seq = [0, 1]
    ps_rr = 0
    for hgp in range(4):
        s = slot_seq[hgp % 2]
        for hh in range(4):
            P.add("pool", lambda e, s=s, hh=hh, hgp=hgp: e.dma_start(
                out=va(k.slots[s])[:, hh, :, :], in_=watt[:, hgp * 4 + hh, :, :]),
                wr=[k.slotr[s]], dma=k.slotr[s])
        P.add("dve", lambda e, s=s: e.tensor_scalar(
            va(k.slots[s])[:, :, :, 192:224], va(k.slots[s])[:, :, :, 192:224], -1.0, None, ALU.mult),
            rd=[k.slotr[s]], wr=[k.slotr[s]])
        W = va(k.slots[s])
        Wr = k.slotr[s]
        for hh in range(4):
            h = hgp * 4 + hh
            pb = h % 2
            for half in range(2):
                b = 6 + (ps_rr % 2)
                ps_rr += 1
                for kc in range(4):
                    P.add("pe", lambda e, kc=kc, half=half, b=b, hh=hh, W=W: e.matmul(
                        k.ps[b][:, :], W[:, hh, kc, 0:128], CQN[:, kc, half * 512:(half + 1) * 512],
                        start=(kc == 0), stop=(kc == 3)),
                        rd=[Wr, CQNr[kc][half]], wr=[k.psr[b]])
                P.add("act", lambda e, b=b, half=half, pb=pb: e.activation(
                    QN[pb][:, half * 512:(half + 1) * 512], k.ps[b][:, :], AF.Copy),
                    rd=[k.psr[b]], wr=[QNr[pb][half]])
            for half in range(2):
                ba = 6 + (ps_rr % 2)
                ps_rr += 1
                bb = 6 + (ps_rr % 2)
                ps_rr += 1
                for (b, c0) in ((ba, 128), (bb, 192)):
                    for kc in range(4):
                        P.add("pe", lambda e, kc=kc, half=half, b=b, hh=hh, W=W, c0=c0: e.matmul(
                            k.ps[b][0:64, :], W[:, hh, kc, c0:c0 + 64], CQN[:, kc, half * 512:(half + 1) * 512],
                            start=(kc == 0), stop=(kc == 3)),
                            rd=[Wr, CQNr[kc][half]], wr=[k.psr[b]])
                t0 = tmp_rr % 2
                tmp_rr += 1
                t1 = tmp_rr % 2
                tmp_rr += 1
                P.add("dve", lambda e, ba=ba, half=half, t0=t0: e.tensor_tensor(
                    TMP[t0][0:64, :], k.ps[ba][0:64, :], COS[0:64, half * 512:(half + 1) * 512], ALU.mult),
                    rd=[k.psr[ba], k.constr], wr=[TMPr[t0]])
                P.add("dve", lambda e, bb=bb, half=half, t1=t1: e.tensor_tensor(
                    TMP[t1][0:64, :], k.ps[bb][0:64, :], SIN[0:64, half * 512:(half + 1) * 512], ALU.mult),
                    rd=[k.psr[bb], k.constr], wr=[TMPr[t1]])
                P.add("dve", lambda e, half=half, t0=t0, t1=t1, pb=pb: e.tensor_tensor(
                    QR[pb][:, half * 512:(half + 1) * 512], TMP[t0][0:64, :], TMP[t1][0:64, :], ALU.add),
                    rd=[TMPr[t0], TMPr[t1]], wr=[QRr[pb][half]])
            for kb in range(8):
                b = 6 + (ps_rr % 2)
                ps_rr += 1
                for kc in range(4):
                    P.add("pe", lambda e, kc=kc, kb=kb, b=b, hh=hh, W=W: e.matmul(
                        k.ps[b][:, :], W[:, hh, kc, 256:384], CKA[:, kc, kb * 512:(kb + 1) * 512],
                        start=(kc == 0), stop=(kc == 3)),
                        rd=[Wr, CKAr], wr=[k.psr[b]])
                P.add("act", lambda e, b=b, kb=kb, pb=pb: e.activation(
                    KNt[:, 0, kb * 512:(kb + 1) * 512], k.ps[b][:, :], AF.Copy),
                    rd=[k.psr[b]], wr=[KNr[pb]])
            for jb in range(8):
                b = 6 + (ps_rr % 2)
                ps_rr += 1
                for j4 in range(4):
                    j = jb * 4 + j4
                    for kc in range(4):
                        P.add("pe", lambda e, kc=kc, j=j, j4=j4, b=b, hh=hh, W=W: e.matmul(
                            k.ps[b][:, j4 * 128:(j4 + 1) * 128], CKA[:, kc, j * 128:(j + 1) * 128],
                            W[:, hh, kc, 384:512], start=(kc == 0), stop=(kc == 3)),
                            rd=[Wr, CKAr], wr=[k.psr[b]])
                P.add("dve", lambda e, b=b, jb=jb, pb=pb: e.tensor_copy(
                    VV[pb][:, jb * 4:(jb + 1) * 4, :], k.ps[b][:, :].rearrange("p (j d) -> p j d", j=4)),
                    rd=[k.psr[b]], wr=[VVr[pb]])
            for G in range(2):
                bo = 2 + (h * 2 + G) % 2
                bl = 4 + (h * 2 + G) % 2
                nblk = 16 * G + 16
                def blk(j, G=G):
                    ip, rp = j // 4, j % 4
                    kcol = rp * 1024 + ip * 128
                    imin = max(ip, 4 * G)
                    c0 = (imin - 4 * G) * 128
                    return ip, rp, kcol, c0, 512 - c0, G * 512 + c0

                def emit_S(j, pb=pb, G=G):
                    ip, rp, kcol, c0, n, q0 = blk(j)
                    bs = j % 2
                    P.add("pe", lambda e, bs=bs, pb=pb, kcol=kcol, q0=q0, n=n: e.matmul(
                        k.ps[bs][:, 0:n], KNt[:, 0, kcol:kcol + 128], QN[pb][:, q0:q0 + n],
                        start=True, stop=False),
                        rd=[KNr[pb], QNr[pb][G]], wr=[k.psr[bs]])
                    P.add("pe", lambda e, bs=bs, pb=pb, kcol=kcol, q0=q0, n=n: e.matmul(
                        k.ps[bs][:, 0:n], KRA[:, kcol:kcol + 128], QR[pb][:, q0:q0 + n],
                        start=False, stop=True),
                        rd=[KRAr, QRr[pb][G]], wr=[k.psr[bs]])

                emit_S(0)
                for j in range(nblk):
                    if j + 1 < nblk:
                        emit_S(j + 1)
                    ip, rp, kcol, c0, n, q0 = blk(j)
                    bs = j % 2
                    pi = j % 2
                    P.add("act", lambda e, bs=bs, pi=pi, n=n: e.activation(
                        PT[pi][:, 0:n], k.ps[bs][:, 0:n], AF.Exp, scale=SM_SCALE),
                        rd=[k.psr[bs]], wr=[PTr[pi]])
                    if ip >= 4 * G:
                        P.add("dve", lambda e, pi=pi, rp=rp: e.tensor_tensor(
                            PT[pi][:, 0:128], PT[pi][:, 0:128], MASKB[:, rp * 128:(rp + 1) * 128], ALU.mult),
                            rd=[PTr[pi], MASKBr], wr=[PTr[pi]])
                    P.add("pe", lambda e, bo=bo, pb=pb, j=j, pi=pi, c0=c0, n=n, nblk=nblk: e.matmul(
                        k.ps[bo][:, c0:c0 + n], VV[pb][:, (j % 4) * 8 + j // 4, :], PT[pi][:, 0:n],
                        start=(j == 0), stop=(j == nblk - 1)),
                        rd=[VVr[pb], PTr[pi]], wr=[k.psr[bo]])
                    P.add("pe", lambda e, bl=bl, pi=pi, c0=c0, n=n, j=j, nblk=nblk: e.matmul(
                        k.ps[bl][:, c0:c0 + n], k.ones[:, 2, :], PT[pi][:, 0:n],
                        start=(j == 0), stop=(j == nblk - 1)),
                        rd=[k.onesr, PTr[pi]], wr=[k.psr[bl]])
                ri = 0
                P.add("dve", lambda e, bl=bl, ri=ri: e.reciprocal(RCP[ri][:, :], k.ps[bl][:, :]),
                      rd=[k.psr[bl]], wr=[RCPr[ri]])
                ob = hh
                P.add("dve", lambda e, bo=bo, ri=ri, ob=ob, G=G: e.tensor_tensor(
                    OT[:, ob, G * 512:(G + 1) * 512], k.ps[bo][:, :], RCP[ri][:, :], ALU.mult),
                    rd=[k.psr[bo], RCPr[ri]], wr=[OTr[ob][G]])
        ws = slot_seq[(hgp + 1) % 2]
        P.add("pool", lambda e, ws=ws, hgp=hgp: e.dma_start(
            out=v4(k.slots[ws]), in_=woc[:, hgp * 4:(hgp + 1) * 4, :]),
            wr=[k.slotr[ws]], dma=k.slotr[ws])
        for oc in range(KC):
            for half in range(2):
                b = 6 + (ps_rr % 2)
                ps_rr += 1
                for kc in range(4):
                    ob = kc
                    P.add("pe", lambda e, kc=kc, half=half, oc=oc, b=b, ws=ws, ob=ob: e.matmul(
                        k.ps[b][:, :], v4(k.slots[ws])[:, kc, oc * 128:(oc + 1) * 128],
                        OT[:, ob, half * 512:(half + 1) * 512], start=(kc == 0), stop=(kc == 3)),
                        rd=[k.slotr[ws], OTr[ob][half]], wr=[k.psr[b]])
                P.add("dve", lambda e, b=b, oc=oc, half=half: e.tensor_tensor(
                    X[:, oc, half * 512:(half + 1) * 512], X[:, oc, half * 512:(half + 1) * 512],
                    k.ps[b][:, :], ALU.add),
                    rd=[k.psr[b], Xr[oc][half]], wr=[Xr[oc][half]])
    k.barrier()
    k.slot_rr = 0
    k.ACTB = [OT, VVt.rearrange("p a b -> p (a b)").rearrange("p (a b) -> p a b", a=4)]
    k.ACTBr = P.Rs("ACTB", 2, 4, 2)
    gf1 = C[:, CB["GF1"]:CB["GF1"] + 16]
    k.norm_fm(X, Xr, KC, [(0, 512, 0), (512, 512, 1)], gf1, H, Hr, 0)
    k.ffn(w_gate, w_up, w_down, 1)
    k.barrier()
    gfin = C[:, CB["GFIN"]:CB["GFIN"] + 16]
    YF = H[:, :, :].rearrange("p a b -> p (a b)").bitcast(F32).rearrange("p (a b) -> p a b", a=8)
    YFr = P.Rs("YF", 8, 2)
    yr = yT.rearrange("(kc p) t -> p kc t", p=128)
    outr = P.R("out")
    ones = k.ones
    rst = []
    for (c0, n, pi) in [(0, 512, 0), (512, 512, 1)]:
        b = k.bank()
        for kc in range(KC):
            qi = k.sq_rr % 2
            k.sq_rr += 1
            P.add("act", lambda e, qi=qi, kc=kc, c0=c0, n=n: e.activation(
                k.sq[qi][:, 0:n], X[:, kc, c0:c0 + n], AF.Square),
                rd=[Xr[kc][pi]], wr=[k.sqr[qi]])
            P.add("pe", lambda e, qi=qi, kc=kc, n=n, b=b: e.matmul(
                k.ps[b][:, 0:n], ones[:, 0, :], k.sq[qi][:, 0:n], start=(kc == 0), stop=(kc == KC - 1)),
                rd=[k.sqr[qi], k.onesr], wr=[k.psr[b]])
        P.add("act", lambda e, b=b, pi=pi: e.activation(k.rstd[pi][:, :], k.ps[b][:, :], AF.Sqrt, bias=EPS, scale=1.0),
              rd=[k.psr[b]], wr=[k.rstdr[pi]])
        P.add("dve", lambda e, pi=pi: e.reciprocal(k.rstd[pi][:, :], k.rstd[pi][:, :]),
              rd=[k.rstdr[pi]], wr=[k.rstdr[pi]])
    for part in range(2):
        for kc8 in range(8):
            kc = part * 8 + kc8
            for pi in range(2):
                P.add("dve", lambda e, kc=kc, kc8=kc8, pi=pi: e.scalar_tensor_tensor(
                    YF[:, kc8, pi * 512:(pi + 1) * 512], X[:, kc, pi * 512:(pi + 1) * 512], gfin[:, kc:kc + 1],
                    k.rstd[pi][:, :], ALU.mult, ALU.mult),
                    rd=[Xr[kc][pi], k.rstdr[pi], k.constr], wr=[YFr[kc8][pi]])
        P.add("sp", lambda e, part=part: e.dma_start(out=yr[:, part * 8:(part + 1) * 8, :], in_=YF),
              rd=[YFr[c][h] for c in range(8) for h in range(2)], wr=[outr], dma=outr)
    P.finish("sp", [outr])
    P.emit(nc, k.es)
    k.es.close()
    return nc


def build_G():
    k = KB()
    nc, P = k.nc, k.P
    xT = k.din("xT", [1, D, T])
    xhT = k.din("xhT", [1, D, 128])
    constsA = k.din("constsA", [1, 128, CA["N"]])
    gvbs = k.din("gvbs", [128, 2048])
    cossin = k.din("cossin", [1, 128, 2048])
    constsB = k.din("constsB", [128, CB["N"]])
    w_in_ab = k.din("w_in_ab", [D, 3072])
    w_sT = k.din("w_sT", [8, 128, 128])
    w_pool = k.din("w_pool", [4, 256, 256])
    w_out_ab = k.din("w_out_ab", [D, D])
    w_gate2 = k.din("w_gate", [2, D, DFF])
    w_up2 = k.din("w_up", [2, D, DFF])
    w_down2 = k.din("w_down", [2, DFF, D])
    w_ckv = k.din("w_ckv", [D, 512])
    w_kr = k.din("w_kr", [D, 128])
    w_cq = k.din("w_cq", [D, 512])
    w_att = k.din("w_att", [512, 16, 512])
    w_out_c = k.din("w_out_c", [D, D])
    yT = k.dout("yT", [D, T])
    lat_own = [nc.dram_tensor(f"lat_own{i}", [n, T], BF16, kind="Internal").ap() for i, n in enumerate((512, 64))]
    lat_g = [nc.dram_tensor(f"lat_g{i}", [4 * n, T], BF16, kind="Internal").ap() for i, n in enumerate((512, 64))]
    latr_d = lat_own[1]
    latgr = [P.R("latg") for _ in range(2)]
    latdr = P.R("latd")

    k.common(2)
    X = k.X = k.sb("X", [128, KC, T], F32)
    Xr = k.Xr = P.Rs("X", KC, 2)
    H = k.H = k.sb("H", [128, KC, T], BF16)
    Hr = k.Hr = P.Rs("H", KC, 2)
    TG = [k.sb(f"TG{i}", [128, 512], F32) for i in range(2)]
    TGr = [P.R("TG") for _ in range(2)]
    k.SG = TG
    k.SGr = TGr
    TMP, TMPr = TG, TGr
    ARENA_N = 24320
    ARENA = k.sb("ARENA", [128, ARENA_N], BF16)
    apos = [0]

    def carve(nelem_bf16):
        a = apos[0]
        apos[0] = a + nelem_bf16
        assert apos[0] <= ARENA_N, apos[0]
        return ARENA[:, a:a + nelem_bf16]

    HH = carve(2048).rearrange("p (a b) -> p a b", a=KC)
    HHr = P.Rs("HH", KC, 1)
    C_A = carve(2 * CA["N"]).bitcast(F32)
    constr_A = P.R("const")
    U = carve(8192).rearrange("p (a b) -> p a b", a=8)
    Ur = P.Rs("U", 8, NT)
    BIG = carve(8192)
    bigr = P.R("BIG")
    BT = U
    BTr = P.Rs("BT", 8, 2)
    S2F = k.slots[2][:, :].bitcast(F32)
    GVBS = S2F[:, 0:2048]
    gvbsr = P.R("GVBS")
    Z = S2F[:, 2048:2048 + 576].rearrange("p (i t) -> p i t", i=4)
    Zr = P.R("Z")
    ZS = [S2F[:, 2624 + 576 * i:2624 + 576 * (i + 1)].rearrange("p (i t) -> p i t", i=4) for i in range(2)]
    ZSr = [P.R("ZS") for _ in range(2)]
    WSWP = carve(2048)
    WST = WSWP[:, 0:1024].rearrange("p (h t) -> p h t", h=8)
    WP = WSWP[:, :].rearrange("p (g c d) -> p g c d", g=4, c=2)
    WSTr = P.R("WSWP")
    WPr = WSTr
    SSQ = carve(32).bitcast(F32)
    SSQr = P.Rs("SSQ", 16)
    RV = carve(16).bitcast(F32)
    RVr = P.R("RV")
    w_gate, w_up, w_down = w_gate2, w_up2, w_down2

    def pass_A(s):
        C = C_A
        k.constr = constr_A
        xr = xT[s].rearrange("(kc p) t -> p kc t", p=128)
        Xin = P.Rs("Xin", 4)
        for q in range(4):
            hq, cg = q // 2, q % 2
            P.add("sp", lambda e, hq=hq, cg=cg: e.dma_start(
                out=X[:, 8 * cg:8 * cg + 8, hq * 512:(hq + 1) * 512],
                in_=xr[:, 8 * cg:8 * cg + 8, hq * 512:(hq + 1) * 512]),
                wr=[Xr[kc][hq] for kc in range(8 * cg, 8 * cg + 8)] + [Xin[q]], dma=Xin[q])
        P.add("sp", lambda e: e.dma_start(out=C[:, :], in_=constsA[s]), wr=[k.constr], dma=k.constr)
        XHt = BIG[:, 0:4096].bitcast(F32).rearrange("p (a b) -> p a b", a=KC)
        XHr = [[bigr] for _ in range(KC)]
        XHd = P.R("XHd")
        P.add("sp", lambda e: e.dma_start(out=XHt, in_=xhT[s].rearrange("(kc p) t -> p kc t", p=128)),
              wr=[bigr, XHd], dma=XHd)
        P.add("sp", lambda e: e.dma_start(out=GVBS, in_=gvbs), wr=[gvbsr], dma=gvbsr)
        P.add("pool", lambda e: e.dma_start(out=WST, in_=w_sT.rearrange("h s t -> s h t")),
              wr=[WSTr], dma=WSTr)
        P.add("dve", lambda e: e.memset(WST[64:128, :, 0:64], 0.0), wr=[WSTr])

        gm0 = C[:, CA["GM0"]:CA["GM0"] + 16]
        k.norm_fm(X, Xr, KC, [(0, 512, 0), (512, 512, 1)], gm0, H, Hr, 0)

        v16 = lambda s: s[:, :].rearrange("p (a b) -> p a b", a=16)
        v8 = lambda s: s[:, 0:4096].rearrange("p (a b) -> p a b", a=8)
        wab = w_in_ab.rearrange("(kc p) n -> p kc n", p=128)
        VG = BIG[:, :].rearrange("p (i c) -> p i c", i=NT)
        VGr = P.Rs("VG", NT)
        k.slot_n = 2

        for sblk in range(2):
            sl = k.load_w(wab[:, :, sblk * 512:(sblk + 1) * 512], v16)
            for half in range(2):
                for oc in range(4):
                    hd = sblk * 4 + oc
                    b = k.bank()
                    for kc in range(KC):
                        P.add("pe", lambda e, kc=kc, half=half, oc=oc, b=b, sl=sl: e.matmul(
                            k.ps[b][:, :], v16(k.slots[sl])[:, kc, oc * 128:(oc + 1) * 128],
                            H[:, kc, half * 512:(half + 1) * 512], start=(kc == 0), stop=(kc == KC - 1)),
                            rd=[k.slotr[sl], Hr[kc][half]], wr=[k.psr[b]])
                    P.add("act", lambda e, b=b, hd=hd, half=half: e.activation(
                        U[:, hd, half * 512:(half + 1) * 512], k.ps[b][:, :], AF.Gelu_apprx_tanh),
                        rd=[k.psr[b]], wr=[Ur[hd][i] for i in range(4 * half, 4 * half + 4)])
        k.norm_fm(XHt, XHr, KC, [(0, 128, 0)], gm0, HH, HHr, 0)
        tg_rr = 0
        for sblk in range(2):
            sl = k.load_w(wab[:, :, 1024 + sblk * 512:1024 + (sblk + 1) * 512], v16)
            for i in range(NT):
                b = k.bank()
                for kc in range(KC):
                    P.add("pe", lambda e, kc=kc, i=i, b=b, sl=sl: e.matmul(
                        k.ps[b][:, :], H[:, kc, i * 128:(i + 1) * 128], v16(k.slots[sl])[:, kc, :],
                        start=(kc == 0), stop=(kc == KC - 1)),
                        rd=[k.slotr[sl], Hr[kc][i // 4]], wr=[k.psr[b]])
                ti = tg_rr % 2
                tg_rr += 1
                P.add("act", lambda e, b=b, ti=ti: e.activation(TG[ti][:, :], k.ps[b][:, :], AF.Gelu_apprx_tanh),
                      rd=[k.psr[b]], wr=[TGr[ti]])
                qi = k.sq_rr % 2
                k.sq_rr += 1
                P.add("act", lambda e, ti=ti, i=i, sblk=sblk, qi=qi: e.activation(
                    k.sq[qi][:, :], TG[ti][:, :], AF.Square,
                    accum_out=SSQ[:, 2 * i + sblk:2 * i + sblk + 1]),
                    rd=[TGr[ti]], wr=[k.sqr[qi], SSQr[2 * i + sblk]])
                P.add("dve", lambda e, ti=ti, i=i, sblk=sblk: e.tensor_copy(
                    VG[:, i, sblk * 512:(sblk + 1) * 512], TG[ti][:, :]),
                    rd=[TGr[ti]], wr=[VGr[i], bigr])
        SS3 = SSQ[:, :].rearrange("p (i s) -> p i s", s=2)
        P.add("dve", lambda e: e.tensor_tensor(RV[:, :], SS3[:, :, 0], SS3[:, :, 1], ALU.add),
              rd=SSQr, wr=[RVr])
        P.add("act", lambda e: e.activation(RV[:, :], RV[:, :], AF.Sqrt, bias=EPS, scale=1.0 / 1024),
              rd=[RVr], wr=[RVr])
        P.add("dve", lambda e: e.reciprocal(RV[:, :], RV[:, :]), rd=[RVr], wr=[RVr])
        GV = GVBS[:, 0:1024]
        BS = GVBS[:, 1024:2048]
        for i in range(NT):
            P.add("dve", lambda e, i=i: e.scalar_tensor_tensor(
                VG[:, i, :], VG[:, i, :], RV[:, i:i + 1], GV, ALU.mult, ALU.mult),
                rd=[VGr[i], RVr, gvbsr], wr=[VGr[i]])
        tmp_rr = 0
        for i in range(NT):
            for hg in range(2):
                b = k.bank()
                for h4 in range(4):
                    hd = hg * 4 + h4
                    P.add("pe", lambda e, i=i, hd=hd, h4=h4, b=b: e.matmul(
                        k.ps[b][:, h4 * 128:(h4 + 1) * 128], VG[:, i, hd * 128:(hd + 1) * 128], WST[:, hd, :],
                        start=True, stop=True),
                        rd=[VGr[i], WSTr], wr=[k.psr[b]])
                ti = tmp_rr % 2
                tmp_rr += 1
                P.add("dve", lambda e, b=b, ti=ti, hg=hg: e.tensor_tensor(
                    TMP[ti][:, :], k.ps[b][:, :], BS[:, hg * 512:(hg + 1) * 512], ALU.add),
                    rd=[k.psr[b], gvbsr], wr=[TMPr[ti]])
                P.add("dve", lambda e, ti=ti, hg=hg, i=i: e.tensor_tensor(
                    U[:, hg * 4:(hg + 1) * 4, i * 128:(i + 1) * 128],
                    U[:, hg * 4:(hg + 1) * 4, i * 128:(i + 1) * 128],
                    TMP[ti][:, :].rearrange("p (h t) -> p h t", h=4), ALU.mult),
                    rd=[TMPr[ti]] + [Ur[hg * 4 + h][i] for h in range(4)],
                    wr=[Ur[hg * 4 + h][i] for h in range(4)])

        wo = w_out_ab.rearrange("(kc p) n -> p kc n", p=128)

        def out_proj(part):
            for sblk in range(2):
                sl = k.load_w(wo[:, part * 8:(part + 1) * 8, sblk * 1024:(sblk + 1) * 1024],
                              lambda s: s[:, :].rearrange("p (a b) -> p a b", a=8))
                W8 = k.slots[sl][:, :].rearrange("p (a b) -> p a b", a=8)
                for oc in range(8):
                    ocg = sblk * 8 + oc
                    for half in range(2):
                        b = k.bank()
                        for kc in range(8):
                            if part == 0:
                                rr = [Ur[kc][i] for i in range(4 * half, 4 * half + 4)]
                            else:
                                rr = [BTr[kc][half]]
                            P.add("pe", lambda e, kc=kc, oc=oc, b=b, W8=W8, half=half: e.matmul(
                                k.ps[b][:, :], W8[:, kc, oc * 128:(oc + 1) * 128],
                                U[:, kc, half * 512:(half + 1) * 512],
                                start=(kc == 0), stop=(kc == 7)),
                                rd=[k.slotr[sl]] + rr, wr=[k.psr[b]])
                        P.add("dve", lambda e, b=b, ocg=ocg, half=half: e.tensor_tensor(
                            X[:, ocg, half * 512:(half + 1) * 512], X[:, ocg, half * 512:(half + 1) * 512],
                            k.ps[b][:, :], ALU.add),
                            rd=[k.psr[b], Xr[ocg][half]], wr=[Xr[ocg][half]])

        out_proj(0)
        P.add("pool", lambda e: e.dma_start(out=WP, in_=w_pool.rearrange("g (cc p) d -> p g cc d", p=128)),
              wr=[WPr], dma=WPr)
        UALL = [Ur[h][i] for h in range(8) for i in range(NT)]

        DT = BIG[:, :].rearrange("p (c t) -> p c t", c=8)
        DTr = VGr
        INVC = C[:, CA["INVC"]:CA["INVC"] + 64]
        PSC = C[:, CA["PSC"]:CA["PSC"] + 8]
        wins = (2, 4, 8, 16)
        for sblk in range(2):
            sl = k.load_w(wab[:, :, 2048 + sblk * 512:2048 + (sblk + 1) * 512], v16)
            for oc in range(4):
                c = sblk * 4 + oc
                g = c // 2
                bm = [k.bank(), k.bank()]
                bh = k.bank()
                for half in range(2):
                    for kc in range(KC):
                        P.add("pe", lambda e, kc=kc, half=half, oc=oc, b=bm[half], sl=sl: e.matmul(
                            k.ps[b][:, :], v16(k.slots[sl])[:, kc, oc * 128:(oc + 1) * 128],
                            H[:, kc, half * 512:(half + 1) * 512], start=(kc == 0), stop=(kc == KC - 1)),
                            rd=[k.slotr[sl], Hr[kc][half]], wr=[k.psr[bm[half]]])
                for kc in range(KC):
                    P.add("pe", lambda e, kc=kc, oc=oc, b=bh, sl=sl: e.matmul(
                        k.ps[b][:, 0:128], v16(k.slots[sl])[:, kc, oc * 128:(oc + 1) * 128],
                        HH[:, kc, :], start=(kc == 0), stop=(kc == KC - 1)),
                        rd=[k.slotr[sl], HHr[kc][0]], wr=[k.psr[bh]])
                for th in range(2):
                    P.add("act", lambda e, th=th, b=bm[th]: e.activation(
                        Z[:, :, 16:144], k.ps[b][:, :].rearrange("p (i t) -> p i t", i=4), AF.Copy),
                        rd=[k.psr[bm[th]]], wr=[Zr])
                    P.add("act", lambda e, b=bh, th=th: e.activation(
                        Z[:, :, 0:16], k.ps[b][:, th * 64:(th + 1) * 64].rearrange("p (i t) -> p i t", i=4), AF.Copy),
                        rd=[k.psr[bh]], wr=[Zr])
                    cur, curr = Z, Zr
                    sh = 1
                    zi = 0
                    while sh < wins[g]:
                        nxt, nxtr = ZS[zi % 2], ZSr[zi % 2]
                        zi += 1
                        P.add("dve", lambda e, cur=cur, nxt=nxt, sh=sh: e.tensor_tensor(
                            nxt[:, :, 2 * sh - 1:144], cur[:, :, 2 * sh - 1:144], cur[:, :, sh - 1:144 - sh], ALU.add),
                            rd=[curr], wr=[nxtr])
                        cur, curr = nxt, nxtr
                        sh *= 2
                    P.add("dve", lambda e, cur=cur, c=c, g=g, th=th: e.scalar_tensor_tensor(
                        DT[:, c, th * 512:(th + 1) * 512].rearrange("p (i t) -> p i t", i=4), cur[:, :, 16:144],
                        1.0 / wins[g], Z[:, :, 16:144], ALU.mult, ALU.subtract),
                        rd=[curr, Zr], wr=DTr[4 * th:4 * th + 4] + [bigr])
                    if th == 0:
                        ti = tmp_rr % 2
                        tmp_rr += 1
                        P.add("dve", lambda e, cur=cur, ti=ti, g=g: e.tensor_tensor(
                            TMP[ti][:, 0:16], cur[:, 0, 16:32], INVC[:, g * 16:(g + 1) * 16], ALU.mult),
                            rd=[curr, k.constr], wr=[TMPr[ti]])
                        P.add("dve", lambda e, ti=ti, c=c: e.tensor_tensor(
                            DT[:, c, 0:16], TMP[ti][:, 0:16], Z[:, 0, 16:32], ALU.subtract),
                            rd=[TMPr[ti], Zr], wr=[DTr[0], bigr])
                if c % 2 == 1:
                    for dc in range(2):
                        for half in range(2):
                            b = k.bank()
                            for cc in range(2):
                                P.add("pe", lambda e, g=g, cc=cc, dc=dc, half=half, b=b: e.matmul(
                                    k.ps[b][:, :], WP[:, g, cc, dc * 128:(dc + 1) * 128],
                                    DT[:, 2 * g + cc, half * 512:(half + 1) * 512], start=(cc == 0), stop=(cc == 1)),
                                    rd=[WPr] + DTr[4 * half:4 * half + 4], wr=[k.psr[b]])
                            P.add("act", lambda e, g=g, dc=dc, half=half, b=b: e.activation(
                                BT[:, 2 * g + dc, half * 512:(half + 1) * 512], k.ps[b][:, :], AF.Copy,
                                scale=PSC[:, 2 * g + dc:2 * g + dc + 1]),
                                rd=[k.psr[b], k.constr], wr=[BTr[2 * g + dc][half]] + (UALL if (g == 0 and dc == 0 and half == 0) else []))
        out_proj(1)
        k.barrier()
        k.slot_n = 3
        k.ACTB = [BIG[:, 0:4096].rearrange("p (a b) -> p a b", a=4), BIG[:, 4096:8192].rearrange("p (a b) -> p a b", a=4)]
        k.ACTBr = P.Rs("ACTB", 2, 4, 2)
        gf0 = C[:, CA["GF0"]:CA["GF0"] + 16]
        k.norm_fm(X, Xr, KC, [(0, 512, 0), (512, 512, 1)], gf0, H, Hr, 0)
        k.ffn(w_gate, w_up, w_down, 0)
        k.barrier()
        k.slot_n = 2
        k.slot_rr = 0
        CSr = P.R("CS")
        P.add("sp", lambda e: e.dma_start(out=S2F[:, 0:2048], in_=cossin[s]), wr=[CSr], dma=CSr)
        outr = latdr
        gm1 = C[:, CA["GM1"]:CA["GM1"] + 16]
        k.norm_fm(X, Xr, KC, [(0, 512, 0), (512, 512, 1)], gm1, H, Hr, 0)

        def gather(pc):
            P.add("pool", lambda e, pc=pc: e.collective_compute(
                "AllGather", ALU.bypass, replica_groups=[[0, 1, 2, 3], [4, 5, 6, 7]],
                ins=[lat_own[pc]], outs=[lat_g[pc]]),
                rd=[latdr], wr=[latgr[pc]], dma=latgr[pc], inc=1)
            P.alltok.pop(("d", latgr[pc]), None)

        wckv = w_ckv.rearrange("(kc p) n -> p kc n", p=128)
        sl = k.load_w(wckv, v16)
        v128 = lambda s: s[:, 0:2048].rearrange("p (a b) -> p a b", a=16)
        sl_kr = k.load_w(w_kr.rearrange("(kc p) n -> p kc n", p=128), v128)
        CKV = BIG[:, :].bitcast(F32).rearrange("p (a b) -> p a b", a=4)
        CKVr = P.Rs("CKV", 4, 2)
        for oc in range(4):
            for half in range(2):
                b = k.bank()
                for kc in range(KC):
                    P.add("pe", lambda e, kc=kc, half=half, oc=oc, b=b, sl=sl: e.matmul(
                        k.ps[b][:, :], v16(k.slots[sl])[:, kc, oc * 128:(oc + 1) * 128],
                        H[:, kc, half * 512:(half + 1) * 512], start=(kc == 0), stop=(kc == KC - 1)),
                        rd=[k.slotr[sl], Hr[kc][half]], wr=[k.psr[b]])
                P.add("act", lambda e, b=b, oc=oc, half=half: e.activation(
                    CKV[:, oc, half * 512:(half + 1) * 512], k.ps[b][:, :], AF.Copy),
                    rd=[k.psr[b]], wr=[CKVr[oc][half]])
        CKN = U[:, 0:4, :]
        CKNr = P.Rs("CKN", 4, 2)
        gckv = C[:, CA["GCKV"]:CA["GCKV"] + 4]
        k.norm_fm(CKV, CKVr, 4, [(0, 512, 0), (512, 512, 1)], gckv, CKN, CKNr, 1)
        P.add("sp", lambda e: e.dma_start(out=lat_own[0].rearrange("(kc p) t -> p kc t", p=128), in_=CKN),
              rd=[CKNr[c][h] for c in range(4) for h in range(2)], wr=[outr], dma=outr)
        gather(0)
        KRB = U[0:64, 4, :]
        KRBr = P.R("KRB")
        sl = sl_kr
        P.add("dve", lambda e, sl=sl: e.tensor_scalar(
            v128(k.slots[sl])[:, :, 64:96], v128(k.slots[sl])[:, :, 64:96], -1.0, None, ALU.mult),
            rd=[k.slotr[sl]], wr=[k.slotr[sl]])
        KTMP = k.rstd[0][0:64, :]
        KTMPr = k.rstdr[0]
        COS = S2F[:, 0:T]
        SIN = S2F[:, T:2 * T]
        for half in range(2):
            ba, bb = k.bank(), k.bank()
            for (b, c0) in ((ba, 0), (bb, 64)):
                for kc in range(KC):
                    P.add("pe", lambda e, kc=kc, half=half, b=b, c0=c0, sl=sl: e.matmul(
                        k.ps[b][0:64, :], v128(k.slots[sl])[:, kc, c0:c0 + 64],
                        H[:, kc, half * 512:(half + 1) * 512], start=(kc == 0), stop=(kc == KC - 1)),
                        rd=[k.slotr[sl], Hr[kc][half]], wr=[k.psr[b]])
            P.add("dve", lambda e, ba=ba, half=half: e.tensor_tensor(
                TG[half][0:64, :], k.ps[ba][0:64, :], COS[0:64, half * 512:(half + 1) * 512], ALU.mult),
                rd=[k.psr[ba], CSr], wr=[TGr[half]])
            P.add("dve", lambda e, bb=bb, half=half: e.tensor_tensor(
                KTMP[:, :], k.ps[bb][0:64, :], SIN[0:64, half * 512:(half + 1) * 512], ALU.mult),
                rd=[k.psr[bb], CSr], wr=[KTMPr])
            P.add("dve", lambda e, half=half: e.tensor_tensor(
                KRB[:, half * 512:(half + 1) * 512], TG[half][0:64, :], KTMP[:, :], ALU.add),
                rd=[KTMPr, TGr[half]], wr=[KRBr])
            P.add("sp", lambda e, half=half: e.dma_start(out=latr_d[:, half * 512:(half + 1) * 512],
                                                         in_=KRB[:, half * 512:(half + 1) * 512]),
                  rd=[KRBr], wr=[outr], dma=outr)
        gather(1)

    for s_ in range(1):
        k.slot_n = 3
        k.slot_rr = 0
        pass_A(s_)
        k.barrier()

    apos[0] = 0
    k.slot_n = 3
    k.slot_rr = 0
    C = carve(2 * CB["N"]).bitcast(F32)
    k.constr = P.R("constB")
    tmp_rr = 0
    CQ = k.slots[2][:, :].bitcast(F32).rearrange("p (a b) -> p a b", a=4)
    CQr = P.Rs("CQ", 4, 2)
    CQN = carve(4096).rearrange("p (a b) -> p a b", a=4)
    CQNr = P.Rs("CQN", 4, 2)
    KRA = carve(4096)[0:64, :]
    KRAr = P.R("KRA")
    MASKB = carve(512)
    MASKBr = P.R("MASKB")
    OT = carve(4096).rearrange("p (a b) -> p a b", a=4)
    OTr = P.Rs("OT", 4, 2)
    S2 = k.slots[2]
    QN = [S2[:, 4096 + 1024 * i:4096 + 1024 * (i + 1)] for i in range(2)]
    QNr = [P.Rs("QN", 2) for _ in range(2)]
    QR = [S2[0:64, 6144 + 1024 * i:6144 + 1024 * (i + 1)] for i in range(2)]
    QRr = [P.Rs("QR", 2) for _ in range(2)]
    VVt = carve(4096).rearrange("p (a b) -> p a b", a=32)
    VV = [VVt, VVt]
    _vvr = P.R("VV")
    VVr = [_vvr, _vvr]
    PT = [carve(512) for i in range(2)]
    PT.append(C[:, CB["MASK"]:CB["MASK"] + 256].bitcast(BF16))
    PTr = [P.R("PT") for _ in range(3)]
    RCPraw = carve(1024)
    RCP = [RCPraw.bitcast(F32)]
    RCPr = [P.R("RCP") for _ in range(1)]
    ACC = [RCPraw[:, 0:512], RCPraw[:, 512:1024]]
    ACCr = [P.R("ACC") for _ in range(2)]
    P.add("sp", lambda e: e.dma_start(out=C[:, :], in_=constsB), wr=[k.constr], dma=k.constr)
    P.add("dve", lambda e: e.tensor_copy(MASKB[:, :], C[:, CB["MASK"]:CB["MASK"] + 512]),
          rd=[k.constr], wr=[MASKBr])
    COS = C[:, CB["COS"]:CB["COS"] + T]
    SIN = C[:, CB["SIN"]:CB["SIN"] + T]

    v16 = lambda s: s[:, :].rearrange("p (a b) -> p a b", a=16)
    v4 = lambda s: s[:, :].rearrange("p (a b) -> p a b", a=4)
    sl = k.load_w(w_cq.rearrange("(kc p) n -> p kc n", p=128), v16)
    va = lambda s: s[:, :].rearrange("p (h kc c) -> p h kc c", h=4, kc=4)
    watt = w_att.rearrange("(kc p) h c -> p h kc c", p=128)
    slot_seq = [1, 0]

    def load_att(hgp, dma=True, neg=True):
        s = slot_seq[hgp % 2]
        for hh in range(4):
            if not dma:
                break
            P.add("pool", lambda e, s=s, hh=hh, hgp=hgp: e.dma_start(
                out=va(k.slots[s])[:, hh, :, :], in_=watt[:, hgp * 4 + hh, :, :]),
                wr=[k.slotr[s]], dma=k.slotr[s])
        if not neg:
            return
        P.add("dve", lambda e, s=s: e.tensor_scalar(
            va(k.slots[s])[:, :, :, 192:224], va(k.slots[s])[:, :, :, 192:224], -1.0, None, ALU.mult),
            rd=[k.slotr[s]], wr=[k.slotr[s]])

    load_att(0, neg=False)
    for oc in range(4):
        for half in range(2):
            b = k.bank()
            for kc in range(KC):
                P.add("pe", lambda e, kc=kc, half=half, oc=oc, b=b, sl=sl: e.matmul(
                    k.ps[b][:, :], v16(k.slots[sl])[:, kc, oc * 128:(oc + 1) * 128],
                    H[:, kc, half * 512:(half + 1) * 512], start=(kc == 0), stop=(kc == KC - 1)),
                    rd=[k.slotr[sl], Hr[kc][half]], wr=[k.psr[b]])
            P.add("act", lambda e, b=b, oc=oc, half=half: e.activation(
                CQ[:, oc, half * 512:(half + 1) * 512], k.ps[b][:, :], AF.Copy),
                rd=[k.psr[b]], wr=[CQr[oc][half]])
    gcq = C[:, CB["GCQ"]:CB["GCQ"] + 4]
    k.norm_fm(CQ, CQr, 4, [(0, 512, 0), (512, 512, 1)], gcq, CQN, CQNr, 1)
    k.barrier()
    CKA = H[:, :, :].rearrange("p a b -> p (a b)").rearrange("p (c n) -> p c n", c=4)
    CKAr = [P.R("CKA") for _ in range(4)]
    for q in range(4):
        P.add("sp", lambda e, q=q: e.dma_start(
            out=KRA[:, q * 1024:(q + 1) * 1024], in_=lat_g[1][q * 64:(q + 1) * 64, :]),
            rd=[latgr[1]], wr=[KRAr], dma=KRAr)
    for q in range(4):
        P.add("sp", lambda e, q=q: e.dma_start(
            out=CKA[:, :, q * 1024:(q + 1) * 1024],
            in_=lat_g[0][q * 512:(q + 1) * 512, :].rearrange("(c p) t -> p c t", p=128)),
            rd=[latgr[0]], wr=[CKAr[q]], dma=CKAr[q])
    KNt = k.slots[2][:, :].rearrange("p (a n) -> p a n", a=2)
    _knr = P.R("KN")
    KNr = [_knr, _knr]
    va = lambda s: s[:, :].rearrange("p (h kc c) -> p h kc c", h=4, kc=4)
    watt = w_att.rearrange("(kc p) h c -> p h kc c", p=128)
    woc = w_out_c.rearrange("(kc p) n -> p kc n", p=128)
    slot_seq = [1, 0]
    ps_rr = 0
    for hgp in range(4):
        s = slot_seq[hgp % 2]
        load_att(hgp, dma=False)
        W = va(k.slots[s])
        Wr = k.slotr[s]
        for hh in range(4):
            h = hgp * 4 + hh
            pb = h % 2
            for half in range(2):
                b = (6, 7, 0, 1)[ps_rr % 4]
                ps_rr += 1
                for kc in range(4):
                    P.add("pe", lambda e, kc=kc, half=half, b=b, hh=hh, W=W: e.matmul(
                        k.ps[b][:, :], W[:, hh, kc, 0:128], CQN[:, kc, half * 512:(half + 1) * 512],
                        start=(kc == 0), stop=(kc == 3)),
                        rd=[Wr, CQNr[kc][half]], wr=[k.psr[b]])
                P.add("act", lambda e, b=b, half=half, pb=pb: e.activation(
                    QN[pb][:, half * 512:(half + 1) * 512], k.ps[b][:, :], AF.Copy),
                    rd=[k.psr[b]], wr=[QNr[pb][half]])
            for half in range(2):
                ba = (6, 7, 0, 1)[ps_rr % 4]
                ps_rr += 1
                bb = (6, 7, 0, 1)[ps_rr % 4]
                ps_rr += 1
                for (b, c0) in ((ba, 128), (bb, 192)):
                    for kc in range(4):
                        P.add("pe", lambda e, kc=kc, half=half, b=b, hh=hh, W=W, c0=c0: e.matmul(
                            k.ps[b][0:64, :], W[:, hh, kc, c0:c0 + 64], CQN[:, kc, half * 512:(half + 1) * 512],
                            start=(kc == 0), stop=(kc == 3)),
                            rd=[Wr, CQNr[kc][half]], wr=[k.psr[b]])
                t0 = tmp_rr % 2
                tmp_rr += 1
                t1 = tmp_rr % 2
                tmp_rr += 1
                P.add("dve", lambda e, ba=ba, half=half, t0=t0: e.tensor_tensor(
                    TMP[t0][0:64, :], k.ps[ba][0:64, :], COS[0:64, half * 512:(half + 1) * 512], ALU.mult),
                    rd=[k.psr[ba], k.constr], wr=[TMPr[t0]])
                P.add("dve", lambda e, bb=bb, half=half, t1=t1: e.tensor_tensor(
                    TMP[t1][0:64, :], k.ps[bb][0:64, :], SIN[0:64, half * 512:(half + 1) * 512], ALU.mult),
                    rd=[k.psr[bb], k.constr], wr=[TMPr[t1]])
                P.add("dve", lambda e, half=half, t0=t0, t1=t1, pb=pb: e.tensor_tensor(
                    QR[pb][:, half * 512:(half + 1) * 512], TMP[t0][0:64, :], TMP[t1][0:64, :], ALU.add),
                    rd=[TMPr[t0], TMPr[t1]], wr=[QRr[pb][half]])
            for kb in range(8):
                b = (6, 7, 0, 1)[ps_rr % 4]
                ps_rr += 1
                for kc in range(4):
                    P.add("pe", lambda e, kc=kc, kb=kb, b=b, hh=hh, W=W: e.matmul(
                        k.ps[b][:, :], W[:, hh, kc, 256:384], CKA[:, kc, kb * 512:(kb + 1) * 512],
                        start=(kc == 0), stop=(kc == 3)),
                        rd=[Wr, CKAr[kb // 2]], wr=[k.psr[b]])
                P.add("act", lambda e, b=b, kb=kb, pb=pb: e.activation(
                    KNt[:, 0, kb * 512:(kb + 1) * 512], k.ps[b][:, :], AF.Copy),
                    rd=[k.psr[b]], wr=[KNr[pb]])
            for jb in range(8):
                b = (6, 7, 0, 1)[ps_rr % 4]
                ps_rr += 1
                for j4 in range(4):
                    j = jb * 4 + j4
                    for kc in range(4):
                        P.add("pe", lambda e, kc=kc, j=j, j4=j4, b=b, hh=hh, W=W: e.matmul(
                            k.ps[b][:, j4 * 128:(j4 + 1) * 128], CKA[:, kc, j * 128:(j + 1) * 128],
                            W[:, hh, kc, 384:512], start=(kc == 0), stop=(kc == 3)),
                            rd=[Wr, CKAr[j // 8]], wr=[k.psr[b]])
                P.add("dve", lambda e, b=b, jb=jb, pb=pb: e.tensor_copy(
                    VV[pb][:, jb * 4:(jb + 1) * 4, :], k.ps[b][:, :].rearrange("p (j d) -> p j d", j=4)),
                    rd=[k.psr[b]], wr=[VVr[pb]])
            if hh == 0 and hgp + 1 < 4:
                load_att(hgp + 1, neg=False)
            if hh == 3:
                P.add("pool", lambda e, s=s, hgp=hgp: e.dma_start(
                    out=v4(k.slots[s]), in_=woc[:, hgp * 4:(hgp + 1) * 4, :]),
                    wr=[k.slotr[s]], dma=k.slotr[s])
            for G in range(2):
                bo = 2 + (h * 2 + G) % 2
                bl = 4 + (h * 2 + G) % 2
                nblk = 16 * G + 16
                def blk(j, G=G):
                    ip, rp = j // 4, j % 4
                    kcol = rp * 1024 + ip * 128
                    imin = max(ip, 4 * G)
                    c0 = (imin - 4 * G) * 128
                    return ip, rp, kcol, c0, 512 - c0, G * 512 + c0

                def emit_S(j, pb=pb, G=G):
                    ip, rp, kcol, c0, n, q0 = blk(j)
                    bs = (0, 1, 6)[j % 3]
                    P.add("pe", lambda e, bs=bs, pb=pb, kcol=kcol, q0=q0, n=n: e.matmul(
                        k.ps[bs][:, 0:n], KNt[:, 0, kcol:kcol + 128], QN[pb][:, q0:q0 + n],
                        start=True, stop=False),
                        rd=[KNr[pb], QNr[pb][G]], wr=[k.psr[bs]])
                    P.add("pe", lambda e, bs=bs, pb=pb, kcol=kcol, q0=q0, n=n: e.matmul(
                        k.ps[bs][:, 0:n], KRA[:, kcol:kcol + 128], QR[pb][:, q0:q0 + n],
                        start=False, stop=True),
                        rd=[KRAr, QRr[pb][G]], wr=[k.psr[bs]])

                emit_S(0)
                emit_S(1)
                for j in range(nblk):
                    if j + 2 < nblk:
                        emit_S(j + 2)
                    ip, rp, kcol, c0, n, q0 = blk(j)
                    bs = (0, 1, 6)[j % 3]
                    pi = j % 3
                    P.add("act", lambda e, bs=bs, pi=pi, n=n: e.activation(
                        PT[pi][:, 0:n], k.ps[bs][:, 0:n], AF.Exp, scale=SM_SCALE),
                        rd=[k.psr[bs]], wr=[PTr[pi]])
                    if ip >= 4 * G:
                        P.add("dve", lambda e, pi=pi, rp=rp: e.tensor_tensor(
                            PT[pi][:, 0:128], PT[pi][:, 0:128], MASKB[:, rp * 128:(rp + 1) * 128], ALU.mult),
                            rd=[PTr[pi], MASKBr], wr=[PTr[pi]])
                    P.add("pe", lambda e, bo=bo, pb=pb, j=j, pi=pi, c0=c0, n=n, nblk=nblk: e.matmul(
                        k.ps[bo][:, c0:c0 + n], VV[pb][:, (j % 4) * 8 + j // 4, :], PT[pi][:, 0:n],
                        start=(j == 0), stop=(j == nblk - 1)),
                        rd=[VVr[pb], PTr[pi]], wr=[k.psr[bo]])
                    ai = ip % 2
                    if rp == 0:
                        P.add("dve", lambda e, ai=ai, pi=pi, n=n: e.tensor_copy(ACC[ai][:, 0:n], PT[pi][:, 0:n]),
                              rd=[PTr[pi]], wr=[ACCr[ai]])
                    else:
                        P.add("dve", lambda e, ai=ai, pi=pi, n=n: e.tensor_tensor(
                            ACC[ai][:, 0:n], ACC[ai][:, 0:n], PT[pi][:, 0:n], ALU.add),
                            rd=[PTr[pi], ACCr[ai]], wr=[ACCr[ai]])
                    if rp == 3:
                        P.add("pe", lambda e, bl=bl, ai=ai, c0=c0, n=n, j=j, nblk=nblk: e.matmul(
                            k.ps[bl][:, c0:c0 + n], k.ones[:, 2, :], ACC[ai][:, 0:n],
                            start=(j == 3), stop=(j == nblk - 1)),
                            rd=[k.onesr, ACCr[ai]], wr=[k.psr[bl]])
                ri = 0
                P.add("dve", lambda e, bl=bl, ri=ri: e.reciprocal(RCP[ri][:, :], k.ps[bl][:, :]),
                      rd=[k.psr[bl]], wr=[RCPr[ri], ACCr[0], ACCr[1]])
                ob = hh
                P.add("dve", lambda e, bo=bo, ri=ri, ob=ob, G=G: e.tensor_tensor(
                    OT[:, ob, G * 512:(G + 1) * 512], k.ps[bo][:, :], RCP[ri][:, :], ALU.mult),
                    rd=[k.psr[bo], RCPr[ri], ACCr[0], ACCr[1]], wr=[OTr[ob][G]])
        ws = s
        for oc in range(KC):
            for half in range(2):
                b = (6, 7, 0, 1)[ps_rr % 4]
                ps_rr += 1
                for kc in range(4):
                    ob = kc
                    P.add("pe", lambda e, kc=kc, half=half, oc=oc, b=b, ws=ws, ob=ob: e.matmul(
                        k.ps[b][:, :], v4(k.slots[ws])[:, kc, oc * 128:(oc + 1) * 128],
                        OT[:, ob, half * 512:(half + 1) * 512], start=(kc == 0), stop=(kc == 3)),
                        rd=[k.slotr[ws], OTr[ob][half]], wr=[k.psr[b]])
                P.add("dve", lambda e, b=b, oc=oc, half=half: e.tensor_tensor(
                    X[:, oc, half * 512:(half + 1) * 512], X[:, oc, half * 512:(half + 1) * 512],
                    k.ps[b][:, :], ALU.add),
                    rd=[k.psr[b], Xr[oc][half]], wr=[Xr[oc][half]])
    k.barrier()
    k.slot_rr = 0
    k.ACTB = [OT, VVt.rearrange("p a b -> p (a b)").rearrange("p (a b) -> p a b", a=4)]
    k.ACTBr = P.Rs("ACTB", 2, 4, 2)
    gf1 = C[:, CB["GF1"]:CB["GF1"] + 16]
    k.norm_fm(X, Xr, KC, [(0, 512, 0), (512, 512, 1)], gf1, H, Hr, 0)
    k.ffn(w_gate, w_up, w_down, 1)
    k.barrier()
    gfin = C[:, CB["GFIN"]:CB["GFIN"] + 16]
    YF = H[:, :, :].rearrange("p a b -> p (a b)").bitcast(F32).rearrange("p (a b) -> p a b", a=8)
    YFr = P.Rs("YF", 8, 2)
    yr = yT.rearrange("(kc p) t -> p kc t", p=128)
    outr = P.R("out")
    ones = k.ones
    rst = []
    for (c0, n, pi) in [(0, 512, 0), (512, 512, 1)]:
        b = k.bank()
        for kc in range(KC):
            qi = k.sq_rr % 2
            k.sq_rr += 1
            P.add("act", lambda e, qi=qi, kc=kc, c0=c0, n=n: e.activation(
                k.sq[qi][:, 0:n], X[:, kc, c0:c0 + n], AF.Square),
                rd=[Xr[kc][pi]], wr=[k.sqr[qi]])
            P.add("pe", lambda e, qi=qi, kc=kc, n=n, b=b: e.matmul(
                k.ps[b][:, 0:n], ones[:, 0, :], k.sq[qi][:, 0:n], start=(kc == 0), stop=(kc == KC - 1)),
                rd=[k.sqr[qi], k.onesr], wr=[k.psr[b]])
        P.add("act", lambda e, b=b, pi=pi: e.activation(k.rstd[pi][:, :], k.ps[b][:, :], AF.Sqrt, bias=EPS, scale=1.0),
              rd=[k.psr[b]], wr=[k.rstdr[pi]])
        P.add("dve", lambda e, pi=pi: e.reciprocal(k.rstd[pi][:, :], k.rstd[pi][:, :]),
              rd=[k.rstdr[pi]], wr=[k.rstdr[pi]])
    for pi in range(2):
        for cg in range(2):
            for kc in range(8 * cg, 8 * cg + 8):
                P.add("dve", lambda e, kc=kc, pi=pi: e.scalar_tensor_tensor(
                    X[:, kc, pi * 512:(pi + 1) * 512], X[:, kc, pi * 512:(pi + 1) * 512], gfin[:, kc:kc + 1],
                    k.rstd[pi][:, :], ALU.mult, ALU.mult),
                    rd=[Xr[kc][pi], k.rstdr[pi], k.constr], wr=[Xr[kc][pi]])
            P.add("sp", lambda e, pi=pi, cg=cg: e.dma_start(
                out=yr[:, 8 * cg:8 * cg + 8, pi * 512:(pi + 1) * 512],
                in_=X[:, 8 * cg:8 * cg + 8, pi * 512:(pi + 1) * 512]),
                rd=[Xr[kc][pi] for kc in range(8 * cg, 8 * cg + 8)], wr=[outr], dma=outr)
    P.finish("sp", [outr])
    P.emit(nc, k.es)
    k.es.close()
    return nc


def _gcols(g):
    return np.ascontiguousarray(np.asarray(g, np.float32).reshape(-1, 128).T)


def _core_tiles(c):
    b, r = c // 4, c % 4
    return b, r, [4 * i + r for i in range(NT)]


def _rope_tables(c, rank=None):
    b, r, tiles = _core_tiles(c)
    if rank is not None:
        tiles = [4 * i + rank for i in range(NT)]
    pos = np.concatenate([np.arange(g * 128, (g + 1) * 128) for g in tiles]).astype(np.float32)
    inv_freq = (10000.0 ** (-np.arange(0, 64, 2, dtype=np.float32) / 64)).astype(np.float32)
    ang = pos[:, None] * inv_freq[None, :]
    cos = np.cos(ang).astype(np.float32).T
    sin = np.sin(ang).astype(np.float32).T
    return np.tile(cos, (4, 1)), np.tile(sin, (4, 1))


def _consts_A(c, inp, rank=None):
    b, r, tiles = _core_tiles(c)
    if rank is not None:
        r = rank
    C = np.zeros((128, CA["N"]), np.float32)
    C[:, CA["GM0"]:CA["GM0"] + 16] = _gcols(inp["g_mix"][0])
    C[:, CA["GF0"]:CA["GF0"] + 16] = _gcols(inp["g_ffn"][0])
    C[:, CA["GM1"]:CA["GM1"] + 16] = _gcols(inp["g_mix"][1])
    C[:, CA["PSC"]:CA["PSC"] + 8] = _gcols(inp["pool_scale"][0])
    C[:, CA["GCKV"]:CA["GCKV"] + 4] = _gcols(inp["g_ckv"][0])
    for g, win in enumerate((2, 4, 8, 16)):
        if r == 0:
            cnt = np.minimum(np.arange(1, 17), win).astype(np.float32)
        else:
            cnt = np.full(16, win, np.float32)
        C[:, CA["INVC"] + g * 16:CA["INVC"] + (g + 1) * 16] = (1.0 / cnt)[None, :]
    return C


def _consts_B(c, inp, order=(0, 1, 2, 3)):
    b, r, tiles = _core_tiles(c)
    C = np.zeros((128, CB["N"]), np.float32)
    C[:, CB["GM1"]:CB["GM1"] + 16] = _gcols(inp["g_mix"][1])
    C[:, CB["GF1"]:CB["GF1"] + 16] = _gcols(inp["g_ffn"][1])
    C[:, CB["GFIN"]:CB["GFIN"] + 16] = _gcols(inp["g_final"])
    C[:, CB["GCQ"]:CB["GCQ"] + 4] = _gcols(inp["g_cq"][0])
    cos, sin = _rope_tables(c)
    C[:, CB["COS"]:CB["COS"] + T] = cos
    C[:, CB["SIN"]:CB["SIN"] + T] = sin
    diag = np.ones((128, 128), np.float32)
    diag[64:, :64] = 0.0
    for p_, rp in enumerate(order):
        m = np.ones((128, 128), np.float32) if rp < r else (diag if rp == r else np.zeros((128, 128), np.float32))
        C[:, CB["MASK"] + p_ * 128:CB["MASK"] + (p_ + 1) * 128] = m
    return C


_CACHE = {}


def _get(name, fn):
    if name not in _CACHE:
        _CACHE[name] = fn()
    return _CACHE[name]


def _prep(inp):
    inp = {k_: np.asarray(v) for k_, v in inp.items()}
    return inp


def maps_A(inp, cores):
    x = inp["x"].astype(np.float32, copy=False)
    w_in_c = inp["w_in_c"][0]
    kr = w_in_c[:, 1024:1088]
    w_kr = np.ascontiguousarray(np.concatenate([kr, kr[:, 32:64], kr[:, 0:32]], axis=1))
    shared_A = dict(
        w_in_ab=np.ascontiguousarray(inp["w_in_ab"][0]),
        w_sT=np.ascontiguousarray(np.transpose(inp["w_s"][0], (0, 2, 1))),
        w_pool=np.ascontiguousarray(inp["w_pool"][0]),
        w_out_ab=np.ascontiguousarray(inp["w_out_ab"][0]),
        w_gate=np.ascontiguousarray(inp["w_gate"][0:1]),
        w_up=np.ascontiguousarray(inp["w_up"][0:1]),
        w_down=np.ascontiguousarray(inp["w_down"][0:1]),
        w_ckv=np.ascontiguousarray(w_in_c[:, 512:1024]),
        w_kr=w_kr,
    )
    gvbs = np.ascontiguousarray(np.broadcast_to(np.concatenate(
        [np.asarray(inp["g_v"][0], np.float32), np.asarray(inp["b_s"][0], np.float32).reshape(1024)])[None, :],
        (128, 2048)))
    maps = []
    for c in cores:
        b, r, tiles = _core_tiles(c)
        xs = np.concatenate([x[b, g * 128:(g + 1) * 128, :] for g in tiles], axis=0)
        halo = np.zeros((128, D), np.float32)
        for i, g in enumerate(tiles):
            if g > 0:
                halo[i * 16:(i + 1) * 16] = x[b, g * 128 - 16:g * 128, :]
        m = dict(xT=np.ascontiguousarray(xs.T), xhT=np.ascontiguousarray(halo.T), consts=_consts_A(c, inp),
                 gvbs=gvbs, cossin=np.ascontiguousarray(np.concatenate(_rope_tables(c), axis=1)))
        m.update(shared_A)
        maps.append(m)
    return maps


def maps_B(inp, cores, lat):
    w_in_c = inp["w_in_c"][0]
    w_uq = inp["w_uq"][0].reshape(512, 16, 192)
    w_ukv = inp["w_ukv"][0].reshape(512, 16, 256)
    w_att = np.ascontiguousarray(np.concatenate(
        [w_uq[:, :, 0:128], w_uq[:, :, 128:192], w_uq[:, :, 160:192], w_uq[:, :, 128:160], w_ukv], axis=2))
    shared_B = dict(
        w_cq=np.ascontiguousarray(w_in_c[:, 0:512]),
        w_att=w_att,
        w_out_c=np.ascontiguousarray(inp["w_out_c"][0]),
        w_gate=np.ascontiguousarray(inp["w_gate"][1:2]),
        w_up=np.ascontiguousarray(inp["w_up"][1:2]),
        w_down=np.ascontiguousarray(inp["w_down"][1:2]),
    )
    maps = []
    for c in cores:
        b = c // 4
        ckv = np.concatenate([lat[b * 4 + rp]["latc"] for rp in range(4)], axis=1)
        krr = np.concatenate([lat[b * 4 + rp]["latr"] for rp in range(4)], axis=1)
        m = dict(x1T=np.ascontiguousarray(lat[c]["x1T"]), consts=_consts_B(c, inp),
                 ckv_all=np.ascontiguousarray(ckv), kr_all=np.ascontiguousarray(krr))
        m.update(shared_B)
        maps.append(m)
    return maps


def maps_F(inp, cores):
    x = inp["x"].astype(np.float32, copy=False)
    w_in_c = inp["w_in_c"][0]
    kr = w_in_c[:, 1024:1088]
    w_kr = np.ascontiguousarray(np.concatenate([kr, kr[:, 32:64], kr[:, 0:32]], axis=1))
    w_uq = inp["w_uq"][0].reshape(512, 16, 192)
    w_ukv = inp["w_ukv"][0].reshape(512, 16, 256)
    w_att = np.ascontiguousarray(np.concatenate(
        [w_uq[:, :, 0:128], w_uq[:, :, 128:192], w_uq[:, :, 160:192], w_uq[:, :, 128:160], w_ukv], axis=2))
    shared = dict(
        w_in_ab=np.ascontiguousarray(inp["w_in_ab"][0]),
        w_sT=np.ascontiguousarray(np.transpose(inp["w_s"][0], (0, 2, 1))),
        w_pool=np.ascontiguousarray(inp["w_pool"][0]),
        w_out_ab=np.ascontiguousarray(inp["w_out_ab"][0]),
        w_gate=np.ascontiguousarray(inp["w_gate"]),
        w_up=np.ascontiguousarray(inp["w_up"]),
        w_down=np.ascontiguousarray(inp["w_down"]),
        w_ckv=np.ascontiguousarray(w_in_c[:, 512:1024]),
        w_kr=w_kr,
        w_cq=np.ascontiguousarray(w_in_c[:, 0:512]),
        w_att=w_att,
        w_out_c=np.ascontiguousarray(inp["w_out_c"][0]),
        gvbs=np.ascontiguousarray(np.broadcast_to(np.concatenate(
            [np.asarray(inp["g_v"][0], np.float32), np.asarray(inp["b_s"][0], np.float32).reshape(1024)])[None, :],
            (128, 2048))),
    )
    maps = []
    for c in cores:
        b, r, _ = _core_tiles(c)
        order = [rp for rp in range(4) if rp != r] + [r]
        xs_l, xh_l, ca_l, cs_l = [], [], [], []
        for rp in order:
            tiles = [4 * i + rp for i in range(NT)]
            xs = np.concatenate([x[b, g * 128:(g + 1) * 128, :] for g in tiles], axis=0)
            halo = np.zeros((128, D), np.float32)
            for i, g in enumerate(tiles):
                if g > 0:
                    halo[i * 16:(i + 1) * 16] = x[b, g * 128 - 16:g * 128, :]
            xs_l.append(xs.T)
            xh_l.append(halo.T)
            ca_l.append(_consts_A(c, inp, rank=rp))
            cs_l.append(np.concatenate(_rope_tables(c, rank=rp), axis=1))
        m = dict(xT=np.ascontiguousarray(np.stack(xs_l)), xhT=np.ascontiguousarray(np.stack(xh_l)),
                 constsA=np.ascontiguousarray(np.stack(ca_l)), cossin=np.ascontiguousarray(np.stack(cs_l)),
                 constsB=_consts_B(c, inp, order=order))
        m.update(shared)
        maps.append(m)
    return maps


def maps_G(inp, cores):
    base = maps_A(inp, cores)
    w_in_c = inp["w_in_c"][0]
    w_uq = inp["w_uq"][0].reshape(512, 16, 192)
    w_ukv = inp["w_ukv"][0].reshape(512, 16, 256)
    w_att = np.ascontiguousarray(np.concatenate(
        [w_uq[:, :, 0:128], w_uq[:, :, 128:192], w_uq[:, :, 160:192], w_uq[:, :, 128:160], w_ukv], axis=2))
    shared = dict(
        w_gate=np.ascontiguousarray(inp["w_gate"]),
        w_up=np.ascontiguousarray(inp["w_up"]),
        w_down=np.ascontiguousarray(inp["w_down"]),
        w_cq=np.ascontiguousarray(w_in_c[:, 0:512]),
        w_att=w_att,
        w_out_c=np.ascontiguousarray(inp["w_out_c"][0]),
    )
    maps = []
    for c, m in zip(cores, base):
        m = dict(m)
        m["xT"] = m["xT"][None]
        m["xhT"] = m["xhT"][None]
        m["constsA"] = m.pop("consts")[None]
        m["cossin"] = m["cossin"][None]
        m["constsB"] = _consts_B(c, inp)
        m.update(shared)
        maps.append(m)
    return maps


FUSED = True


def kernel(**inp):
    inp = _prep(inp)
    cores = list(range(NCORES))
    if FUSED:
        ncG = _get("G", build_G)
        res = run_bass_kernel_spmd(ncG, maps_G(inp, cores), core_ids=cores)
        results = res.results
    else:
        ncA = _get("A", build_A)
        resA = run_bass_kernel_spmd(ncA, maps_A(inp, cores), core_ids=cores)
        lat = {c: resA.results[c] for c in cores}
        ncB = _get("B", build_B)
        resB = run_bass_kernel_spmd(ncB, maps_B(inp, cores, lat), core_ids=cores)
        results = resB.results
    out = np.empty((2, 4096, D), np.float32)
    for c in cores:
        b, r, tiles = _core_tiles(c)
        y = results[c]["yT"].T
        for i, g in enumerate(tiles):
            out[b, g * 128:(g + 1) * 128, :] = y[i * 128:(i + 1) * 128]
    return out
```

```python
import numpy as np
from contextlib import ExitStack
import concourse.bass as bass
import concourse.mybir as mybir
from concourse.bass_utils import run_bass_kernel_spmd

F32 = mybir.dt.float32
BF16 = mybir.dt.bfloat16
AF = mybir.ActivationFunctionType
ALU = mybir.AluOpType

D = 2048
KC = 16
T = 1024
NT = 8
DFF = 5632
EPS = 1e-6
NCORES = 8
SM_SCALE = 192.0 ** -0.5


class Reg:
    __slots__ = ("name", "w", "rd", "cnt")

    def __init__(self, name):
        self.name = name
        self.w = None
        self.rd = {}
        self.cnt = 0


class Op:
    __slots__ = ("eng", "fn", "waits", "signal", "idx", "dma", "ticket", "inc")


class Prog:
    ENG = ("pe", "act", "dve", "pool", "sp")

    def __init__(self):
        self.streams = {e: [] for e in self.ENG}
        self.known = {e: {} for e in self.ENG}
        self.bar = None
        self.alltok = {}
        self.dma_regs = []
        self.nreg = 0

    def R(self, name="r"):
        self.nreg += 1
        return Reg(f"{name}{self.nreg}")

    def Rs(self, name, *dims):
        if len(dims) == 1:
            return [self.R(name) for _ in range(dims[0])]
        return [self.Rs(name, *dims[1:]) for _ in range(dims[0])]

    def add(self, eng, fn, rd=(), wr=(), dma=None, inc=16):
        st = self.streams[eng]
        op = Op()
        op.inc = inc
        op.idx = len(st)
        op.eng = eng
        op.fn = fn
        op.dma = dma
        op.signal = False
        op.ticket = 0
        deps = {}

        def need(tok):
            key = tok[0]
            if key in deps and deps[key][1] >= tok[1]:
                return
            deps[key] = tok

        if self.bar is not None:
            need(self.bar)
        for r in rd:
            if r.w is not None:
                need(r.w)
        for r in wr:
            if r.w is not None:
                need(r.w)
            for tok in r.rd.values():
                need(tok)
        known = self.known[eng]
        waits = []
        for key, tok in deps.items():
            if key == ("c", "pe") and eng == "pe" and dma is None:
                continue
            if known.get(key, -1) >= tok[1]:
                continue
            known[key] = tok[1]
            waits.append(tok)
            if tok[2] is not None:
                tok[2].signal = True
        op.waits = waits
        if dma is not None:
            if dma.cnt == 0:
                self.dma_regs.append(dma)
            dma.cnt += inc
            mytok = (("d", dma), dma.cnt, None)
        else:
            mytok = (("c", eng), op.idx, op)
        for r in rd:
            r.rd[mytok[0]] = mytok
        for r in wr:
            r.w = mytok
            r.rd = {}
        self.alltok[mytok[0]] = mytok
        st.append(op)
        return op

    def barrier(self, fn):
        st = self.streams["dve"]
        op = Op()
        op.idx = len(st)
        op.eng = "dve"
        op.fn = fn
        op.dma = None
        op.signal = True
        op.ticket = 0
        op.inc = 16
        known = self.known["dve"]
        waits = []
        for key, tok in self.alltok.items():
            if known.get(key, -1) >= tok[1]:
                continue
            known[key] = tok[1]
            waits.append(tok)
            if tok[2] is not None:
                tok[2].signal = True
        op.waits = waits
        st.append(op)
        mytok = (("c", "dve"), op.idx, op)
        self.bar = mytok
        self.alltok[mytok[0]] = mytok
        for e in self.ENG:
            if e != "dve":
                for key, tok in self.alltok.items():
                    if key != ("c", "dve"):
                        self.known[e][key] = max(self.known[e].get(key, -1), tok[1])

    def finish(self, eng, regs):
        op = Op()
        op.idx = len(self.streams[eng])
        op.eng = eng
        op.fn = None
        op.dma = None
        op.signal = False
        op.ticket = 0
        op.inc = 16
        op.waits = [(("d", r), r.cnt, None) for r in regs]
        self.streams[eng].append(op)

    def emit(self, nc, es):
        sems = {}
        for e in ("pe", "act", "dve", "pool"):
            sems[("c", e)] = es.enter_context(nc.semaphore("s_" + e))
        for r in self.dma_regs:
            sems[("d", r)] = es.enter_context(nc.semaphore("d_" + r.name))
        for e in ("pe", "act", "dve", "pool"):
            n = 0
            for op in self.streams[e]:
                if op.dma is None and op.signal:
                    n += 1
                    op.ticket = n
        streams = self.streams

        def run(ename, e):
            for op in streams[ename]:
                for key, val, p in op.waits:
                    v = p.ticket if p is not None else val
                    e.wait_ge(sems[key], v)
                if op.fn is None:
                    continue
                ins = op.fn(e)
                if op.dma is not None:
                    if op.inc == 1:
                        ins.then_inc(sems[("d", op.dma)])
                    else:
                        ins.then_inc(sems[("d", op.dma)], op.inc)
                elif op.signal:
                    ins.then_inc(sems[("c", ename)], 1)

        with nc.Block() as block:
            block.tensor(lambda e: run("pe", e))
            block.scalar(lambda e: run("act", e))
            block.vector(lambda e: run("dve", e))
            block.gpsimd(lambda e: run("pool", e))
            block.sync(lambda e: run("sp", e))


class KB:
    def __init__(self):
        self.nc = bass.Bass("TRN2", target_bir_lowering=False)
        self.P = Prog()
        self.es = ExitStack()
        self.bankrr = 0

    def din(self, name, shape, dt=F32):
        return self.nc.dram_tensor(name, list(shape), dt, kind="ExternalInput").ap()

    def dout(self, name, shape, dt=F32):
        return self.nc.dram_tensor(name, list(shape), dt, kind="ExternalOutput").ap()

    def sb(self, name, shape, dt):
        return self.es.enter_context(self.nc.sbuf_tensor(name, list(shape), dt))

    def pst(self, name):
        return self.es.enter_context(self.nc.psum_tensor(name, [128, 512], F32))

    def common(self, nrstd=2):
        P = self.P
        self.ps = [self.pst(f"ps{i}") for i in range(8)]
        self.psr = [P.R("psr") for _ in range(8)]
        self.slots = [self.sb(f"slot{i}", [128, 8192], BF16) for i in range(3)]
        self.slotr = [P.R("slot") for _ in range(3)]
        self.slot_rr = 0
        self.slot_n = 3
        self.ones = self.sb("ones", [128, 3, 128], BF16)
        self.onesr = P.R("ones")
        self.bscr = self.sb("bscr", [128, 8], F32)
        ones = self.ones
        P.add("dve", lambda e: e.memset(ones[:, 0, :], 1.0 / 2048), wr=[self.onesr])
        P.add("dve", lambda e: e.memset(ones[:, 1, :], 1.0 / 512), wr=[self.onesr])
        P.add("dve", lambda e: e.memset(ones[:, 2, :], 1.0), wr=[self.onesr])
        self.sq = [self.sb(f"sq{i}", [128, 512], BF16) for i in range(2)]
        self.sqr = [P.R("sq") for _ in range(2)]
        self.sq_rr = 0
        self.rstd = [self.sb(f"rstd{i}", [128, 512], F32) for i in range(nrstd)]
        self.rstdr = [P.R("rstd") for _ in range(nrstd)]
        self.nrstd = nrstd
        self.rstd_rr = 0

    def bank(self, lo=0, hi=8):
        b = lo + (self.bankrr % (hi - lo))
        self.bankrr += 1
        return b

    def barrier(self):
        bscr = self.bscr
        self.P.barrier(lambda e: e.memset(bscr[:, 0:1], 0.0))

    def next_slot(self):
        s = self.slot_rr % self.slot_n
        self.slot_rr += 1
        return s

    def load_w(self, src_ap, view):
        s = self.next_slot()
        slot = self.slots[s]
        self.P.add("pool", lambda e: e.dma_start(out=view(slot), in_=src_ap),
                   wr=[self.slotr[s]], dma=self.slotr[s])
        return s

    def norm_fm(self, src, src_regs, nchunk, pieces, gcols, dst, dst_regs, ones_idx, dst_is_f32=False,
                act_square=True):
        P = self.P
        ones = self.ones
        for (c0, n, pi) in pieces:
            b = self.bank()
            ps = self.ps[b]
            for kc in range(nchunk):
                qi = self.sq_rr % 2
                self.sq_rr += 1
                sq = self.sq[qi]
                P.add("act", lambda e, sq=sq, kc=kc, c0=c0, n=n: e.activation(
                    sq[:, 0:n], src[:, kc, c0:c0 + n], AF.Square),
                    rd=[src_regs[kc][pi]], wr=[self.sqr[qi]])
                P.add("pe", lambda e, sq=sq, kc=kc, n=n, ps=ps: e.matmul(
                    ps[:, 0:n], ones[:, ones_idx, :], sq[:, 0:n], start=(kc == 0), stop=(kc == nchunk - 1)),
                    rd=[self.sqr[qi], self.onesr], wr=[self.psr[b]])
            ri = self.rstd_rr % self.nrstd
            self.rstd_rr += 1
            rstd = self.rstd[ri]
            P.add("act", lambda e, rstd=rstd, ps=ps, n=n: e.activation(
                rstd[:, 0:n], ps[:, 0:n], AF.Sqrt, bias=EPS, scale=1.0),
                rd=[self.psr[b]], wr=[self.rstdr[ri]])
            P.add("dve", lambda e, rstd=rstd, n=n: e.reciprocal(rstd[:, 0:n], rstd[:, 0:n]),
                  rd=[self.rstdr[ri]], wr=[self.rstdr[ri]])
            for kc in range(nchunk):
                P.add("dve", lambda e, rstd=rstd, kc=kc, c0=c0, n=n: e.scalar_tensor_tensor(
                    dst[:, kc, c0:c0 + n], src[:, kc, c0:c0 + n], gcols[:, kc:kc + 1], rstd[:, 0:n],
                    ALU.mult, ALU.mult),
                    rd=[src_regs[kc][pi], self.rstdr[ri], self.constr], wr=[dst_regs[kc][pi]])

    def ffn(self, w_gate, w_up, w_down, layer):
        P = self.P
        X, Xr, H, Hr = self.X, self.Xr, self.H, self.Hr
        ACTB, ACTBr = self.ACTB, self.ACTBr
        SG, SGr = self.SG, self.SGr
        NG = DFF // 512
        wg = w_gate[layer].rearrange("(kc p) n -> p kc n", p=128)
        wu = w_up[layer].rearrange("(kc p) n -> p kc n", p=128)
        wd = w_down[layer].rearrange("(kc p) n -> p kc n", p=128)
        v16 = lambda s: s[:, :].rearrange("p (a b) -> p a b", a=16)
        v4 = lambda s: s[:, :].rearrange("p (a b) -> p a b", a=4)
        sg_rr = 0

        def gate_up(fg):
            nonlocal sg_rr
            sgl = self.load_w(wg[:, :, fg * 512:(fg + 1) * 512], v16)
            sul = self.load_w(wu[:, :, fg * 512:(fg + 1) * 512], v16)
            ab = fg % 2
            for fc in range(4):
                bg = [self.bank(0, 4), self.bank(0, 4)]
                for half in range(2):
                    for kc in range(KC):
                        P.add("pe", lambda e, kc=kc, half=half, fc=fc, b=bg[half], sl=self.slots[sgl]: e.matmul(
                            self.ps[b][:, :], v16(sl)[:, kc, fc * 128:(fc + 1) * 128],
                            H[:, kc, half * 512:(half + 1) * 512], start=(kc == 0), stop=(kc == KC - 1)),
                            rd=[self.slotr[sgl], Hr[kc][half]], wr=[self.psr[bg[half]]])
                sgi = []
                for half in range(2):
                    si = sg_rr % 2
                    sg_rr += 1
                    sgi.append(si)
                    P.add("act", lambda e, b=bg[half], si=si: e.activation(SG[si][:, :], self.ps[b][:, :], AF.Silu),
                          rd=[self.psr[bg[half]]], wr=[SGr[si]])
                bu = [self.bank(0, 4), self.bank(0, 4)]
                for half in range(2):
                    for kc in range(KC):
                        P.add("pe", lambda e, kc=kc, half=half, fc=fc, b=bu[half], sl=self.slots[sul]: e.matmul(
                            self.ps[b][:, :], v16(sl)[:, kc, fc * 128:(fc + 1) * 128],
                            H[:, kc, half * 512:(half + 1) * 512], start=(kc == 0), stop=(kc == KC - 1)),
                            rd=[self.slotr[sul], Hr[kc][half]], wr=[self.psr[bu[half]]])
                for half in range(2):
                    P.add("dve", lambda e, b=bu[half], si=sgi[half], fc=fc, half=half, ab=ab: e.tensor_tensor(
                        ACTB[ab][:, fc, half * 512:(half + 1) * 512], SG[si][:, :], self.ps[b][:, :], ALU.mult),
                        rd=[self.psr[bu[half]], SGr[sgi[half]]], wr=[ACTBr[ab][fc][half]])

        def down(fg):
            sdl = self.load_w(wd[:, fg * 4:(fg + 1) * 4, :], v4)
            ab = fg % 2
            for oc in range(KC):
                for half in range(2):
                    b = self.bank(4, 8)
                    for kc in range(4):
                        P.add("pe", lambda e, kc=kc, half=half, oc=oc, b=b, sl=self.slots[sdl], ab=ab: e.matmul(
                            self.ps[b][:, :], v4(sl)[:, kc, oc * 128:(oc + 1) * 128],
                            ACTB[ab][:, kc, half * 512:(half + 1) * 512], start=(kc == 0), stop=(kc == 3)),
                            rd=[self.slotr[sdl], ACTBr[ab][kc][half]], wr=[self.psr[b]])
                    P.add("dve", lambda e, b=b, oc=oc, half=half: e.tensor_tensor(
                        X[:, oc, half * 512:(half + 1) * 512], X[:, oc, half * 512:(half + 1) * 512],
                        self.ps[b][:, :], ALU.add),
                        rd=[self.psr[b], Xr[oc][half]], wr=[Xr[oc][half]])

        gate_up(0)
        for fg in range(NG):
            if fg + 1 < NG:
                gate_up(fg + 1)
            down(fg)


CA = dict(GM0=0, GF0=16, GM1=32, PSC=48, GCKV=56, INVC=64, N=128)


def build_A(debug=False):
    k = KB()
    nc, P = k.nc, k.P
    xT = k.din("xT", [D, T])
    xhT = k.din("xhT", [D, 128])
    consts = k.din("consts", [128, CA["N"]])
    gvbs = k.din("gvbs", [128, 2048])
    cossin = k.din("cossin", [128, 2048])
    w_in_ab = k.din("w_in_ab", [D, 3072])
    w_sT = k.din("w_sT", [8, 128, 128])
    w_pool = k.din("w_pool", [4, 256, 256])
    w_out_ab = k.din("w_out_ab", [D, D])
    w_gate = k.din("w_gate", [1, D, DFF])
    w_up = k.din("w_up", [1, D, DFF])
    w_down = k.din("w_down", [1, DFF, D])
    w_ckv = k.din("w_ckv", [D, 512])
    w_kr = k.din("w_kr", [D, 128])
    x1T = k.dout("x1T", [D, T])
    latc = k.dout("latc", [512, T])
    latr = k.dout("latr", [64, T])

    k.common(1)
    X = k.X = k.sb("X", [128, KC, T], F32)
    Xr = k.Xr = P.Rs("X", KC, 2)
    H = k.H = k.sb("H", [128, KC, T], BF16)
    Hr = k.Hr = P.Rs("H", KC, 2)
    HH = k.sb("HH", [128, KC, 128], BF16)
    HHr = P.Rs("HH", KC, 1)
    C = k.sb("C", [128, CA["N"]], F32)
    k.constr = P.R("const")
    U = k.sb("U", [128, 8, T], BF16)
    Ur = P.Rs("U", 8, NT)
    BIG = k.sb("BIG", [128, 8192], BF16)
    bigr = P.R("BIG")
    BT = U
    BTr = P.Rs("BT", 8, 2)
    S2F = k.slots[2][:, :].bitcast(F32)
    GVBS = S2F[:, 0:2048]
    gvbsr = P.R("GVBS")
    Z = S2F[:, 2048:2048 + 576].rearrange("p (i t) -> p i t", i=4)
    Zr = P.R("Z")
    ZS = [S2F[:, 2624 + 576 * i:2624 + 576 * (i + 1)].rearrange("p (i t) -> p i t", i=4) for i in range(2)]
    ZSr = [P.R("ZS") for _ in range(2)]
    WSWP = k.sb("WSWP", [128, 2048], BF16)
    WST = WSWP[:, 0:1024].rearrange("p (h t) -> p h t", h=8)
    WP = WSWP[:, :].rearrange("p (g c d) -> p g c d", g=4, c=2)
    WSTr = P.R("WSWP")
    WPr = WSTr
    SSQ = k.sb("SSQ", [128, 16], F32)
    SSQr = P.Rs("SSQ", 16)
    RV = k.sb("RV", [128, 8], F32)
    RVr = P.R("RV")
    TG = [k.sb(f"TG{i}", [128, 512], F32) for i in range(2)]
    TGr = [P.R("TG") for _ in range(2)]
    k.SG = TG
    k.SGr = TGr
    TMP, TMPr = TG, TGr

    xr = xT.rearrange("(kc p) t -> p kc t", p=128)
    Xin = P.Rs("Xin", 4)
    for q in range(4):
        P.add("sp", lambda e, q=q: e.dma_start(out=X[:, 4 * q:4 * q + 4, :], in_=xr[:, 4 * q:4 * q + 4, :]),
              wr=[Xr[kc][h] for kc in range(4 * q, 4 * q + 4) for h in range(2)] + [Xin[q]], dma=Xin[q])
    P.add("sp", lambda e: e.dma_start(out=C[:, :], in_=consts), wr=[k.constr], dma=k.constr)
    XHt = BIG[:, 0:4096].bitcast(F32).rearrange("p (a b) -> p a b", a=KC)
    XHr = [[bigr] for _ in range(KC)]
    XHd = P.R("XHd")
    P.add("sp", lambda e: e.dma_start(out=XHt, in_=xhT.rearrange("(kc p) t -> p kc t", p=128)),
          wr=[bigr, XHd], dma=XHd)
    P.add("sp", lambda e: e.dma_start(out=GVBS, in_=gvbs), wr=[gvbsr], dma=gvbsr)
    P.add("pool", lambda e: e.dma_start(out=WST, in_=w_sT.rearrange("h s t -> s h t")),
          wr=[WSTr], dma=WSTr)
    P.add("dve", lambda e: e.memset(WST[64:128, :, 0:64], 0.0), wr=[WSTr])

    gm0 = C[:, CA["GM0"]:CA["GM0"] + 16]
    k.norm_fm(X, Xr, KC, [(0, 512, 0), (512, 512, 1)], gm0, H, Hr, 0)
    k.norm_fm(XHt, XHr, KC, [(0, 128, 0)], gm0, HH, HHr, 0)

    v16 = lambda s: s[:, :].rearrange("p (a b) -> p a b", a=16)
    v8 = lambda s: s[:, 0:4096].rearrange("p (a b) -> p a b", a=8)
    wab = w_in_ab.rearrange("(kc p) n -> p kc n", p=128)
    VG = BIG[:, :].rearrange("p (i c) -> p i c", i=NT)
    VGr = P.Rs("VG", NT)
    k.slot_n = 2

    for sblk in range(2):
        sl = k.load_w(wab[:, :, sblk * 512:(sblk + 1) * 512], v16)
        for oc in range(4):
            hd = sblk * 4 + oc
            for half in range(2):
                b = k.bank()
                for kc in range(KC):
                    P.add("pe", lambda e, kc=kc, half=half, oc=oc, b=b, sl=sl: e.matmul(
                        k.ps[b][:, :], v16(k.slots[sl])[:, kc, oc * 128:(oc + 1) * 128],
                        H[:, kc, half * 512:(half + 1) * 512], start=(kc == 0), stop=(kc == KC - 1)),
                        rd=[k.slotr[sl], Hr[kc][half]], wr=[k.psr[b]])
                P.add("act", lambda e, b=b, hd=hd, half=half: e.activation(
                    U[:, hd, half * 512:(half + 1) * 512], k.ps[b][:, :], AF.Gelu_apprx_tanh),
                    rd=[k.psr[b]], wr=[Ur[hd][i] for i in range(4 * half, 4 * half + 4)])
    tg_rr = 0
    for sblk in range(2):
        sl = k.load_w(wab[:, :, 1024 + sblk * 512:1024 + (sblk + 1) * 512], v16)
        for i in range(NT):
            b = k.bank()
            for kc in range(KC):
                P.add("pe", lambda e, kc=kc, i=i, b=b, sl=sl: e.matmul(
                    k.ps[b][:, :], H[:, kc, i * 128:(i + 1) * 128], v16(k.slots[sl])[:, kc, :],
                    start=(kc == 0), stop=(kc == KC - 1)),
                    rd=[k.slotr[sl], Hr[kc][i // 4]], wr=[k.psr[b]])
            ti = tg_rr % 2
            tg_rr += 1
            P.add("act", lambda e, b=b, ti=ti: e.activation(TG[ti][:, :], k.ps[b][:, :], AF.Gelu_apprx_tanh),
                  rd=[k.psr[b]], wr=[TGr[ti]])
            qi = k.sq_rr % 2
            k.sq_rr += 1
            P.add("act", lambda e, ti=ti, i=i, sblk=sblk, qi=qi: e.activation(
                k.sq[qi][:, :], TG[ti][:, :], AF.Square,
                accum_out=SSQ[:, 2 * i + sblk:2 * i + sblk + 1]),
                rd=[TGr[ti]], wr=[k.sqr[qi], SSQr[2 * i + sblk]])
            P.add("dve", lambda e, ti=ti, i=i, sblk=sblk: e.tensor_copy(
                VG[:, i, sblk * 512:(sblk + 1) * 512], TG[ti][:, :]),
                rd=[TGr[ti]], wr=[VGr[i], bigr])
    SS3 = SSQ[:, :].rearrange("p (i s) -> p i s", s=2)
    P.add("dve", lambda e: e.tensor_tensor(RV[:, :], SS3[:, :, 0], SS3[:, :, 1], ALU.add),
          rd=SSQr, wr=[RVr])
    P.add("act", lambda e: e.activation(RV[:, :], RV[:, :], AF.Sqrt, bias=EPS, scale=1.0 / 1024),
          rd=[RVr], wr=[RVr])
    P.add("dve", lambda e: e.reciprocal(RV[:, :], RV[:, :]), rd=[RVr], wr=[RVr])
    GV = GVBS[:, 0:1024]
    BS = GVBS[:, 1024:2048]
    for i in range(NT):
        P.add("dve", lambda e, i=i: e.scalar_tensor_tensor(
            VG[:, i, :], VG[:, i, :], RV[:, i:i + 1], GV, ALU.mult, ALU.mult),
            rd=[VGr[i], RVr, gvbsr], wr=[VGr[i]])
    tmp_rr = 0
    for i in range(NT):
        for hg in range(2):
            b = k.bank()
            for h4 in range(4):
                hd = hg * 4 + h4
                P.add("pe", lambda e, i=i, hd=hd, h4=h4, b=b: e.matmul(
                    k.ps[b][:, h4 * 128:(h4 + 1) * 128], VG[:, i, hd * 128:(hd + 1) * 128], WST[:, hd, :],
                    start=True, stop=True),
                    rd=[VGr[i], WSTr], wr=[k.psr[b]])
            ti = tmp_rr % 2
            tmp_rr += 1
            P.add("dve", lambda e, b=b, ti=ti, hg=hg: e.tensor_tensor(
                TMP[ti][:, :], k.ps[b][:, :], BS[:, hg * 512:(hg + 1) * 512], ALU.add),
                rd=[k.psr[b], gvbsr], wr=[TMPr[ti]])
            P.add("dve", lambda e, ti=ti, hg=hg, i=i: e.tensor_tensor(
                U[:, hg * 4:(hg + 1) * 4, i * 128:(i + 1) * 128],
                U[:, hg * 4:(hg + 1) * 4, i * 128:(i + 1) * 128],
                TMP[ti][:, :].rearrange("p (h t) -> p h t", h=4), ALU.mult),
                rd=[TMPr[ti]] + [Ur[hg * 4 + h][i] for h in range(4)],
                wr=[Ur[hg * 4 + h][i] for h in range(4)])

    wo = w_out_ab.rearrange("(kc p) n -> p kc n", p=128)

    def out_proj(part):
        for sblk in range(2):
            sl = k.load_w(wo[:, part * 8:(part + 1) * 8, sblk * 1024:(sblk + 1) * 1024],
                          lambda s: s[:, :].rearrange("p (a b) -> p a b", a=8))
            W8 = k.slots[sl][:, :].rearrange("p (a b) -> p a b", a=8)
            for oc in range(8):
                ocg = sblk * 8 + oc
                for half in range(2):
                    b = k.bank()
                    for kc in range(8):
                        if part == 0:
                            rr = [Ur[kc][i] for i in range(4 * half, 4 * half + 4)]
                        else:
                            rr = [BTr[kc][half]]
                        P.add("pe", lambda e, kc=kc, oc=oc, b=b, W8=W8, half=half: e.matmul(
                            k.ps[b][:, :], W8[:, kc, oc * 128:(oc + 1) * 128],
                            U[:, kc, half * 512:(half + 1) * 512],
                            start=(kc == 0), stop=(kc == 7)),
                            rd=[k.slotr[sl]] + rr, wr=[k.psr[b]])
                    P.add("dve", lambda e, b=b, ocg=ocg, half=half: e.tensor_tensor(
                        X[:, ocg, half * 512:(half + 1) * 512], X[:, ocg, half * 512:(half + 1) * 512],
                        k.ps[b][:, :], ALU.add),
                        rd=[k.psr[b], Xr[ocg][half]], wr=[Xr[ocg][half]])

    out_proj(0)
    P.add("pool", lambda e: e.dma_start(out=WP, in_=w_pool.rearrange("g (cc p) d -> p g cc d", p=128)),
          wr=[WPr], dma=WPr)
    UALL = [Ur[h][i] for h in range(8) for i in range(NT)]

    DT = BIG[:, :].rearrange("p (c t) -> p c t", c=8)
    DTr = VGr
    INVC = C[:, CA["INVC"]:CA["INVC"] + 64]
    PSC = C[:, CA["PSC"]:CA["PSC"] + 8]
    wins = (2, 4, 8, 16)
    for sblk in range(2):
        sl = k.load_w(wab[:, :, 2048 + sblk * 512:2048 + (sblk + 1) * 512], v16)
        for oc in range(4):
            c = sblk * 4 + oc
            g = c // 2
            bm = [k.bank(), k.bank()]
            bh = k.bank()
            for half in range(2):
                for kc in range(KC):
                    P.add("pe", lambda e, kc=kc, half=half, oc=oc, b=bm[half], sl=sl: e.matmul(
                        k.ps[b][:, :], v16(k.slots[sl])[:, kc, oc * 128:(oc + 1) * 128],
                        H[:, kc, half * 512:(half + 1) * 512], start=(kc == 0), stop=(kc == KC - 1)),
                        rd=[k.slotr[sl], Hr[kc][half]], wr=[k.psr[bm[half]]])
            for kc in range(KC):
                P.add("pe", lambda e, kc=kc, oc=oc, b=bh, sl=sl: e.matmul(
                    k.ps[b][:, 0:128], v16(k.slots[sl])[:, kc, oc * 128:(oc + 1) * 128],
                    HH[:, kc, :], start=(kc == 0), stop=(kc == KC - 1)),
                    rd=[k.slotr[sl], HHr[kc][0]], wr=[k.psr[bh]])
            for th in range(2):
                P.add("act", lambda e, th=th, b=bm[th]: e.activation(
                    Z[:, :, 16:144], k.ps[b][:, :].rearrange("p (i t) -> p i t", i=4), AF.Copy),
                    rd=[k.psr[bm[th]]], wr=[Zr])
                P.add("act", lambda e, b=bh, th=th: e.activation(
                    Z[:, :, 0:16], k.ps[b][:, th * 64:(th + 1) * 64].rearrange("p (i t) -> p i t", i=4), AF.Copy),
                    rd=[k.psr[bh]], wr=[Zr])
                cur, curr = Z, Zr
                sh = 1
                zi = 0
                while sh < wins[g]:
                    nxt, nxtr = ZS[zi % 2], ZSr[zi % 2]
                    zi += 1
                    P.add("dve", lambda e, cur=cur, nxt=nxt, sh=sh: e.tensor_tensor(
                        nxt[:, :, 2 * sh - 1:144], cur[:, :, 2 * sh - 1:144], cur[:, :, sh - 1:144 - sh], ALU.add),
                        rd=[curr], wr=[nxtr])
                    cur, curr = nxt, nxtr
                    sh *= 2
                P.add("dve", lambda e, cur=cur, c=c, g=g, th=th: e.scalar_tensor_tensor(
                    DT[:, c, th * 512:(th + 1) * 512].rearrange("p (i t) -> p i t", i=4), cur[:, :, 16:144],
                    1.0 / wins[g], Z[:, :, 16:144], ALU.mult, ALU.subtract),
                    rd=[curr, Zr], wr=DTr[4 * th:4 * th + 4] + [bigr])
                if th == 0:
                    ti = tmp_rr % 2
                    tmp_rr += 1
                    P.add("dve", lambda e, cur=cur, ti=ti, g=g: e.tensor_tensor(
                        TMP[ti][:, 0:16], cur[:, 0, 16:32], INVC[:, g * 16:(g + 1) * 16], ALU.mult),
                        rd=[curr, k.constr], wr=[TMPr[ti]])
                    P.add("dve", lambda e, ti=ti, c=c: e.tensor_tensor(
                        DT[:, c, 0:16], TMP[ti][:, 0:16], Z[:, 0, 16:32], ALU.subtract),
                        rd=[TMPr[ti], Zr], wr=[DTr[0], bigr])
            if c % 2 == 1:
                for dc in range(2):
                    for half in range(2):
                        b = k.bank()
                        for cc in range(2):
                            P.add("pe", lambda e, g=g, cc=cc, dc=dc, half=half, b=b: e.matmul(
                                k.ps[b][:, :], WP[:, g, cc, dc * 128:(dc + 1) * 128],
                                DT[:, 2 * g + cc, half * 512:(half + 1) * 512], start=(cc == 0), stop=(cc == 1)),
                                rd=[WPr] + DTr[4 * half:4 * half + 4], wr=[k.psr[b]])
                        P.add("act", lambda e, g=g, dc=dc, half=half, b=b: e.activation(
                            BT[:, 2 * g + dc, half * 512:(half + 1) * 512], k.ps[b][:, :], AF.Copy,
                            scale=PSC[:, 2 * g + dc:2 * g + dc + 1]),
                            rd=[k.psr[b], k.constr], wr=[BTr[2 * g + dc][half]] + (UALL if (g == 0 and dc == 0 and half == 0) else []))
    out_proj(1)
    k.barrier()
    k.slot_n = 3
    k.ACTB = [BIG[:, 0:4096].rearrange("p (a b) -> p a b", a=4), BIG[:, 4096:8192].rearrange("p (a b) -> p a b", a=4)]
    k.ACTBr = P.Rs("ACTB", 2, 4, 2)
    gf0 = C[:, CA["GF0"]:CA["GF0"] + 16]
    k.norm_fm(X, Xr, KC, [(0, 512, 0), (512, 512, 1)], gf0, H, Hr, 0)
    k.ffn(w_gate, w_up, w_down, 0)
    k.barrier()
    k.slot_n = 2
    k.slot_rr = 0
    CSr = P.R("CS")
    P.add("sp", lambda e: e.dma_start(out=S2F[:, 0:2048], in_=cossin), wr=[CSr], dma=CSr)
    x1r = x1T.rearrange("(kc p) t -> p kc t", p=128)
    outr = P.R("out")
    for q in range(4):
        P.add("sp", lambda e, q=q: e.dma_start(out=x1r[:, 4 * q:4 * q + 4, :], in_=X[:, 4 * q:4 * q + 4, :]),
              rd=[Xr[kc][h] for kc in range(4 * q, 4 * q + 4) for h in range(2)], wr=[outr], dma=outr)
    gm1 = C[:, CA["GM1"]:CA["GM1"] + 16]
    k.norm_fm(X, Xr, KC, [(0, 512, 0), (512, 512, 1)], gm1, H, Hr, 0)
    wckv = w_ckv.rearrange("(kc p) n -> p kc n", p=128)
    sl = k.load_w(wckv, v16)
    CKV = BIG[:, :].bitcast(F32).rearrange("p (a b) -> p a b", a=4)
    CKVr = P.Rs("CKV", 4, 2)
    for oc in range(4):
        for half in range(2):
            b = k.bank()
            for kc in range(KC):
                P.add("pe", lambda e, kc=kc, half=half, oc=oc, b=b, sl=sl: e.matmul(
                    k.ps[b][:, :], v16(k.slots[sl])[:, kc, oc * 128:(oc + 1) * 128],
                    H[:, kc, half * 512:(half + 1) * 512], start=(kc == 0), stop=(kc == KC - 1)),
                    rd=[k.slotr[sl], Hr[kc][half]], wr=[k.psr[b]])
            P.add("act", lambda e, b=b, oc=oc, half=half: e.activation(
                CKV[:, oc, half * 512:(half + 1) * 512], k.ps[b][:, :], AF.Copy),
                rd=[k.psr[b]], wr=[CKVr[oc][half]])
    CKN = U[:, :, :].rearrange("p a b -> p (a b)").bitcast(F32).rearrange("p (a b) -> p a b", a=4)
    CKNr = P.Rs("CKN", 4, 2)
    gckv = C[:, CA["GCKV"]:CA["GCKV"] + 4]
    k.norm_fm(CKV, CKVr, 4, [(0, 512, 0), (512, 512, 1)], gckv, CKN, CKNr, 1)
    lcr = latc.rearrange("(kc p) t -> p kc t", p=128)
    P.add("sp", lambda e: e.dma_start(out=lcr, in_=CKN),
          rd=[CKNr[c][h] for c in range(4) for h in range(2)], wr=[outr], dma=outr)
    v128 = lambda s: s[:, 0:2048].rearrange("p (a b) -> p a b", a=16)
    sl = k.load_w(w_kr.rearrange("(kc p) n -> p kc n", p=128), v128)
    P.add("dve", lambda e, sl=sl: e.tensor_scalar(
        v128(k.slots[sl])[:, :, 64:96], v128(k.slots[sl])[:, :, 64:96], -1.0, None, ALU.mult),
        rd=[k.slotr[sl]], wr=[k.slotr[sl]])
    KTMP = k.rstd[0][0:64, :]
    KTMPr = k.rstdr[0]
    COS = S2F[:, 0:T]
    SIN = S2F[:, T:2 * T]
    for half in range(2):
        ba, bb = k.bank(), k.bank()
        for (b, c0) in ((ba, 0), (bb, 64)):
            for kc in range(KC):
                P.add("pe", lambda e, kc=kc, half=half, b=b, c0=c0, sl=sl: e.matmul(
                    k.ps[b][0:64, :], v128(k.slots[sl])[:, kc, c0:c0 + 64],
                    H[:, kc, half * 512:(half + 1) * 512], start=(kc == 0), stop=(kc == KC - 1)),
                    rd=[k.slotr[sl], Hr[kc][half]], wr=[k.psr[b]])
        P.add("dve", lambda e, ba=ba, half=half: e.tensor_tensor(
            TG[half][0:64, :], k.ps[ba][0:64, :], COS[0:64, half * 512:(half + 1) * 512], ALU.mult),
            rd=[k.psr[ba], CSr], wr=[TGr[half]])
        P.add("dve", lambda e, bb=bb, half=half: e.tensor_tensor(
            KTMP[:, :], k.ps[bb][0:64, :], SIN[0:64, half * 512:(half + 1) * 512], ALU.mult),
            rd=[k.psr[bb], CSr], wr=[KTMPr])
        P.add("dve", lambda e, half=half: e.tensor_tensor(
            TG[half][0:64, :], TG[half][0:64, :], KTMP[:, :], ALU.add),
            rd=[KTMPr, TGr[half]], wr=[TGr[half]])
        P.add("sp", lambda e, half=half: e.dma_start(out=latr[:, half * 512:(half + 1) * 512], in_=TG[half][0:64, :]),
              rd=[TGr[half]], wr=[outr], dma=outr)
    P.finish("sp", [outr])
    P.emit(nc, k.es)
    k.es.close()
    return nc


CB = dict(GM1=0, GF1=16, GFIN=32, GCQ=48, COS=64, SIN=1088, MASK=2112, N=2624)


def build_B():
    k = KB()
    nc, P = k.nc, k.P
    x1T = k.din("x1T", [D, T])
    consts = k.din("consts", [128, CB["N"]])
    ckv_all = k.din("ckv_all", [512, 4 * T])
    kr_all = k.din("kr_all", [64, 4 * T])
    w_cq = k.din("w_cq", [D, 512])
    w_att = k.din("w_att", [512, 16, 512])
    w_out_c = k.din("w_out_c", [D, D])
    w_gate = k.din("w_gate", [1, D, DFF])
    w_up = k.din("w_up", [1, D, DFF])
    w_down = k.din("w_down", [1, DFF, D])
    yT = k.dout("yT", [D, T])

    k.common()
    X = k.X = k.sb("X", [128, KC, T], F32)
    Xr = k.Xr = P.Rs("X", KC, 2)
    H = k.H = k.sb("H", [128, KC, T], BF16)
    Hr = k.Hr = P.Rs("H", KC, 2)
    C = k.sb("C", [128, CB["N"]], F32)
    k.constr = P.R("const")
    TMP = [k.sb(f"TMP{i}", [128, 512], F32) for i in range(2)]
    TMPr = [P.R("TMP") for _ in range(2)]
    tmp_rr = 0
    k.SG = TMP
    k.SGr = TMPr
    CQ = k.slots[2][:, :].bitcast(F32).rearrange("p (a b) -> p a b", a=4)
    CQr = P.Rs("CQ", 4, 2)
    CQN = k.sb("CQN", [128, 4, T], BF16)
    CQNr = P.Rs("CQN", 4, 2)
    KRA = k.sb("KRA", [64, 4 * T], BF16)
    KRAr = P.R("KRA")
    MASKB = k.sb("MASKB", [128, 512], BF16)
    MASKBr = P.R("MASKB")
    OT = k.sb("OT", [128, 4, T], BF16)
    OTr = P.Rs("OT", 4, 2)
    S2 = k.slots[2]
    QN = [S2[:, 4096 + 1024 * i:4096 + 1024 * (i + 1)] for i in range(2)]
    QNr = [P.Rs("QN", 2) for _ in range(2)]
    QR = [S2[0:64, 6144 + 1024 * i:6144 + 1024 * (i + 1)] for i in range(2)]
    QRr = [P.Rs("QR", 2) for _ in range(2)]
    VVt = k.sb("VV", [128, 32, 128], BF16)
    VV = [VVt, VVt]
    _vvr = P.R("VV")
    VVr = [_vvr, _vvr]
    PT = [k.sb(f"PT{i}", [128, 512], BF16) for i in range(2)]
    PTr = [P.R("PT") for _ in range(2)]
    RCP = [k.sb(f"RCP{i}", [128, 512], F32) for i in range(1)]
    RCPr = [P.R("RCP") for _ in range(1)]

    xr = x1T.rearrange("(kc p) t -> p kc t", p=128)
    Xin = P.Rs("Xin", 4)
    for q in range(4):
        P.add("sp", lambda e, q=q: e.dma_start(out=X[:, 4 * q:4 * q + 4, :], in_=xr[:, 4 * q:4 * q + 4, :]),
              wr=[Xr[kc][h] for kc in range(4 * q, 4 * q + 4) for h in range(2)] + [Xin[q]], dma=Xin[q])
    P.add("sp", lambda e: e.dma_start(out=C[:, :], in_=consts), wr=[k.constr], dma=k.constr)
    for q in range(4):
        P.add("pool", lambda e, q=q: e.dma_start(out=KRA[:, q * 1024:(q + 1) * 1024], in_=kr_all[:, q * 1024:(q + 1) * 1024]),
              wr=[KRAr], dma=KRAr)
    P.add("dve", lambda e: e.tensor_copy(MASKB[:, :], C[:, CB["MASK"]:CB["MASK"] + 512]),
          rd=[k.constr], wr=[MASKBr])
    COS = C[:, CB["COS"]:CB["COS"] + T]
    SIN = C[:, CB["SIN"]:CB["SIN"] + T]

    gm1 = C[:, CB["GM1"]:CB["GM1"] + 16]
    k.norm_fm(X, Xr, KC, [(0, 512, 0), (512, 512, 1)], gm1, H, Hr, 0)
    v16 = lambda s: s[:, :].rearrange("p (a b) -> p a b", a=16)
    v4 = lambda s: s[:, :].rearrange("p (a b) -> p a b", a=4)
    sl = k.load_w(w_cq.rearrange("(kc p) n -> p kc n", p=128), v16)
    for oc in range(4):
        for half in range(2):
            b = k.bank()
            for kc in range(KC):
                P.add("pe", lambda e, kc=kc, half=half, oc=oc, b=b, sl=sl: e.matmul(
                    k.ps[b][:, :], v16(k.slots[sl])[:, kc, oc * 128:(oc + 1) * 128],
                    H[:, kc, half * 512:(half + 1) * 512], start=(kc == 0), stop=(kc == KC - 1)),
                    rd=[k.slotr[sl], Hr[kc][half]], wr=[k.psr[b]])
            P.add("act", lambda e, b=b, oc=oc, half=half: e.activation(
                CQ[:, oc, half * 512:(half + 1) * 512], k.ps[b][:, :], AF.Copy),
                rd=[k.psr[b]], wr=[CQr[oc][half]])
    gcq = C[:, CB["GCQ"]:CB["GCQ"] + 4]
    k.norm_fm(CQ, CQr, 4, [(0, 512, 0), (512, 512, 1)], gcq, CQN, CQNr, 1)
    k.barrier()
    CKA = H[:, :, :].rearrange("p a b -> p (a b)").rearrange("p (c n) -> p c n", c=4)
    CKAr = P.R("CKA")
    ckr = ckv_all.rearrange("(c p) n -> p c n", p=128)
    for q in range(4):
        P.add("pool", lambda e, q=q: e.dma_start(out=CKA[:, :, q * 1024:(q + 1) * 1024], in_=ckr[:, :, q * 1024:(q + 1) * 1024]),
              wr=[CKAr], dma=CKAr)
    KNt = k.slots[2][:, :].rearrange("p (a n) -> p a n", a=2)
    _knr = P.R("KN")
    KNr = [_knr, _knr]
    va = lambda s: s[:, :].rearrange("p (h kc c) -> p h kc c", h=4, kc=4)
    watt = w_att.rearrange("(kc p) h c -> p h kc c", p=128)
    woc = w_out_c.rearrange("(kc p) n -> p kc n", p=128)
    slot_seq = [0, 1]
    ps_rr = 0
    for hgp in range(4):
        s = slot_seq[hgp % 2]
        for hh in range(4):
            P.add("pool", lambda e, s=s, hh=hh, hgp=hgp: e.dma_start(
                out=va(k.slots[s])[:, hh, :, :], in_=watt[:, hgp * 4 + hh, :, :]),
                wr=[k.slotr[s]], dma=k.slotr[s])
        P.add("dve", lambda e, s=s: e.tensor_scalar(
            va(k.slots[s])[:, :, :, 192:224], va(k.slots[s])[:, :, :, 192:224], -1.0, None, ALU.mult),
            rd=[k.slotr[s]], wr=[k.slotr[s]])
        W = va(k.slots[s])
        Wr = k.slotr[s]
        for hh in range(4):
            h = hgp * 4 + hh
            pb = h % 2
            for half in range(2):
                b = 6 + (ps_rr % 2)
                ps_rr += 1
                for kc in range(4):
                    P.add("pe", lambda e, kc=kc, half=half, b=b, hh=hh, W=W: e.matmul(
                        k.ps[b][:, :], W[:, hh, kc, 0:128], CQN[:, kc, half * 512:(half + 1) * 512],
                        start=(kc == 0), stop=(kc == 3)),
                        rd=[Wr, CQNr[kc][half]], wr=[k.psr[b]])
                P.add("act", lambda e, b=b, half=half, pb=pb: e.activation(
                    QN[pb][:, half * 512:(half + 1) * 512], k.ps[b][:, :], AF.Copy),
                    rd=[k.psr[b]], wr=[QNr[pb][half]])
            for half in range(2):
                ba = 6 + (ps_rr % 2)
                ps_rr += 1
                bb = 6 + (ps_rr % 2)
                ps_rr += 1
                for (b, c0) in ((ba, 128), (bb, 192)):
                    for kc in range(4):
                        P.add("pe", lambda e, kc=kc, half=half, b=b, hh=hh, W=W, c0=c0: e.matmul(
                            k.ps[b][0:64, :], W[:, hh, kc, c0:c0 + 64], CQN[:, kc, half * 512:(half + 1) * 512],
                            start=(kc == 0), stop=(kc == 3)),
                            rd=[Wr, CQNr[kc][half]], wr=[k.psr[b]])
                t0 = tmp_rr % 2
                tmp_rr += 1
                t1 = tmp_rr % 2
                tmp_rr += 1
                P.add("dve", lambda e, ba=ba, half=half, t0=t0: e.tensor_tensor(
                    TMP[t0][0:64, :], k.ps[ba][0:64, :], COS[0:64, half * 512:(half + 1) * 512], ALU.mult),
                    rd=[k.psr[ba], k.constr], wr=[TMPr[t0]])
                P.add("dve", lambda e, bb=bb, half=half, t1=t1: e.tensor_tensor(
                    TMP[t1][0:64, :], k.ps[bb][0:64, :], SIN[0:64, half * 512:(half + 1) * 512], ALU.mult),
                    rd=[k.psr[bb], k.constr], wr=[TMPr[t1]])
                P.add("dve", lambda e, half=half, t0=t0, t1=t1, pb=pb: e.tensor_tensor(
                    QR[pb][:, half * 512:(half + 1) * 512], TMP[t0][0:64, :], TMP[t1][0:64, :], ALU.add),
                    rd=[TMPr[t0], TMPr[t1]], wr=[QRr[pb][half]])
            for kb in range(8):
                b = 6 + (ps_rr % 2)
                ps_rr += 1
                for kc in range(4):
                    P.add("pe", lambda e, kc=kc, kb=kb, b=b, hh=hh, W=W: e.matmul(
                        k.ps[b][:, :], W[:, hh, kc, 256:384], CKA[:, kc, kb * 512:(kb + 1) * 512],
                        start=(kc == 0), stop=(kc == 3)),
                        rd=[Wr, CKAr], wr=[k.psr[b]])
                P.add("act", lambda e, b=b, kb=kb, pb=pb: e.activation(
                    KNt[:, 0, kb * 512:(kb + 1) * 512], k.ps[b][:, :], AF.Copy),
                    rd=[k.psr[b]], wr=[KNr[pb]])
            for jb in range(8):
                b = 6 + (ps_rr % 2)
                ps_rr += 1
                for j4 in range(4):
                    j = jb * 4 + j4
                    for kc in range(4):
                        P.add("pe", lambda e, kc=kc, j=j, j4=j4, b=b, hh=hh, W=W: e.matmul(
                            k.ps[b][:, j4 * 128:(j4 + 1) * 128], CKA[:, kc, j * 128:(j + 1) * 128],
                            W[:, hh, kc, 384:512], start=(kc == 0), stop=(kc == 3)),
                            rd=[Wr, CKAr], wr=[k.psr[b]])
                P.add("dve", lambda e, b=b, jb=jb, pb=pb: e.tensor_copy(
                    VV[pb][:, jb * 4:(jb + 1) * 4, :], k.ps[b][:, :].rearrange("p (j d) -> p j d", j=4)),
                    rd=[k.psr[b]], wr=[VVr[pb]])
            for G in range(2):
                bo = 2 + (h * 2 + G) % 2
                bl = 4 + (h * 2 + G) % 2
                nblk = 16 * G + 16
                def blk(j, G=G):
                    ip, rp = j // 4, j % 4
                    kcol = rp * 1024 + ip * 128
                    imin = max(ip, 4 * G)
                    c0 = (imin - 4 * G) * 128
                    return ip, rp, kcol, c0, 512 - c0, G * 512 + c0

                def emit_S(j, pb=pb, G=G):
                    ip, rp, kcol, c0, n, q0 = blk(j)
                    bs = j % 2
                    P.add("pe", lambda e, bs=bs, pb=pb, kcol=kcol, q0=q0, n=n: e.matmul(
                        k.ps[bs][:, 0:n], KNt[:, 0, kcol:kcol + 128], QN[pb][:, q0:q0 + n],
                        start=True, stop=False),
                        rd=[KNr[pb], QNr[pb][G]], wr=[k.psr[bs]])
                    P.add("pe", lambda e, bs=bs, pb=pb, kcol=kcol, q0=q0, n=n: e.matmul(
                        k.ps[bs][:, 0:n], KRA[:, kcol:kcol + 128], QR[pb][:, q0:q0 + n],
                        start=False, stop=True),
                        rd=[KRAr, QRr[pb][G]], wr=[k.psr[bs]])

                emit_S(0)
                for j in range(nblk):
                    if j + 1 < nblk:
                        emit_S(j + 1)
                    ip, rp, kcol, c0, n, q0 = blk(j)
                    bs = j % 2
                    pi = j % 2
                    P.add("act", lambda e, bs=bs, pi=pi, n=n: e.activation(
                        PT[pi][:, 0:n], k.ps[bs][:, 0:n], AF.Exp, scale=SM_SCALE),
                        rd=[k.psr[bs]], wr=[PTr[pi]])
                    if ip >= 4 * G:
                        P.add("dve", lambda e, pi=pi, rp=rp: e.tensor_tensor(
                            PT[pi][:, 0:128], PT[pi][:, 0:128], MASKB[:, rp * 128:(rp + 1) * 128], ALU.mult),
                            rd=[PTr[pi], MASKBr], wr=[PTr[pi]])
                    P.add("pe", lambda e, bo=bo, pb=pb, j=j, pi=pi, c0=c0, n=n, nblk=nblk: e.matmul(
                        k.ps[bo][:, c0:c0 + n], VV[pb][:, (j % 4) * 8 + j // 4, :], PT[pi][:, 0:n],
                        start=(j == 0), stop=(j == nblk - 1)),
                        rd=[VVr[pb], PTr[pi]], wr=[k.psr[bo]])
                    P.add("pe", lambda e, bl=bl, pi=pi, c0=c0, n=n, j=j, nblk=nblk: e.matmul(
                        k.ps[bl][:, c0:c0 + n], k.ones[:, 2, :], PT[pi][:, 0:n],
                        start=(j == 0), stop=(j == nblk - 1)),
                        rd=[k.onesr, PTr[pi]], wr=[k.psr[bl]])
                ri = 0
                P.add("dve", lambda e, bl=bl, ri=ri: e.reciprocal(RCP[ri][:, :], k.ps[bl][:, :]),
                      rd=[k.psr[bl]], wr=[RCPr[ri]])
                ob = hh
                P.add("dve", lambda e, bo=bo, ri=ri, ob=ob, G=G: e.tensor_tensor(
                    OT[:, ob, G * 512:(G + 1) * 512], k.ps[bo][:, :], RCP[ri][:, :], ALU.mult),
                    rd=[k.psr[bo], RCPr[ri]], wr=[OTr[ob][G]])
        ws = slot_seq[(hgp + 1) % 2]
        P.add("pool", lambda e, ws=ws, hgp=hgp: e.dma_start(
            out=v4(k.slots[ws]), in_=woc[:, hgp * 4:(hgp + 1) * 4, :]),
            wr=[k.slotr[ws]], dma=k.slotr[ws])
        for oc in range(KC):
            for half in range(2):
                b = 6 + (ps_rr % 2)
                ps_rr += 1
                for kc in range(4):
                    ob = kc
                    P.add("pe", lambda e, kc=kc, half=half, oc=oc, b=b, ws=ws, ob=ob: e.matmul(
                        k.ps[b][:, :], v4(k.slots[ws])[:, kc, oc * 128:(oc + 1) * 128],
                        OT[:, ob, half * 512:(half + 1) * 512], start=(kc == 0), stop=(kc == 3)),
                        rd=[k.slotr[ws], OTr[ob][half]], wr=[k.psr[b]])
                P.add("dve", lambda e, b=b, oc=oc, half=half: e.tensor_tensor(
                    X[:, oc, half * 512:(half + 1) * 512], X[:, oc, half * 512:(half + 1) * 512],
                    k.ps[b][:, :], ALU.add),
                    rd=[k.psr[b], Xr[oc][half]], wr=[Xr[oc][half]])
    k.barrier()
    k.slot_rr = 0
    k.ACTB = [OT, VVt[:, :, :].rearrange("p a b -> p (a b)").rearrange("p (a b) -> p a b", a=4)]
    k.ACTBr = P.Rs("ACTB", 2, 4, 2)
    gf1 = C[:, CB["GF1"]:CB["GF1"] + 16]
    k.norm_fm(X, Xr, KC, [(0, 512, 0), (512, 512, 1)], gf1, H, Hr, 0)
    k.ffn(w_gate, w_up, w_down, 0)
    k.barrier()
    gfin = C[:, CB["GFIN"]:CB["GFIN"] + 16]
    YF = H[:, :, :].rearrange("p a b -> p (a b)").bitcast(F32).rearrange("p (a b) -> p a b", a=8)
    YFr = P.Rs("YF", 8, 2)
    yr = yT.rearrange("(kc p) t -> p kc t", p=128)
    outr = P.R("out")
    ones = k.ones
    rst = []
    for (c0, n, pi) in [(0, 512, 0), (512, 512, 1)]:
        b = k.bank()
        for kc in range(KC):
            qi = k.sq_rr % 2
            k.sq_rr += 1
            P.add("act", lambda e, qi=qi, kc=kc, c0=c0, n=n: e.activation(
                k.sq[qi][:, 0:n], X[:, kc, c0:c0 + n], AF.Square),
                rd=[Xr[kc][pi]], wr=[k.sqr[qi]])
            P.add("pe", lambda e, qi=qi, kc=kc, n=n, b=b: e.matmul(
                k.ps[b][:, 0:n], ones[:, 0, :], k.sq[qi][:, 0:n], start=(kc == 0), stop=(kc == KC - 1)),
                rd=[k.sqr[qi], k.onesr], wr=[k.psr[b]])
        P.add("act", lambda e, b=b, pi=pi: e.activation(k.rstd[pi][:, :], k.ps[b][:, :], AF.Sqrt, bias=EPS, scale=1.0),
              rd=[k.psr[b]], wr=[k.rstdr[pi]])
        P.add("dve", lambda e, pi=pi: e.reciprocal(k.rstd[pi][:, :], k.rstd[pi][:, :]),
              rd=[k.rstdr[pi]], wr=[k.rstdr[pi]])
    for part in range(2):
        for kc8 in range(8):
            kc = part * 8 + kc8
            for pi in range(2):
                P.add("dve", lambda e, kc=kc, kc8=kc8, pi=pi: e.scalar_tensor_tensor(
                    YF[:, kc8, pi * 512:(pi + 1) * 512], X[:, kc, pi * 512:(pi + 1) * 512], gfin[:, kc:kc + 1],
                    k.rstd[pi][:, :], ALU.mult, ALU.mult),
                    rd=[Xr[kc][pi], k.rstdr[pi], k.constr], wr=[YFr[kc8][pi]])
        P.add("sp", lambda e, part=part: e.dma_start(out=yr[:, part * 8:(part + 1) * 8, :], in_=YF),
              rd=[YFr[c][h] for c in range(8) for h in range(2)], wr=[outr], dma=outr)
    P.finish("sp", [outr])
    P.emit(nc, k.es)
    k.es.close()
    return nc


def build_F():
    k = KB()
    nc, P = k.nc, k.P
    xT = k.din("xT", [4, D, T])
    xhT = k.din("xhT", [4, D, 128])
    constsA = k.din("constsA", [4, 128, CA["N"]])
    gvbs = k.din("gvbs", [128, 2048])
    cossin = k.din("cossin", [4, 128, 2048])
    constsB = k.din("constsB", [128, CB["N"]])
    w_in_ab = k.din("w_in_ab", [D, 3072])
    w_sT = k.din("w_sT", [8, 128, 128])
    w_pool = k.din("w_pool", [4, 256, 256])
    w_out_ab = k.din("w_out_ab", [D, D])
    w_gate2 = k.din("w_gate", [2, D, DFF])
    w_up2 = k.din("w_up", [2, D, DFF])
    w_down2 = k.din("w_down", [2, DFF, D])
    w_ckv = k.din("w_ckv", [D, 512])
    w_kr = k.din("w_kr", [D, 128])
    w_cq = k.din("w_cq", [D, 512])
    w_att = k.din("w_att", [512, 16, 512])
    w_out_c = k.din("w_out_c", [D, D])
    yT = k.dout("yT", [D, T])
    latc_d = nc.dram_tensor("latc_d", [512, 4 * T], F32, kind="Internal").ap()
    latr_d = nc.dram_tensor("latr_d", [64, 4 * T], F32, kind="Internal").ap()
    latdr = P.R("latd")

    k.common(2)
    X = k.X = k.sb("X", [128, KC, T], F32)
    Xr = k.Xr = P.Rs("X", KC, 2)
    H = k.H = k.sb("H", [128, KC, T], BF16)
    Hr = k.Hr = P.Rs("H", KC, 2)
    TG = [k.sb(f"TG{i}", [128, 512], F32) for i in range(2)]
    TGr = [P.R("TG") for _ in range(2)]
    k.SG = TG
    k.SGr = TGr
    TMP, TMPr = TG, TGr
    ARENA_N = 24320
    ARENA = k.sb("ARENA", [128, ARENA_N], BF16)
    apos = [0]

    def carve(nelem_bf16):
        a = apos[0]
        apos[0] = a + nelem_bf16
        assert apos[0] <= ARENA_N, apos[0]
        return ARENA[:, a:a + nelem_bf16]

    HH = carve(2048).rearrange("p (a b) -> p a b", a=KC)
    HHr = P.Rs("HH", KC, 1)
    C_A = carve(2 * CA["N"]).bitcast(F32)
    constr_A = P.R("const")
    U = carve(8192).rearrange("p (a b) -> p a b", a=8)
    Ur = P.Rs("U", 8, NT)
    BIG = carve(8192)
    bigr = P.R("BIG")
    BT = U
    BTr = P.Rs("BT", 8, 2)
    S2F = k.slots[2][:, :].bitcast(F32)
    GVBS = S2F[:, 0:2048]
    gvbsr = P.R("GVBS")
    Z = S2F[:, 2048:2048 + 576].rearrange("p (i t) -> p i t", i=4)
    Zr = P.R("Z")
    ZS = [S2F[:, 2624 + 576 * i:2624 + 576 * (i + 1)].rearrange("p (i t) -> p i t", i=4) for i in range(2)]
    ZSr = [P.R("ZS") for _ in range(2)]
    WSWP = carve(2048)
    WST = WSWP[:, 0:1024].rearrange("p (h t) -> p h t", h=8)
    WP = WSWP[:, :].rearrange("p (g c d) -> p g c d", g=4, c=2)
    WSTr = P.R("WSWP")
    WPr = WSTr
    SSQ = carve(32).bitcast(F32)
    SSQr = P.Rs("SSQ", 16)
    RV = carve(16).bitcast(F32)
    RVr = P.R("RV")
    w_gate, w_up, w_down = w_gate2, w_up2, w_down2

    def pass_A(s):
        C = C_A
        k.constr = constr_A
        xr = xT[s].rearrange("(kc p) t -> p kc t", p=128)
        Xin = P.Rs("Xin", 4)
        for q in range(4):
            P.add("sp", lambda e, q=q: e.dma_start(out=X[:, 4 * q:4 * q + 4, :], in_=xr[:, 4 * q:4 * q + 4, :]),
                  wr=[Xr[kc][h] for kc in range(4 * q, 4 * q + 4) for h in range(2)] + [Xin[q]], dma=Xin[q])
        P.add("sp", lambda e: e.dma_start(out=C[:, :], in_=constsA[s]), wr=[k.constr], dma=k.constr)
        XHt = BIG[:, 0:4096].bitcast(F32).rearrange("p (a b) -> p a b", a=KC)
        XHr = [[bigr] for _ in range(KC)]
        XHd = P.R("XHd")
        P.add("sp", lambda e: e.dma_start(out=XHt, in_=xhT[s].rearrange("(kc p) t -> p kc t", p=128)),
              wr=[bigr, XHd], dma=XHd)
        P.add("sp", lambda e: e.dma_start(out=GVBS, in_=gvbs), wr=[gvbsr], dma=gvbsr)
        P.add("pool", lambda e: e.dma_start(out=WST, in_=w_sT.rearrange("h s t -> s h t")),
              wr=[WSTr], dma=WSTr)
        P.add("dve", lambda e: e.memset(WST[64:128, :, 0:64], 0.0), wr=[WSTr])

        gm0 = C[:, CA["GM0"]:CA["GM0"] + 16]
        k.norm_fm(X, Xr, KC, [(0, 512, 0), (512, 512, 1)], gm0, H, Hr, 0)
        k.norm_fm(XHt, XHr, KC, [(0, 128, 0)], gm0, HH, HHr, 0)

        v16 = lambda s: s[:, :].rearrange("p (a b) -> p a b", a=16)
        v8 = lambda s: s[:, 0:4096].rearrange("p (a b) -> p a b", a=8)
        wab = w_in_ab.rearrange("(kc p) n -> p kc n", p=128)
        VG = BIG[:, :].rearrange("p (i c) -> p i c", i=NT)
        VGr = P.Rs("VG", NT)
        k.slot_n = 2

        for sblk in range(2):
            sl = k.load_w(wab[:, :, sblk * 512:(sblk + 1) * 512], v16)
            for oc in range(4):
                hd = sblk * 4 + oc
                for half in range(2):
                    b = k.bank()
                    for kc in range(KC):
                        P.add("pe", lambda e, kc=kc, half=half, oc=oc, b=b, sl=sl: e.matmul(
                            k.ps[b][:, :], v16(k.slots[sl])[:, kc, oc * 128:(oc + 1) * 128],
                            H[:, kc, half * 512:(half + 1) * 512], start=(kc == 0), stop=(kc == KC - 1)),
                            rd=[k.slotr[sl], Hr[kc][half]], wr=[k.psr[b]])
                    P.add("act", lambda e, b=b, hd=hd, half=half: e.activation(
                        U[:, hd, half * 512:(half + 1) * 512], k.ps[b][:, :], AF.Gelu_apprx_tanh),
                        rd=[k.psr[b]], wr=[Ur[hd][i] for i in range(4 * half, 4 * half + 4)])
        tg_rr = 0
        for sblk in range(2):
            sl = k.load_w(wab[:, :, 1024 + sblk * 512:1024 + (sblk + 1) * 512], v16)
            for i in range(NT):
                b = k.bank()
                for kc in range(KC):
                    P.add("pe", lambda e, kc=kc, i=i, b=b, sl=sl: e.matmul(
                        k.ps[b][:, :], H[:, kc, i * 128:(i + 1) * 128], v16(k.slots[sl])[:, kc, :],
                        start=(kc == 0), stop=(kc == KC - 1)),
                        rd=[k.slotr[sl], Hr[kc][i // 4]], wr=[k.psr[b]])
                ti = tg_rr % 2
                tg_rr += 1
                P.add("act", lambda e, b=b, ti=ti: e.activation(TG[ti][:, :], k.ps[b][:, :], AF.Gelu_apprx_tanh),
                      rd=[k.psr[b]], wr=[TGr[ti]])
                qi = k.sq_rr % 2
                k.sq_rr += 1
                P.add("act", lambda e, ti=ti, i=i, sblk=sblk, qi=qi: e.activation(
                    k.sq[qi][:, :], TG[ti][:, :], AF.Square,
                    accum_out=SSQ[:, 2 * i + sblk:2 * i + sblk + 1]),
                    rd=[TGr[ti]], wr=[k.sqr[qi], SSQr[2 * i + sblk]])
                P.add("dve", lambda e, ti=ti, i=i, sblk=sblk: e.tensor_copy(
                    VG[:, i, sblk * 512:(sblk + 1) * 512], TG[ti][:, :]),
                    rd=[TGr[ti]], wr=[VGr[i], bigr])
        SS3 = SSQ[:, :].rearrange("p (i s) -> p i s", s=2)
        P.add("dve", lambda e: e.tensor_tensor(RV[:, :], SS3[:, :, 0], SS3[:, :, 1], ALU.add),
              rd=SSQr, wr=[RVr])
        P.add("act", lambda e: e.activation(RV[:, :], RV[:, :], AF.Sqrt, bias=EPS, scale=1.0 / 1024),
              rd=[RVr], wr=[RVr])
        P.add("dve", lambda e: e.reciprocal(RV[:, :], RV[:, :]), rd=[RVr], wr=[RVr])
        GV = GVBS[:, 0:1024]
        BS = GVBS[:, 1024:2048]
        for i in range(NT):
            P.add("dve", lambda e, i=i: e.scalar_tensor_tensor(
                VG[:, i, :], VG[:, i, :], RV[:, i:i + 1], GV, ALU.mult, ALU.mult),
                rd=[VGr[i], RVr, gvbsr], wr=[VGr[i]])
        tmp_rr = 0
        for i in range(NT):
            for hg in range(2):
                b = k.bank()
                for h4 in range(4):
                    hd = hg * 4 + h4
                    P.add("pe", lambda e, i=i, hd=hd, h4=h4, b=b: e.matmul(
                        k.ps[b][:, h4 * 128:(h4 + 1) * 128], VG[:, i, hd * 128:(hd + 1) * 128], WST[:, hd, :],
                        start=True, stop=True),
                        rd=[VGr[i], WSTr], wr=[k.psr[b]])
                ti = tmp_rr % 2
                tmp_rr += 1
                P.add("dve", lambda e, b=b, ti=ti, hg=hg: e.tensor_tensor(
                    TMP[ti][:, :], k.ps[b][:, :], BS[:, hg * 512:(hg + 1) * 512], ALU.add),
                    rd=[k.psr[b], gvbsr], wr=[TMPr[ti]])
                P.add("dve", lambda e, ti=ti, hg=hg, i=i: e.tensor_tensor(
                    U[:, hg * 4:(hg + 1) * 4, i * 128:(i + 1) * 128],
                    U[:, hg * 4:(hg + 1) * 4, i * 128:(i + 1) * 128],
                    TMP[ti][:, :].rearrange("p (h t) -> p h t", h=4), ALU.mult),
                    rd=[TMPr[ti]] + [Ur[hg * 4 + h][i] for h in range(4)],
                    wr=[Ur[hg * 4 + h][i] for h in range(4)])

        wo = w_out_ab.rearrange("(kc p) n -> p kc n", p=128)

        def out_proj(part):
            for sblk in range(2):
                sl = k.load_w(wo[:, part * 8:(part + 1) * 8, sblk * 1024:(sblk + 1) * 1024],
                              lambda s: s[:, :].rearrange("p (a b) -> p a b", a=8))
                W8 = k.slots[sl][:, :].rearrange("p (a b) -> p a b", a=8)
                for oc in range(8):
                    ocg = sblk * 8 + oc
                    for half in range(2):
                        b = k.bank()
                        for kc in range(8):
                            if part == 0:
                                rr = [Ur[kc][i] for i in range(4 * half, 4 * half + 4)]
                            else:
                                rr = [BTr[kc][half]]
                            P.add("pe", lambda e, kc=kc, oc=oc, b=b, W8=W8, half=half: e.matmul(
                                k.ps[b][:, :], W8[:, kc, oc * 128:(oc + 1) * 128],
                                U[:, kc, half * 512:(half + 1) * 512],
                                start=(kc == 0), stop=(kc == 7)),
                                rd=[k.slotr[sl]] + rr, wr=[k.psr[b]])
                        P.add("dve", lambda e, b=b, ocg=ocg, half=half: e.tensor_tensor(
                            X[:, ocg, half * 512:(half + 1) * 512], X[:, ocg, half * 512:(half + 1) * 512],
                            k.ps[b][:, :], ALU.add),
                            rd=[k.psr[b], Xr[ocg][half]], wr=[Xr[ocg][half]])

        out_proj(0)
        P.add("pool", lambda e: e.dma_start(out=WP, in_=w_pool.rearrange("g (cc p) d -> p g cc d", p=128)),
              wr=[WPr], dma=WPr)
        UALL = [Ur[h][i] for h in range(8) for i in range(NT)]

        DT = BIG[:, :].rearrange("p (c t) -> p c t", c=8)
        DTr = VGr
        INVC = C[:, CA["INVC"]:CA["INVC"] + 64]
        PSC = C[:, CA["PSC"]:CA["PSC"] + 8]
        wins = (2, 4, 8, 16)
        for sblk in range(2):
            sl = k.load_w(wab[:, :, 2048 + sblk * 512:2048 + (sblk + 1) * 512], v16)
            for oc in range(4):
                c = sblk * 4 + oc
                g = c // 2
                bm = [k.bank(), k.bank()]
                bh = k.bank()
                for half in range(2):
                    for kc in range(KC):
                        P.add("pe", lambda e, kc=kc, half=half, oc=oc, b=bm[half], sl=sl: e.matmul(
                            k.ps[b][:, :], v16(k.slots[sl])[:, kc, oc * 128:(oc + 1) * 128],
                            H[:, kc, half * 512:(half + 1) * 512], start=(kc == 0), stop=(kc == KC - 1)),
                            rd=[k.slotr[sl], Hr[kc][half]], wr=[k.psr[bm[half]]])
                for kc in range(KC):
                    P.add("pe", lambda e, kc=kc, oc=oc, b=bh, sl=sl: e.matmul(
                        k.ps[b][:, 0:128], v16(k.slots[sl])[:, kc, oc * 128:(oc + 1) * 128],
                        HH[:, kc, :], start=(kc == 0), stop=(kc == KC - 1)),
                        rd=[k.slotr[sl], HHr[kc][0]], wr=[k.psr[bh]])
                for th in range(2):
                    P.add("act", lambda e, th=th, b=bm[th]: e.activation(
                        Z[:, :, 16:144], k.ps[b][:, :].rearrange("p (i t) -> p i t", i=4), AF.Copy),
                        rd=[k.psr[bm[th]]], wr=[Zr])
                    P.add("act", lambda e, b=bh, th=th: e.activation(
                        Z[:, :, 0:16], k.ps[b][:, th * 64:(th + 1) * 64].rearrange("p (i t) -> p i t", i=4), AF.Copy),
                        rd=[k.psr[bh]], wr=[Zr])
                    cur, curr = Z, Zr
                    sh = 1
                    zi = 0
                    while sh < wins[g]:
                        nxt, nxtr = ZS[zi % 2], ZSr[zi % 2]
                        zi += 1
                        P.add("dve", lambda e, cur=cur, nxt=nxt, sh=sh: e.tensor_tensor(
                            nxt[:, :, 2 * sh - 1:144], cur[:, :, 2 * sh - 1:144], cur[:, :, sh - 1:144 - sh], ALU.add),
                            rd=[curr], wr=[nxtr])
                        cur, curr = nxt, nxtr
                        sh *= 2
                    P.add("dve", lambda e, cur=cur, c=c, g=g, th=th: e.scalar_tensor_tensor(
                        DT[:, c, th * 512:(th + 1) * 512].rearrange("p (i t) -> p i t", i=4), cur[:, :, 16:144],
                        1.0 / wins[g], Z[:, :, 16:144], ALU.mult, ALU.subtract),
                        rd=[curr, Zr], wr=DTr[4 * th:4 * th + 4] + [bigr])
                    if th == 0:
                        ti = tmp_rr % 2
                        tmp_rr += 1
                        P.add("dve", lambda e, cur=cur, ti=ti, g=g: e.tensor_tensor(
                            TMP[ti][:, 0:16], cur[:, 0, 16:32], INVC[:, g * 16:(g + 1) * 16], ALU.mult),
                            rd=[curr, k.constr], wr=[TMPr[ti]])
                        P.add("dve", lambda e, ti=ti, c=c: e.tensor_tensor(
                            DT[:, c, 0:16], TMP[ti][:, 0:16], Z[:, 0, 16:32], ALU.subtract),
                            rd=[TMPr[ti], Zr], wr=[DTr[0], bigr])
                if c % 2 == 1:
                    for dc in range(2):
                        for half in range(2):
                            b = k.bank()
                            for cc in range(2):
                                P.add("pe", lambda e, g=g, cc=cc, dc=dc, half=half, b=b: e.matmul(
                                    k.ps[b][:, :], WP[:, g, cc, dc * 128:(dc + 1) * 128],
                                    DT[:, 2 * g + cc, half * 512:(half + 1) * 512], start=(cc == 0), stop=(cc == 1)),
                                    rd=[WPr] + DTr[4 * half:4 * half + 4], wr=[k.psr[b]])
                            P.add("act", lambda e, g=g, dc=dc, half=half, b=b: e.activation(
                                BT[:, 2 * g + dc, half * 512:(half + 1) * 512], k.ps[b][:, :], AF.Copy,
                                scale=PSC[:, 2 * g + dc:2 * g + dc + 1]),
                                rd=[k.psr[b], k.constr], wr=[BTr[2 * g + dc][half]] + (UALL if (g == 0 and dc == 0 and half == 0) else []))
        out_proj(1)
        k.barrier()
        k.slot_n = 3
        k.ACTB = [BIG[:, 0:4096].rearrange("p (a b) -> p a b", a=4), BIG[:, 4096:8192].rearrange("p (a b) -> p a b", a=4)]
        k.ACTBr = P.Rs("ACTB", 2, 4, 2)
        gf0 = C[:, CA["GF0"]:CA["GF0"] + 16]
        k.norm_fm(X, Xr, KC, [(0, 512, 0), (512, 512, 1)], gf0, H, Hr, 0)
        k.ffn(w_gate, w_up, w_down, 0)
        k.barrier()
        k.slot_n = 2
        k.slot_rr = 0
        CSr = P.R("CS")
        P.add("sp", lambda e: e.dma_start(out=S2F[:, 0:2048], in_=cossin[s]), wr=[CSr], dma=CSr)
        outr = latdr
        gm1 = C[:, CA["GM1"]:CA["GM1"] + 16]
        k.norm_fm(X, Xr, KC, [(0, 512, 0), (512, 512, 1)], gm1, H, Hr, 0)
        wckv = w_ckv.rearrange("(kc p) n -> p kc n", p=128)
        sl = k.load_w(wckv, v16)
        CKV = BIG[:, :].bitcast(F32).rearrange("p (a b) -> p a b", a=4)
        CKVr = P.Rs("CKV", 4, 2)
        for oc in range(4):
            for half in range(2):
                b = k.bank()
                for kc in range(KC):
                    P.add("pe", lambda e, kc=kc, half=half, oc=oc, b=b, sl=sl: e.matmul(
                        k.ps[b][:, :], v16(k.slots[sl])[:, kc, oc * 128:(oc + 1) * 128],
                        H[:, kc, half * 512:(half + 1) * 512], start=(kc == 0), stop=(kc == KC - 1)),
                        rd=[k.slotr[sl], Hr[kc][half]], wr=[k.psr[b]])
                P.add("act", lambda e, b=b, oc=oc, half=half: e.activation(
                    CKV[:, oc, half * 512:(half + 1) * 512], k.ps[b][:, :], AF.Copy),
                    rd=[k.psr[b]], wr=[CKVr[oc][half]])
        CKN = U[:, :, :].rearrange("p a b -> p (a b)").bitcast(F32).rearrange("p (a b) -> p a b", a=4)
        CKNr = P.Rs("CKN", 4, 2)
        gckv = C[:, CA["GCKV"]:CA["GCKV"] + 4]
        k.norm_fm(CKV, CKVr, 4, [(0, 512, 0), (512, 512, 1)], gckv, CKN, CKNr, 1)
        lcr = latc_d[:, s * T:(s + 1) * T].rearrange("(kc p) t -> p kc t", p=128)
        P.add("sp", lambda e: e.dma_start(out=lcr, in_=CKN),
              rd=[CKNr[c][h] for c in range(4) for h in range(2)], wr=[outr], dma=outr)
        v128 = lambda s: s[:, 0:2048].rearrange("p (a b) -> p a b", a=16)
        sl = k.load_w(w_kr.rearrange("(kc p) n -> p kc n", p=128), v128)
        P.add("dve", lambda e, sl=sl: e.tensor_scalar(
            v128(k.slots[sl])[:, :, 64:96], v128(k.slots[sl])[:, :, 64:96], -1.0, None, ALU.mult),
            rd=[k.slotr[sl]], wr=[k.slotr[sl]])
        KTMP = k.rstd[0][0:64, :]
        KTMPr = k.rstdr[0]
        COS = S2F[:, 0:T]
        SIN = S2F[:, T:2 * T]
        for half in range(2):
            ba, bb = k.bank(), k.bank()
            for (b, c0) in ((ba, 0), (bb, 64)):
                for kc in range(KC):
                    P.add("pe", lambda e, kc=kc, half=half, b=b, c0=c0, sl=sl: e.matmul(
                        k.ps[b][0:64, :], v128(k.slots[sl])[:, kc, c0:c0 + 64],
                        H[:, kc, half * 512:(half + 1) * 512], start=(kc == 0), stop=(kc == KC - 1)),
                        rd=[k.slotr[sl], Hr[kc][half]], wr=[k.psr[b]])
            P.add("dve", lambda e, ba=ba, half=half: e.tensor_tensor(
                TG[half][0:64, :], k.ps[ba][0:64, :], COS[0:64, half * 512:(half + 1) * 512], ALU.mult),
                rd=[k.psr[ba], CSr], wr=[TGr[half]])
            P.add("dve", lambda e, bb=bb, half=half: e.tensor_tensor(
                KTMP[:, :], k.ps[bb][0:64, :], SIN[0:64, half * 512:(half + 1) * 512], ALU.mult),
                rd=[k.psr[bb], CSr], wr=[KTMPr])
            P.add("dve", lambda e, half=half: e.tensor_tensor(
                TG[half][0:64, :], TG[half][0:64, :], KTMP[:, :], ALU.add),
                rd=[KTMPr, TGr[half]], wr=[TGr[half]])
            P.add("sp", lambda e, half=half: e.dma_start(out=latr_d[:, s * T + half * 512:s * T + (half + 1) * 512], in_=TG[half][0:64, :]),
                  rd=[TGr[half]], wr=[outr], dma=outr)

    for s_ in range(4):
        k.slot_n = 3
        k.slot_rr = 0
        pass_A(s_)
        k.barrier()

    apos[0] = 0
    k.slot_n = 3
    k.slot_rr = 0
    C = carve(2 * CB["N"]).bitcast(F32)
    k.constr = P.R("constB")
    tmp_rr = 0
    CQ = k.slots[2][:, :].bitcast(F32).rearrange("p (a b) -> p a b", a=4)
    CQr = P.Rs("CQ", 4, 2)
    CQN = carve(4096).rearrange("p (a b) -> p a b", a=4)
    CQNr = P.Rs("CQN", 4, 2)
    KRA = carve(4096)[0:64, :]
    KRAr = P.R("KRA")
    MASKB = carve(512)
    MASKBr = P.R("MASKB")
    OT = carve(4096).rearrange("p (a b) -> p a b", a=4)
    OTr = P.Rs("OT", 4, 2)
    S2 = k.slots[2]
    QN = [S2[:, 4096 + 1024 * i:4096 + 1024 * (i + 1)] for i in range(2)]
    QNr = [P.Rs("QN", 2) for _ in range(2)]
    QR = [S2[0:64, 6144 + 1024 * i:6144 + 1024 * (i + 1)] for i in range(2)]
    QRr = [P.Rs("QR", 2) for _ in range(2)]
    VVt = carve(4096).rearrange("p (a b) -> p a b", a=32)
    VV = [VVt, VVt]
    _vvr = P.R("VV")
    VVr = [_vvr, _vvr]
    PT = [carve(512) for i in range(2)]
    PTr = [P.R("PT") for _ in range(2)]
    RCP = [carve(1024).bitcast(F32) for i in range(1)]
    RCPr = [P.R("RCP") for _ in range(1)]
    P.add("sp", lambda e: e.dma_start(out=C[:, :], in_=constsB), wr=[k.constr], dma=k.constr)
    for q in range(4):
        P.add("pool", lambda e, q=q: e.dma_start(out=KRA[:, q * 1024:(q + 1) * 1024], in_=latr_d[:, q * 1024:(q + 1) * 1024]),
              rd=[latdr], wr=[KRAr], dma=KRAr)
    P.add("dve", lambda e: e.tensor_copy(MASKB[:, :], C[:, CB["MASK"]:CB["MASK"] + 512]),
          rd=[k.constr], wr=[MASKBr])
    COS = C[:, CB["COS"]:CB["COS"] + T]
    SIN = C[:, CB["SIN"]:CB["SIN"] + T]

    gm1 = C[:, CB["GM1"]:CB["GM1"] + 16]
    k.norm_fm(X, Xr, KC, [(0, 512, 0), (512, 512, 1)], gm1, H, Hr, 0)
    v16 = lambda s: s[:, :].rearrange("p (a b) -> p a b", a=16)
    v4 = lambda s: s[:, :].rearrange("p (a b) -> p a b", a=4)
    sl = k.load_w(w_cq.rearrange("(kc p) n -> p kc n", p=128), v16)
    for oc in range(4):
        for half in range(2):
            b = k.bank()
            for kc in range(KC):
                P.add("pe", lambda e, kc=kc, half=half, oc=oc, b=b, sl=sl: e.matmul(
                    k.ps[b][:, :], v16(k.slots[sl])[:, kc, oc * 128:(oc + 1) * 128],
                    H[:, kc, half * 512:(half + 1) * 512], start=(kc == 0), stop=(kc == KC - 1)),
                    rd=[k.slotr[sl], Hr[kc][half]], wr=[k.psr[b]])
            P.add("act", lambda e, b=b, oc=oc, half=half: e.activation(
                CQ[:, oc, half * 512:(half + 1) * 512], k.ps[b][:, :], AF.Copy),
                rd=[k.psr[b]], wr=[CQr[oc][half]])
    gcq = C[:, CB["GCQ"]:CB["GCQ"] + 4]
    k.norm_fm(CQ, CQr, 4, [(0, 512, 0), (512, 512, 1)], gcq, CQN, CQNr, 1)
    k.barrier()
    CKA = H[:, :, :].rearrange("p a b -> p (a b)").rearrange("p (c n) -> p c n", c=4)
    CKAr = P.R("CKA")
    ckr = latc_d.rearrange("(c p) n -> p c n", p=128)
    for q in range(4):
        P.add("pool", lambda e, q=q: e.dma_start(out=CKA[:, :, q * 1024:(q + 1) * 1024], in_=ckr[:, :, q * 1024:(q + 1) * 1024]),
              rd=[latdr], wr=[CKAr], dma=CKAr)
    KNt = k.slots[2][:, :].rearrange("p (a n) -> p a n", a=2)
    _knr = P.R("KN")
    KNr = [_knr, _knr]
    va = lambda s: s[:, :].rearrange("p (h kc c) -> p h kc c", h=4, kc=4)
    watt = w_att.rearrange("(kc p) h c -> p h kc c", p=128)
    woc = w_out_c.rearrange("(kc p) n -> p kc n", p=128)
    slot_seq = [0, 1]
    ps_rr = 0
    for hgp in range(4):
        s = slot_seq[hgp % 2]
        for hh in range(4):
            P.add("pool", lambda e, s=s, hh=hh, hgp=hgp: e.dma_start(
                out=va(k.slots[s])[:, hh, :, :], in_=watt[:, hgp * 4 + hh, :, :]),
                wr=[k.slotr[s]], dma=k.slotr[s])
        P.add("dve", lambda e, s=s: e.tensor_scalar(
            va(k.slots[s])[:, :, :, 192:224], va(k.slots[s])[:, :, :, 192:224], -1.0, None, ALU.mult),
            rd=[k.slotr[s]], wr=[k.slotr[s]])
        W = va(k.slots[s])
        Wr = k.slotr[s]
        for hh in range(4):
            h = hgp * 4 + hh
            pb = h % 2
            for half in range(2):
                b = 6 + (ps_rr % 2)
                ps_rr += 1
                for kc in range(4):
                    P.add("pe", lambda e, kc=kc, half=half, b=b, hh=hh, W=W: e.matmul(
                        k.ps[b][:, :], W[:, hh, kc, 0:128], CQN[:, kc, half * 512:(half + 1) * 512],
                        start=(kc == 0), stop=(kc == 3)),
                        rd=[Wr, CQNr[kc][half]], wr=[k.psr[b]])
                P.add("act", lambda e, b=b, half=half, pb=pb: e.activation(
                    QN[pb][:, half * 512:(half + 1) * 512], k.ps[b][:, :], AF.Copy),
                    rd=[k.psr[b]], wr=[QNr[pb][half]])
            for half in range(2):
                ba = 6 + (ps_rr % 2)
                ps_rr += 1
                bb = 6 + (ps_rr % 2)
                ps_rr += 1
                for (b, c0) in ((ba, 128), (bb, 192)):
                    for kc in range(4):
                        P.add("pe", lambda e, kc=kc, half=half, b=b, hh=hh, W=W, c0=c0: e.matmul(
                            k.ps[b][0:64, :], W[:, hh, kc, c0:c0 + 64], CQN[:, kc, half * 512:(half + 1) * 512],
                            start=(kc == 0), stop=(kc == 3)),
                            rd=[Wr, CQNr[kc][half]], wr=[k.psr[b]])
                t0 = tmp_rr % 2
                tmp_rr += 1
                t1 = tmp_rr % 2
                tmp_rr += 1
                P.add("dve", lambda e, ba=ba, half=half, t0=t0: e.tensor_tensor(
                    TMP[t0][0:64, :], k.ps[ba][0:64, :], COS[0:64, half * 512:(half + 1) * 512], ALU.mult),
                    rd=[k.psr[ba], k.constr], wr=[TMPr[t0]])
                P.add("dve", lambda e, bb=bb, half=half, t1=t1: e.tensor_tensor(
                    TMP[t1][0:64, :], k.ps[bb][0:64, :], SIN[0:64, half * 512:(half + 1) * 512], ALU.mult),
                    rd=[k.psr[bb], k.constr], wr=[TMPr[t1]])
                P.add("dve", lambda e, half=half, t0=t0, t1=t1, pb=pb: e.tensor_tensor(
                    QR[pb][:, half * 512:(half + 1) * 512], TMP[t0][0:64, :], TMP[t1][0:64, :], ALU.add),
                    rd=[TMPr[t0], TMPr[t1]], wr=[QRr[pb][half]])
            for kb in range(8):
                b = 6 + (ps_rr % 2)
                ps_rr += 1
                for kc in range(4):
                    P.add("pe", lambda e, kc=kc, kb=kb, b=b, hh=hh, W=W: e.matmul(
                        k.ps[b][:, :], W[:, hh, kc, 256:384], CKA[:, kc, kb * 512:(kb + 1) * 512],
                        start=(kc == 0), stop=(kc == 3)),
                        rd=[Wr, CKAr], wr=[k.psr[b]])
                P.add("act", lambda e, b=b, kb=kb, pb=pb: e.activation(
                    KNt[:, 0, kb * 512:(kb + 1) * 512], k.ps[b][:, :], AF.Copy),
                    rd=[k.psr[b]], wr=[KNr[pb]])
            for jb in range(8):
                b = 6 + (ps_rr % 2)
                ps_rr += 1
                for j4 in range(4):
                    j = jb * 4 + j4
                    for kc in range(4):
                        P.add("pe", lambda e, kc=kc, j=j, j4=j4, b=b, hh=hh, W=W: e.matmul(
                            k.ps[b][:, j4 * 128:(j4 + 1) * 128], CKA[:, kc, j * 128:(j + 1) * 128],
                            W[:, hh, kc, 384:512], start=(kc == 0), stop=(kc == 3)),
                            rd=[Wr, CKAr], wr=[k.psr[b]])
                P.add("dve", lambda e, b=b, jb=jb, pb=pb: e.tensor_copy(
                    VV[pb][:, jb * 4:(jb + 1) * 4, :], k.ps[b][:, :].rearrange("p (j d) -> p j d", j=4)),
                    rd=[k.psr[b]], wr=[VVr[pb]])
            for G in range(2):
                bo = 2 + (h * 2 + G) % 2
                bl = 4 + (h * 2 + G) % 2
                nblk = 16 * G + 16
                def blk(j, G=G):
                    ip, rp = j // 4, j % 4
                    kcol = rp * 1024 + ip * 128
                    imin = max(ip, 4 * G)
                    c0 = (imin - 4 * G) * 128
                    return ip, rp, kcol, c0, 512 - c0, G * 512 + c0

                def emit_S(j, pb=pb, G=G):
                    ip, rp, kcol, c0, n, q0 = blk(j)
                    bs = j % 2
                    P.add("pe", lambda e, bs=bs, pb=pb, kcol=kcol, q0=q0, n=n: e.matmul(
                        k.ps[bs][:, 0:n], KNt[:, 0, kcol:kcol + 128], QN[pb][:, q0:q0 + n],
                        start=True, stop=False),
                        rd=[KNr[pb], QNr[pb][G]], wr=[k.psr[bs]])
                    P.add("pe", lambda e, bs=bs, pb=pb, kcol=kcol, q0=q0, n=n: e.matmul(
                        k.ps[bs][:, 0:n], KRA[:, kcol:kcol + 128], QR[pb][:, q0:q0 + n],
                        start=False, stop=True),
                        rd=[KRAr, QRr[pb][G]], wr=[k.psr[bs]])

                emit_S(0)
                for j in range(nblk):
                    if j + 1 < nblk:
                        emit_S(j + 1)
                    ip, rp, kcol, c0, n, q0 = blk(j)
                    bs = j % 2
                    pi = j % 2
                    P.add("act", lambda e, bs=bs, pi=pi, n=n: e.activation(
                        PT[pi][:, 0:n], k.ps[bs][:, 0:n], AF.Exp, scale=SM_SCALE),
                        rd=[k.psr[bs]], wr=[PTr[pi]])
                    if ip >= 4 * G:
                        P.add("dve", lambda e, pi=pi, rp=rp: e.tensor_tensor(
                            PT[pi][:, 0:128], PT[pi][:, 0:128], MASKB[:, rp * 128:(rp + 1) * 128], ALU.mult),
                            rd=[PTr[pi], MASKBr], wr=[PTr[pi]])
                    P.add("pe", lambda e, bo=bo, pb=pb, j=j, pi=pi, c0=c0, n=n, nblk=nblk: e.matmul(
                        k.ps[bo][:, c0:c0 + n], VV[pb][:, (j % 4) * 8 + j // 4, :], PT[pi][:, 0:n],
                        start=(j == 0), stop=(j == nblk - 1)),
                        rd=[VVr[pb], PTr[pi]], wr=[k.psr[bo]])
                    P.add("pe", lambda e, bl=bl, pi=pi, c0=c0, n=n, j=j, nblk=nblk: e.matmul(
                        k.ps[bl][:, c0:c0 + n], k.ones[:, 2, :], PT[pi][:, 0:n],
                        start=(j == 0), stop=(j == nblk - 1)),
                        rd=[k.onesr, PTr[pi]], wr=[k.psr[bl]])
                ri = 0
                P.add("dve", lambda e, bl=bl, ri=ri: e.reciprocal(RCP[ri][:, :], k.ps[bl][:, :]),
                      rd=[k.psr[bl]], wr=[RCPr[ri]])
                ob = hh
                P.add("dve", lambda e, bo=bo, ri=ri, ob=ob, G=G: e.tensor_tensor(
                    OT[:, ob, G * 512:(G + 1) * 512], k.ps[bo][:, :], RCP[ri][:, :], ALU.mult),
                    rd=[k.psr[bo], RCPr[ri]], wr=[OTr[ob][G]])
        ws = slot_seq[(hgp + 1) % 2]
        P.add("pool", lambda e, ws=ws, hgp=hgp: e.dma_start(
            out=v4(k.slots[ws]), in_=woc[:, hgp * 4:(hgp + 1) * 4, :]),
            wr=[k.slotr[ws]], dma=k.slotr[ws])
        for oc in range(KC):
            for half in range(2):
                b = 6 + (ps_rr % 2)
                ps_rr += 1
                for kc in range(4):
                    ob = kc
                    P.add("pe", lambda e, kc=kc, half=half, oc=oc, b=b, ws=ws, ob=ob: e.matmul(
                        k.ps[b][:, :], v4(k.slots[ws])[:, kc, oc * 128:(oc + 1) * 128],
                        OT[:, ob, half * 512:(half + 1) * 512], start=(kc == 0), stop=(kc == 3)),
                        rd=[k.slotr[ws], OTr[ob][half]], wr=[k.psr[b]])
                P.add("dve", lambda e, b=b, oc=oc, half=half: e.tensor_tensor(
                    X[:, oc, half * 512:(half + 1) * 512], X[:, oc, half * 512:(half + 1) * 512],
                    k.ps[b][:, :], ALU.add),
                    rd=[k.psr[b], Xr[oc][half]], wr=[Xr[oc][half]])
    k.barrier()
    k.slot_rr = 0
    k.ACTB = [OT, VVt.rearrange("p a b -> p (a b)").rearrange("p (a b) -> p a b", a=4)]
    k.ACTBr = P.Rs("ACTB", 2, 4, 2)
    gf1 = C[:, CB["GF1"]:CB["GF1"] + 16]
    k.norm_fm(X, Xr, KC, [(0, 512, 0), (512, 512, 1)], gf1, H, Hr, 0)
    k.ffn(w_gate, w_up, w_down, 1)
    k.barrier()
    gfin = C[:, CB["GFIN"]:CB["GFIN"] + 16]
    YF = H[:, :, :].rearrange("p a b -> p (a b)").bitcast(F32).rearrange("p (a b) -> p a b", a=8)
    YFr = P.Rs("YF", 8, 2)
    yr = yT.rearrange("(kc p) t -> p kc t", p=128)
    outr = P.R("out")
    ones = k.ones
    rst = []
    for (c0, n, pi) in [(0, 512, 0), (512, 512, 1)]:
        b = k.bank()
        for kc in range(KC):
            qi = k.sq_rr % 2
            k.sq_rr += 1
            P.add("act", lambda e, qi=qi, kc=kc, c0=c0, n=n: e.activation(
                k.sq[qi][:, 0:n], X[:, kc, c0:c0 + n], AF.Square),
                rd=[Xr[kc][pi]], wr=[k.sqr[qi]])
            P.add("pe", lambda e, qi=qi, kc=kc, n=n, b=b: e.matmul(
                k.ps[b][:, 0:n], ones[:, 0, :], k.sq[qi][:, 0:n], start=(kc == 0), stop=(kc == KC - 1)),
                rd=[k.sqr[qi], k.onesr], wr=[k.psr[b]])
        P.add("act", lambda e, b=b, pi=pi: e.activation(k.rstd[pi][:, :], k.ps[b][:, :], AF.Sqrt, bias=EPS, scale=1.0),
              rd=[k.psr[b]], wr=[k.rstdr[pi]])
        P.add("dve", lambda e, pi=pi: e.reciprocal(k.rstd[pi][:, :], k.rstd[pi][:, :]),
              rd=[k.rstdr[pi]], wr=[k.rstdr[pi]])
    for part in range(2):
        for kc8 in range(8):
            kc = part * 8 + kc8
            for pi in range(2):
                P.add("dve", lambda e, kc=kc, kc8=kc8, pi=pi: e.scalar_tensor_tensor(
                    YF[:, kc8, pi * 512:(pi + 1) * 512], X[:, kc, pi * 512:(pi + 1) * 512], gfin[:, kc:kc + 1],
                    k.rstd[pi][:, :], ALU.mult, ALU.mult),
                    rd=[Xr[kc][pi], k.rstdr[pi], k.constr], wr=[YFr[kc8][pi]])
        P.add("sp", lambda e, part=part: e.dma_start(out=yr[:, part * 8:(part + 1) * 8, :], in_=YF),
              rd=[YFr[c][h] for c in range(8) for h in range(2)], wr=[outr], dma=outr)
    P.finish("sp", [outr])
    P.emit(nc, k.es)
    k.es.close()
    return nc


def build_G():
    k = KB()
    nc, P = k.nc, k.P
    xT = k.din("xT", [1, D, T])
    xhT = k.din("xhT", [1, D, 128])
    constsA = k.din("constsA", [1, 128, CA["N"]])
    gvbs = k.din("gvbs", [128, 2048])
    cossin = k.din("cossin", [1, 128, 2048])
    constsB = k.din("constsB", [128, CB["N"]])
    w_in_ab = k.din("w_in_ab", [D, 3072])
    w_sT = k.din("w_sT", [8, 128, 128])
    w_pool = k.din("w_pool", [4, 256, 256])
    w_out_ab = k.din("w_out_ab", [D, D])
    w_gate2 = k.din("w_gate", [2, D, DFF])
    w_up2 = k.din("w_up", [2, D, DFF])
    w_down2 = k.din("w_down", [2, DFF, D])
    w_ckv = k.din("w_ckv", [D, 512])
    w_kr = k.din("w_kr", [D, 128])
    w_cq = k.din("w_cq", [D, 512])
    w_att = k.din("w_att", [512, 16, 512])
    w_out_c = k.din("w_out_c", [D, D])
    yT = k.dout("yT", [D, T])
    lat_own = [nc.dram_tensor(f"lat_own{i}", [n, T], BF16, kind="Internal").ap() for i, n in enumerate((512, 64))]
    lat_g = [nc.dram_tensor(f"lat_g{i}", [4 * n, T], BF16, kind="Internal").ap() for i, n in enumerate((512, 64))]
    latr_d = lat_own[1]
    latgr = [P.R("latg") for _ in range(2)]
    latdr = P.R("latd")

    k.common(2)
    X = k.X = k.sb("X", [128, KC, T], F32)
    Xr = k.Xr = P.Rs("X", KC, 2)
    H = k.H = k.sb("H", [128, KC, T], BF16)
    Hr = k.Hr = P.Rs("H", KC, 2)
    TG = [k.sb(f"TG{i}", [128, 512], F32) for i in range(2)]
    TGr = [P.R("TG") for _ in range(2)]
    k.SG = TG
    k.SGr = TGr
    TMP, TMPr = TG, TGr
    ARENA_N = 24320
    ARENA = k.sb("ARENA", [128, ARENA_N], BF16)
    apos = [0]

    def carve(nelem_bf16):
        a = apos[0]
        apos[0] = a + nelem_bf16
        assert apos[0] <= ARENA_N, apos[0]
        return ARENA[:, a:a + nelem_bf16]

    HH = carve(2048).rearrange("p (a b) -> p a b", a=KC)
    HHr = P.Rs("HH", KC, 1)
    C_A = carve(2 * CA["N"]).bitcast(F32)
    constr_A = P.R("const")
    U = carve(8192).rearrange("p (a b) -> p a b", a=8)
    Ur = P.Rs("U", 8, NT)
    BIG = carve(8192)
    bigr = P.R("BIG")
    BT = U
    BTr = P.Rs("BT", 8, 2)
    S2F = k.slots[2][:, :].bitcast(F32)
    GVBS = S2F[:, 0:2048]
    gvbsr = P.R("GVBS")
    Z = S2F[:, 2048:2048 + 576].rearrange("p (i t) -> p i t", i=4)
    Zr = P.R("Z")
    ZS = [S2F[:, 2624 + 576 * i:2624 + 576 * (i + 1)].rearrange("p (i t) -> p i t", i=4) for i in range(2)]
    ZSr = [P.R("ZS") for _ in range(2)]
    WSWP = carve(2048)
    WST = WSWP[:, 0:1024].rearrange("p (h t) -> p h t", h=8)
    WP = WSWP[:, :].rearrange("p (g c d) -> p g c d", g=4, c=2)
    WSTr = P.R("WSWP")
    WPr = WSTr
    SSQ = carve(32).bitcast(F32)
    SSQr = P.Rs("SSQ", 16)
    RV = carve(16).bitcast(F32)
    RVr = P.R("RV")
    w_gate, w_up, w_down = w_gate2, w_up2, w_down2

    def pass_A(s):
        C = C_A
        k.constr = constr_A
        xr = xT[s].rearrange("(kc p) t -> p kc t", p=128)
        Xin = P.Rs("Xin", 4)
        for q in range(4):
            hq, cg = q // 2, q % 2
            P.add("sp", lambda e, hq=hq, cg=cg: e.dma_start(
                out=X[:, 8 * cg:8 * cg + 8, hq * 512:(hq + 1) * 512],
                in_=xr[:, 8 * cg:8 * cg + 8, hq * 512:(hq + 1) * 512]),
                wr=[Xr[kc][hq] for kc in range(8 * cg, 8 * cg + 8)] + [Xin[q]], dma=Xin[q])
        P.add("sp", lambda e: e.dma_start(out=C[:, :], in_=constsA[s]), wr=[k.constr], dma=k.constr)
        XHt = BIG[:, 0:4096].bitcast(F32).rearrange("p (a b) -> p a b", a=KC)
        XHr = [[bigr] for _ in range(KC)]
        XHd = P.R("XHd")
        P.add("sp", lambda e: e.dma_start(out=XHt, in_=xhT[s].rearrange("(kc p) t -> p kc t", p=128)),
              wr=[bigr, XHd], dma=XHd)
        P.add("sp", lambda e: e.dma_start(out=GVBS, in_=gvbs), wr=[gvbsr], dma=gvbsr)
        P.add("pool", lambda e: e.dma_start(out=WST, in_=w_sT.rearrange("h s t -> s h t")),
              wr=[WSTr], dma=WSTr)
        P.add("dve", lambda e: e.memset(WST[64:128, :, 0:64], 0.0), wr=[WSTr])

        gm0 = C[:, CA["GM0"]:CA["GM0"] + 16]
        k.norm_fm(X, Xr, KC, [(0, 512, 0), (512, 512, 1)], gm0, H, Hr, 0)

        v16 = lambda s: s[:, :].rearrange("p (a b) -> p a b", a=16)
        v8 = lambda s: s[:, 0:4096].rearrange("p (a b) -> p a b", a=8)
        wab = w_in_ab.rearrange("(kc p) n -> p kc n", p=128)
        VG = BIG[:, :].rearrange("p (i c) -> p i c", i=NT)
        VGr = P.Rs("VG", NT)
        k.slot_n = 2

        for sblk in range(2):
            sl = k.load_w(wab[:, :, sblk * 512:(sblk + 1) * 512], v16)
            for half in range(2):
                for oc in range(4):
                    hd = sblk * 4 + oc
                    b = k.bank()
                    for kc in range(KC):
                        P.add("pe", lambda e, kc=kc, half=half, oc=oc, b=b, sl=sl: e.matmul(
                            k.ps[b][:, :], v16(k.slots[sl])[:, kc, oc * 128:(oc + 1) * 128],
                            H[:, kc, half * 512:(half + 1) * 512], start=(kc == 0), stop=(kc == KC - 1)),
                            rd=[k.slotr[sl], Hr[kc][half]], wr=[k.psr[b]])
                    P.add("act", lambda e, b=b, hd=hd, half=half: e.activation(
                        U[:, hd, half * 512:(half + 1) * 512], k.ps[b][:, :], AF.Gelu_apprx_tanh),
                        rd=[k.psr[b]], wr=[Ur[hd][i] for i in range(4 * half, 4 * half + 4)])
        k.norm_fm(XHt, XHr, KC, [(0, 128, 0)], gm0, HH, HHr, 0)
        tg_rr = 0
        for sblk in range(2):
            sl = k.load_w(wab[:, :, 1024 + sblk * 512:1024 + (sblk + 1) * 512], v16)
            for i in range(NT):
                b = k.bank()
                for kc in range(KC):
                    P.add("pe", lambda e, kc=kc, i=i, b=b, sl=sl: e.matmul(
                        k.ps[b][:, :], H[:, kc, i * 128:(i + 1) * 128], v16(k.slots[sl])[:, kc, :],
                        start=(kc == 0), stop=(kc == KC - 1)),
                        rd=[k.slotr[sl], Hr[kc][i // 4]], wr=[k.psr[b]])
                ti = tg_rr % 2
                tg_rr += 1
                P.add("act", lambda e, b=b, ti=ti: e.activation(TG[ti][:, :], k.ps[b][:, :], AF.Gelu_apprx_tanh),
                      rd=[k.psr[b]], wr=[TGr[ti]])
                qi = k.sq_rr % 2
                k.sq_rr += 1
                P.add("act", lambda e, ti=ti, i=i, sblk=sblk, qi=qi: e.activation(
                    k.sq[qi][:, :], TG[ti][:, :], AF.Square,
                    accum_out=SSQ[:, 2 * i + sblk:2 * i + sblk + 1]),
                    rd=[TGr[ti]], wr=[k.sqr[qi], SSQr[2 * i + sblk]])
                P.add("dve", lambda e, ti=ti, i=i, sblk=sblk: e.tensor_copy(
                    VG[:, i, sblk * 512:(sblk + 1) * 512], TG[ti][:, :]),
                    rd=[TGr[ti]], wr=[VGr[i], bigr])
        SS3 = SSQ[:, :].rearrange("p (i s) -> p i s", s=2)
        P.add("dve", lambda e: e.tensor_tensor(RV[:, :], SS3[:, :, 0], SS3[:, :, 1], ALU.add),
              rd=SSQr, wr=[RVr])
        P.add("act", lambda e: e.activation(RV[:, :], RV[:, :], AF.Sqrt, bias=EPS, scale=1.0 / 1024),
              rd=[RVr], wr=[RVr])
        P.add("dve", lambda e: e.reciprocal(RV[:, :], RV[:, :]), rd=[RVr], wr=[RVr])
        GV = GVBS[:, 0:1024]
        BS = GVBS[:, 1024:2048]
        for i in range(NT):
            P.add("dve", lambda e, i=i: e.scalar_tensor_tensor(
                VG[:, i, :], VG[:, i, :], RV[:, i:i + 1], GV, ALU.mult, ALU.mult),
                rd=[VGr[i], RVr, gvbsr], wr=[VGr[i]])
        tmp_rr = 0
        for i in range(NT):
            for hg in range(2):
                b = k.bank()
                for h4 in range(4):
                    hd = hg * 4 + h4
                    P.add("pe", lambda e, i=i, hd=hd, h4=h4, b=b: e.matmul(
                        k.ps[b][:, h4 * 128:(h4 + 1) * 128], VG[:, i, hd * 128:(hd + 1) * 128], WST[:, hd, :],
                        start=True, stop=True),
                        rd=[VGr[i], WSTr], wr=[k.psr[b]])
                ti = tmp_rr % 2
                tmp_rr += 1
                P.add("dve", lambda e, b=b, ti=ti, hg=hg: e.tensor_tensor(
                    TMP[ti][:, :], k.ps[b][:, :], BS[:, hg * 512:(hg + 1) * 512], ALU.add),
                    rd=[k.psr[b], gvbsr], wr=[TMPr[ti]])
                P.add("dve", lambda e, ti=ti, hg=hg, i=i: e.tensor_tensor(
                    U[:, hg * 4:(hg + 1) * 4, i * 128:(i + 1) * 128],
                    U[:, hg * 4:(hg + 1) * 4, i * 128:(i + 1) * 128],
                    TMP[ti][:, :].rearrange("p (h t) -> p h t", h=4), ALU.mult),
                    rd=[TMPr[ti]] + [Ur[hg * 4 + h][i] for h in range(4)],
                    wr=[Ur[hg * 4 + h][i] for h in range(4)])

        wo = w_out_ab.rearrange("(kc p) n -> p kc n", p=128)

        def out_proj(part):
            for sblk in range(2):
                sl = k.load_w(wo[:, part * 8:(part + 1) * 8, sblk * 1024:(sblk + 1) * 1024],
                              lambda s: s[:, :].rearrange("p (a b) -> p a b", a=8))
                W8 = k.slots[sl][:, :].rearrange("p (a b) -> p a b", a=8)
                for oc in range(8):
                    ocg = sblk * 8 + oc
                    for half in range(2):
                        b = k.bank()
                        for kc in range(8):
                            if part == 0:
                                rr = [Ur[kc][i] for i in range(4 * half, 4 * half + 4)]
                            else:
                                rr = [BTr[kc][half]]
                            P.add("pe", lambda e, kc=kc, oc=oc, b=b, W8=W8, half=half: e.matmul(
                                k.ps[b][:, :], W8[:, kc, oc * 128:(oc + 1) * 128],
                                U[:, kc, half * 512:(half + 1) * 512],
                                start=(kc == 0), stop=(kc == 7)),
                                rd=[k.slotr[sl]] + rr, wr=[k.psr[b]])
                        P.add("dve", lambda e, b=b, ocg=ocg, half=half: e.tensor_tensor(
                            X[:, ocg, half * 512:(half + 1) * 512], X[:, ocg, half * 512:(half + 1) * 512],
                            k.ps[b][:, :], ALU.add),
                            rd=[k.psr[b], Xr[ocg][half]], wr=[Xr[ocg][half]])

        out_proj(0)
        P.add("pool", lambda e: e.dma_start(out=WP, in_=w_pool.rearrange("g (cc p) d -> p g cc d", p=128)),
              wr=[WPr], dma=WPr)
        UALL = [Ur[h][i] for h in range(8) for i in range(NT)]

        DT = BIG[:, :].rearrange("p (c t) -> p c t", c=8)
        DTr = VGr
        INVC = C[:, CA["INVC"]:CA["INVC"] + 64]
        PSC = C[:, CA["PSC"]:CA["PSC"] + 8]
        wins = (2, 4, 8, 16)
        for sblk in range(2):
            sl = k.load_w(wab[:, :, 2048 + sblk * 512:2048 + (sblk + 1) * 512], v16)
            for oc in range(4):
                c = sblk * 4 + oc
                g = c // 2
                bm = [k.bank(), k.bank()]
                bh = k.bank()
                for half in range(2):
                    for kc in range(KC):
                        P.add("pe", lambda e, kc=kc, half=half, oc=oc, b=bm[half], sl=sl: e.matmul(
                            k.ps[b][:, :], v16(k.slots[sl])[:, kc, oc * 128:(oc + 1) * 128],
                            H[:, kc, half * 512:(half + 1) * 512], start=(kc == 0), stop=(kc == KC - 1)),
                            rd=[k.slotr[sl], Hr[kc][half]], wr=[k.psr[bm[half]]])
                for kc in range(KC):
                    P.add("pe", lambda e, kc=kc, oc=oc, b=bh, sl=sl: e.matmul(
                        k.ps[b][:, 0:128], v16(k.slots[sl])[:, kc, oc * 128:(oc + 1) * 128],
                        HH[:, kc, :], start=(kc == 0), stop=(kc == KC - 1)),
                        rd=[k.slotr[sl], HHr[kc][0]], wr=[k.psr[bh]])
                for th in range(2):
                    P.add("act", lambda e, th=th, b=bm[th]: e.activation(
                        Z[:, :, 16:144], k.ps[b][:, :].rearrange("p (i t) -> p i t", i=4), AF.Copy),
                        rd=[k.psr[bm[th]]], wr=[Zr])
                    P.add("act", lambda e, b=bh, th=th: e.activation(
                        Z[:, :, 0:16], k.ps[b][:, th * 64:(th + 1) * 64].rearrange("p (i t) -> p i t", i=4), AF.Copy),
                        rd=[k.psr[bh]], wr=[Zr])
                    cur, curr = Z, Zr
                    sh = 1
                    zi = 0
                    while sh < wins[g]:
                        nxt, nxtr = ZS[zi % 2], ZSr[zi % 2]
                        zi += 1
                        P.add("dve", lambda e, cur=cur, nxt=nxt, sh=sh: e.tensor_tensor(
                            nxt[:, :, 2 * sh - 1:144], cur[:, :, 2 * sh - 1:144], cur[:, :, sh - 1:144 - sh], ALU.add),
                            rd=[curr], wr=[nxtr])
                        cur, curr = nxt, nxtr
                        sh *= 2
                    P.add("dve", lambda e, cur=cur, c=c, g=g, th=th: e.scalar_tensor_tensor(
                        DT[:, c, th * 512:(th + 1) * 512].rearrange("p (i t) -> p i t", i=4), cur[:, :, 16:144],
                        1.0 / wins[g], Z[:, :, 16:144], ALU.mult, ALU.subtract),
                        rd=[curr, Zr], wr=DTr[4 * th:4 * th + 4] + [bigr])
                    if th == 0:
                        ti = tmp_rr % 2
                        tmp_rr += 1
                        P.add("dve", lambda e, cur=cur, ti=ti, g=g: e.tensor_tensor(
                            TMP[ti][:, 0:16], cur[:, 0, 16:32], INVC[:, g * 16:(g + 1) * 16], ALU.mult),
                            rd=[curr, k.constr], wr=[TMPr[ti]])
                        P.add("dve", lambda e, ti=ti, c=c: e.tensor_tensor(
                            DT[:, c, 0:16], TMP[ti][:, 0:16], Z[:, 0, 16:32], ALU.subtract),
                            rd=[TMPr[ti], Zr], wr=[DTr[0], bigr])
                if c % 2 == 1:
                    for dc in range(2):
                        for half in range(2):
                            b = k.bank()
                            for cc in range(2):
                                P.add("pe", lambda e, g=g, cc=cc, dc=dc, half=half, b=b: e.matmul(
                                    k.ps[b][:, :], WP[:, g, cc, dc * 128:(dc + 1) * 128],
                                    DT[:, 2 * g + cc, half * 512:(half + 1) * 512], start=(cc == 0), stop=(cc == 1)),
                                    rd=[WPr] + DTr[4 * half:4 * half + 4], wr=[k.psr[b]])
                            P.add("act", lambda e, g=g, dc=dc, half=half, b=b: e.activation(
                                BT[:, 2 * g + dc, half * 512:(half + 1) * 512], k.ps[b][:, :], AF.Copy,
                                scale=PSC[:, 2 * g + dc:2 * g + dc + 1]),
                                rd=[k.psr[b], k.constr], wr=[BTr[2 * g + dc][half]] + (UALL if (g == 0 and dc == 0 and half == 0) else []))
        out_proj(1)
        k.barrier()
        k.slot_n = 3
        k.ACTB = [BIG[:, 0:4096].rearrange("p (a b) -> p a b", a=4), BIG[:, 4096:8192].rearrange("p (a b) -> p a b", a=4)]
        k.ACTBr = P.Rs("ACTB", 2, 4, 2)
        gf0 = C[:, CA["GF0"]:CA["GF0"] + 16]
        k.norm_fm(X, Xr, KC, [(0, 512, 0), (512, 512, 1)], gf0, H, Hr, 0)
        k.ffn(w_gate, w_up, w_down, 0)
        k.barrier()
        k.slot_n = 2
        k.slot_rr = 0
        CSr = P.R("CS")
        P.add("sp", lambda e: e.dma_start(out=S2F[:, 0:2048], in_=cossin[s]), wr=[CSr], dma=CSr)
        outr = latdr
        gm1 = C[:, CA["GM1"]:CA["GM1"] + 16]
        k.norm_fm(X, Xr, KC, [(0, 512, 0), (512, 512, 1)], gm1, H, Hr, 0)

        def gather(pc):
            P.add("pool", lambda e, pc=pc: e.collective_compute(
                "AllGather", ALU.bypass, replica_groups=[[0, 1, 2, 3], [4, 5, 6, 7]],
                ins=[lat_own[pc]], outs=[lat_g[pc]]),
                rd=[latdr], wr=[latgr[pc]], dma=latgr[pc], inc=1)
            P.alltok.pop(("d", latgr[pc]), None)

        wckv = w_ckv.rearrange("(kc p) n -> p kc n", p=128)
        sl = k.load_w(wckv, v16)
        v128 = lambda s: s[:, 0:2048].rearrange("p (a b) -> p a b", a=16)
        sl_kr = k.load_w(w_kr.rearrange("(kc p) n -> p kc n", p=128), v128)
        CKV = BIG[:, :].bitcast(F32).rearrange("p (a b) -> p a b", a=4)
        CKVr = P.Rs("CKV", 4, 2)
        for oc in range(4):
            for half in range(2):
                b = k.bank()
                for kc in range(KC):
                    P.add("pe", lambda e, kc=kc, half=half, oc=oc, b=b, sl=sl: e.matmul(
                        k.ps[b][:, :], v16(k.slots[sl])[:, kc, oc * 128:(oc + 1) * 128],
                        H[:, kc, half * 512:(half + 1) * 512], start=(kc == 0), stop=(kc == KC - 1)),
                        rd=[k.slotr[sl], Hr[kc][half]], wr=[k.psr[b]])
                P.add("act", lambda e, b=b, oc=oc, half=half: e.activation(
                    CKV[:, oc, half * 512:(half + 1) * 512], k.ps[b][:, :], AF.Copy),
                    rd=[k.psr[b]], wr=[CKVr[oc][half]])
        CKN = U[:, 0:4, :]
        CKNr = P.Rs("CKN", 4, 2)
        gckv = C[:, CA["GCKV"]:CA["GCKV"] + 4]
        k.norm_fm(CKV, CKVr, 4, [(0, 512, 0), (512, 512, 1)], gckv, CKN, CKNr, 1)
        P.add("sp", lambda e: e.dma_start(out=lat_own[0].rearrange("(kc p) t -> p kc t", p=128), in_=CKN),
              rd=[CKNr[c][h] for c in range(4) for h in range(2)], wr=[outr], dma=outr)
        gather(0)
        KRB = U[0:64, 4, :]
        KRBr = P.R("KRB")
        sl = sl_kr
        P.add("dve", lambda e, sl=sl: e.tensor_scalar(
            v128(k.slots[sl])[:, :, 64:96], v128(k.slots[sl])[:, :, 64:96], -1.0, None, ALU.mult),
            rd=[k.slotr[sl]], wr=[k.slotr[sl]])
        KTMP = k.rstd[0][0:64, :]
        KTMPr = k.rstdr[0]
        COS = S2F[:, 0:T]
        SIN = S2F[:, T:2 * T]
        for half in range(2):
            ba, bb = k.bank(), k.bank()
            for (b, c0) in ((ba, 0), (bb, 64)):
                for kc in range(KC):
                    P.add("pe", lambda e, kc=kc, half=half, b=b, c0=c0, sl=sl: e.matmul(
                        k.ps[b][0:64, :], v128(k.slots[sl])[:, kc, c0:c0 + 64],
                        H[:, kc, half * 512:(half + 1) * 512], start=(kc == 0), stop=(kc == KC - 1)),
                        rd=[k.slotr[sl], Hr[kc][half]], wr=[k.psr[b]])
            P.add("dve", lambda e, ba=ba, half=half: e.tensor_tensor(
                TG[half][0:64, :], k.ps[ba][0:64, :], COS[0:64, half * 512:(half + 1) * 512], ALU.mult),
                rd=[k.psr[ba], CSr], wr=[TGr[half]])
            P.add("dve", lambda e, bb=bb, half=half: e.tensor_tensor(
                KTMP[:, :], k.ps[bb][0:64, :], SIN[0:64, half * 512:(half + 1) * 512], ALU.mult),
                rd=[k.psr[bb], CSr], wr=[KTMPr])
            P.add("dve", lambda e, half=half: e.tensor_tensor(
                KRB[:, half * 512:(half + 1) * 512], TG[half][0:64, :], KTMP[:, :], ALU.add),
                rd=[KTMPr, TGr[half]], wr=[KRBr])
            P.add("sp", lambda e, half=half: e.dma_start(out=latr_d[:, half * 512:(half + 1) * 512],
                                                         in_=KRB[:, half * 512:(half + 1) * 512]),
                  rd=[KRBr], wr=[outr], dma=outr)
        gather(1)

    for s_ in range(1):
        k.slot_n = 3
        k.slot_rr = 0
        pass_A(s_)
        k.barrier()

    apos[0] = 0
    k.slot_n = 3
    k.slot_rr = 0
    C = carve(2 * CB["N"]).bitcast(F32)
    k.constr = P.R("constB")
    tmp_rr = 0
    CQ = k.slots[2][:, :].bitcast(F32).rearrange("p (a b) -> p a b", a=4)
    CQr = P.Rs("CQ", 4, 2)
    CQN = carve(4096).rearrange("p (a b) -> p a b", a=4)
    CQNr = P.Rs("CQN", 4, 2)
    KRA = carve(4096)[0:64, :]
    KRAr = P.R("KRA")
    MASKB = carve(512)
    MASKBr = P.R("MASKB")
    OT = carve(4096).rearrange("p (a b) -> p a b", a=4)
    OTr = P.Rs("OT", 4, 2)
    S2 = k.slots[2]
    QN = [S2[:, 4096 + 1024 * i:4096 + 1024 * (i + 1)] for i in range(2)]
    QNr = [P.Rs("QN", 2) for _ in range(2)]
    QR = [S2[0:64, 6144 + 1024 * i:6144 + 1024 * (i + 1)] for i in range(2)]
    QRr = [P.Rs("QR", 2) for _ in range(2)]
    VVt = carve(4096).rearrange("p (a b) -> p a b", a=32)
    VV = [VVt, VVt]
    _vvr = P.R("VV")
    VVr = [_vvr, _vvr]
    PT = [carve(512) for i in range(2)]
    PT.append(C[:, CB["MASK"]:CB["MASK"] + 256].bitcast(BF16))
    PT.append(C[:, CB["MASK"] + 256:CB["MASK"] + 512].bitcast(BF16))
    PTr = [P.R("PT") for _ in range(4)]
    RCP = [carve(1024).bitcast(F32) for i in range(1)]
    RCPr = [P.R("RCP") for _ in range(1)]
    P.add("sp", lambda e: e.dma_start(out=C[:, :], in_=constsB), wr=[k.constr], dma=k.constr)
    P.add("dve", lambda e: e.tensor_copy(MASKB[:, :], C[:, CB["MASK"]:CB["MASK"] + 512]),
          rd=[k.constr], wr=[MASKBr])
    COS = C[:, CB["COS"]:CB["COS"] + T]
    SIN = C[:, CB["SIN"]:CB["SIN"] + T]

    v16 = lambda s: s[:, :].rearrange("p (a b) -> p a b", a=16)
    v4 = lambda s: s[:, :].rearrange("p (a b) -> p a b", a=4)
    sl = k.load_w(w_cq.rearrange("(kc p) n -> p kc n", p=128), v16)
    va = lambda s: s[:, :].rearrange("p (h kc c) -> p h kc c", h=4, kc=4)
    watt = w_att.rearrange("(kc p) h c -> p h kc c", p=128)
    slot_seq = [1, 0]

    def load_att(hgp, dma=True, neg=True):
        s = slot_seq[hgp % 2]
        for hh in range(4):
            if not dma:
                break
            P.add("pool", lambda e, s=s, hh=hh, hgp=hgp: e.dma_start(
                out=va(k.slots[s])[:, hh, :, :], in_=watt[:, hgp * 4 + hh, :, :]),
                wr=[k.slotr[s]], dma=k.slotr[s])
        if not neg:
            return
        P.add("dve", lambda e, s=s: e.tensor_scalar(
            va(k.slots[s])[:, :, :, 192:224], va(k.slots[s])[:, :, :, 192:224], -1.0, None, ALU.mult),
            rd=[k.slotr[s]], wr=[k.slotr[s]])

    load_att(0, neg=False)
    for oc in range(4):
        for half in range(2):
            b = k.bank()
            for kc in range(KC):
                P.add("pe", lambda e, kc=kc, half=half, oc=oc, b=b, sl=sl: e.matmul(
                    k.ps[b][:, :], v16(k.slots[sl])[:, kc, oc * 128:(oc + 1) * 128],
                    H[:, kc, half * 512:(half + 1) * 512], start=(kc == 0), stop=(kc == KC - 1)),
                    rd=[k.slotr[sl], Hr[kc][half]], wr=[k.psr[b]])
            P.add("act", lambda e, b=b, oc=oc, half=half: e.activation(
                CQ[:, oc, half * 512:(half + 1) * 512], k.ps[b][:, :], AF.Copy),
                rd=[k.psr[b]], wr=[CQr[oc][half]])
    gcq = C[:, CB["GCQ"]:CB["GCQ"] + 4]
    k.norm_fm(CQ, CQr, 4, [(0, 512, 0), (512, 512, 1)], gcq, CQN, CQNr, 1)
    k.barrier()
    CKA = H[:, :, :].rearrange("p a b -> p (a b)").rearrange("p (c n) -> p c n", c=4)
    CKAr = [P.R("CKA") for _ in range(4)]
    for q in range(4):
        P.add("sp", lambda e, q=q: e.dma_start(
            out=KRA[:, q * 1024:(q + 1) * 1024], in_=lat_g[1][q * 64:(q + 1) * 64, :]),
            rd=[latgr[1]], wr=[KRAr], dma=KRAr)
    for q in range(4):
        P.add("sp", lambda e, q=q: e.dma_start(
            out=CKA[:, :, q * 1024:(q + 1) * 1024],
            in_=lat_g[0][q * 512:(q + 1) * 512, :].rearrange("(c p) t -> p c t", p=128)),
            rd=[latgr[0]], wr=[CKAr[q]], dma=CKAr[q])
    KNt = k.slots[2][:, :].rearrange("p (a n) -> p a n", a=2)
    _knr = P.R("KN")
    KNr = [_knr, _knr]
    va = lambda s: s[:, :].rearrange("p (h kc c) -> p h kc c", h=4, kc=4)
    watt = w_att.rearrange("(kc p) h c -> p h kc c", p=128)
    woc = w_out_c.rearrange("(kc p) n -> p kc n", p=128)
    slot_seq = [1, 0]
    ps_rr = 0
    for hgp in range(4):
        s = slot_seq[hgp % 2]
        load_att(hgp, dma=False)
        W = va(k.slots[s])
        Wr = k.slotr[s]
        for hh in range(4):
            h = hgp * 4 + hh
            pb = h % 2
            for half in range(2):
                b = (6, 7, 0, 1)[ps_rr % 4]
                ps_rr += 1
                for kc in range(4):
                    P.add("pe", lambda e, kc=kc, half=half, b=b, hh=hh, W=W: e.matmul(
                        k.ps[b][:, :], W[:, hh, kc, 0:128], CQN[:, kc, half * 512:(half + 1) * 512],
                        start=(kc == 0), stop=(kc == 3)),
                        rd=[Wr, CQNr[kc][half]], wr=[k.psr[b]])
                P.add("act", lambda e, b=b, half=half, pb=pb: e.activation(
                    QN[pb][:, half * 512:(half + 1) * 512], k.ps[b][:, :], AF.Copy),
                    rd=[k.psr[b]], wr=[QNr[pb][half]])
            for half in range(2):
                ba = (6, 7, 0, 1)[ps_rr % 4]
                ps_rr += 1
                bb = (6, 7, 0, 1)[ps_rr % 4]
                ps_rr += 1
                for (b, c0) in ((ba, 128), (bb, 192)):
                    for kc in range(4):
                        P.add("pe", lambda e, kc=kc, half=half, b=b, hh=hh, W=W, c0=c0: e.matmul(
                            k.ps[b][0:64, :], W[:, hh, kc, c0:c0 + 64], CQN[:, kc, half * 512:(half + 1) * 512],
                            start=(kc == 0), stop=(kc == 3)),
                            rd=[Wr, CQNr[kc][half]], wr=[k.psr[b]])
                t0 = tmp_rr % 2
                tmp_rr += 1
                t1 = tmp_rr % 2
                tmp_rr += 1
                P.add("dve", lambda e, ba=ba, half=half, t0=t0: e.tensor_tensor(
                    TMP[t0][0:64, :], k.ps[ba][0:64, :], COS[0:64, half * 512:(half + 1) * 512], ALU.mult),
                    rd=[k.psr[ba], k.constr], wr=[TMPr[t0]])
                P.add("dve", lambda e, bb=bb, half=half, t1=t1: e.tensor_tensor(
                    TMP[t1][0:64, :], k.ps[bb][0:64, :], SIN[0:64, half * 512:(half + 1) * 512], ALU.mult),
                    rd=[k.psr[bb], k.constr], wr=[TMPr[t1]])
                P.add("dve", lambda e, half=half, t0=t0, t1=t1, pb=pb: e.tensor_tensor(
                    QR[pb][:, half * 512:(half + 1) * 512], TMP[t0][0:64, :], TMP[t1][0:64, :], ALU.add),
                    rd=[TMPr[t0], TMPr[t1]], wr=[QRr[pb][half]])
            for kb in range(8):
                b = (6, 7, 0, 1)[ps_rr % 4]
                ps_rr += 1
                for kc in range(4):
                    P.add("pe", lambda e, kc=kc, kb=kb, b=b, hh=hh, W=W: e.matmul(
                        k.ps[b][:, :], W[:, hh, kc, 256:384], CKA[:, kc, kb * 512:(kb + 1) * 512],
                        start=(kc == 0), stop=(kc == 3)),
                        rd=[Wr, CKAr[kb // 2]], wr=[k.psr[b]])
                P.add("act", lambda e, b=b, kb=kb, pb=pb: e.activation(
                    KNt[:, 0, kb * 512:(kb + 1) * 512], k.ps[b][:, :], AF.Copy),
                    rd=[k.psr[b]], wr=[KNr[pb]])
            for jb in range(8):
                b = (6, 7, 0, 1)[ps_rr % 4]
                ps_rr += 1
                for j4 in range(4):
                    j = jb * 4 + j4
                    for kc in range(4):
                        P.add("pe", lambda e, kc=kc, j=j, j4=j4, b=b, hh=hh, W=W: e.matmul(
                            k.ps[b][:, j4 * 128:(j4 + 1) * 128], CKA[:, kc, j * 128:(j + 1) * 128],
                            W[:, hh, kc, 384:512], start=(kc == 0), stop=(kc == 3)),
                            rd=[Wr, CKAr[j // 8]], wr=[k.psr[b]])
                P.add("dve", lambda e, b=b, jb=jb, pb=pb: e.tensor_copy(
                    VV[pb][:, jb * 4:(jb + 1) * 4, :], k.ps[b][:, :].rearrange("p (j d) -> p j d", j=4)),
                    rd=[k.psr[b]], wr=[VVr[pb]])
            if hh == 0 and hgp + 1 < 4:
                load_att(hgp + 1, neg=False)
            if hh == 3:
                P.add("pool", lambda e, s=s, hgp=hgp: e.dma_start(
                    out=v4(k.slots[s]), in_=woc[:, hgp * 4:(hgp + 1) * 4, :]),
                    wr=[k.slotr[s]], dma=k.slotr[s])
            for G in range(2):
                bo = 2 + (h * 2 + G) % 2
                bl = 4 + (h * 2 + G) % 2
                nblk = 16 * G + 16
                def blk(j, G=G):
                    ip, rp = j // 4, j % 4
                    kcol = rp * 1024 + ip * 128
                    imin = max(ip, 4 * G)
                    c0 = (imin - 4 * G) * 128
                    return ip, rp, kcol, c0, 512 - c0, G * 512 + c0

                def emit_S(j, pb=pb, G=G):
                    ip, rp, kcol, c0, n, q0 = blk(j)
                    bs = (0, 1, 6, 7)[j % 4]
                    P.add("pe", lambda e, bs=bs, pb=pb, kcol=kcol, q0=q0, n=n: e.matmul(
                        k.ps[bs][:, 0:n], KNt[:, 0, kcol:kcol + 128], QN[pb][:, q0:q0 + n],
                        start=True, stop=False),
                        rd=[KNr[pb], QNr[pb][G]], wr=[k.psr[bs]])
                    P.add("pe", lambda e, bs=bs, pb=pb, kcol=kcol, q0=q0, n=n: e.matmul(
                        k.ps[bs][:, 0:n], KRA[:, kcol:kcol + 128], QR[pb][:, q0:q0 + n],
                        start=False, stop=True),
                        rd=[KRAr, QRr[pb][G]], wr=[k.psr[bs]])

                emit_S(0)
                emit_S(1)
                emit_S(2)
                for j in range(nblk):
                    if j + 3 < nblk:
                        emit_S(j + 3)
                    ip, rp, kcol, c0, n, q0 = blk(j)
                    bs = (0, 1, 6, 7)[j % 4]
                    pi = j % 4
                    P.add("act", lambda e, bs=bs, pi=pi, n=n: e.activation(
                        PT[pi][:, 0:n], k.ps[bs][:, 0:n], AF.Exp, scale=SM_SCALE),
                        rd=[k.psr[bs]], wr=[PTr[pi]])
                    if ip >= 4 * G:
                        P.add("dve", lambda e, pi=pi, rp=rp: e.tensor_tensor(
                            PT[pi][:, 0:128], PT[pi][:, 0:128], MASKB[:, rp * 128:(rp + 1) * 128], ALU.mult),
                            rd=[PTr[pi], MASKBr], wr=[PTr[pi]])
                    P.add("pe", lambda e, bo=bo, pb=pb, j=j, pi=pi, c0=c0, n=n, nblk=nblk: e.matmul(
                        k.ps[bo][:, c0:c0 + n], VV[pb][:, (j % 4) * 8 + j // 4, :], PT[pi][:, 0:n],
                        start=(j == 0), stop=(j == nblk - 1)),
                        rd=[VVr[pb], PTr[pi]], wr=[k.psr[bo]])
                    P.add("pe", lambda e, bl=bl, pi=pi, c0=c0, n=n, j=j, nblk=nblk: e.matmul(
                        k.ps[bl][:, c0:c0 + n], k.ones[:, 2, :], PT[pi][:, 0:n],
                        start=(j == 0), stop=(j == nblk - 1)),
                        rd=[k.onesr, PTr[pi]], wr=[k.psr[bl]])
                ri = 0
                P.add("dve", lambda e, bl=bl, ri=ri: e.reciprocal(RCP[ri][:, :], k.ps[bl][:, :]),
                      rd=[k.psr[bl]], wr=[RCPr[ri]])
                ob = hh
                P.add("dve", lambda e, bo=bo, ri=ri, ob=ob, G=G: e.tensor_tensor(
                    OT[:, ob, G * 512:(G + 1) * 512], k.ps[bo][:, :], RCP[ri][:, :], ALU.mult),
                    rd=[k.psr[bo], RCPr[ri]], wr=[OTr[ob][G]])
        ws = s
        for oc in range(KC):
            for half in range(2):
                b = (6, 7, 0, 1)[ps_rr % 4]
                ps_rr += 1
                for kc in range(4):
                    ob = kc
                    P.add("pe", lambda e, kc=kc, half=half, oc=oc, b=b, ws=ws, ob=ob: e.matmul(
                        k.ps[b][:, :], v4(k.slots[ws])[:, kc, oc * 128:(oc + 1) * 128],
                        OT[:, ob, half * 512:(half + 1) * 512], start=(kc == 0), stop=(kc == 3)),
                        rd=[k.slotr[ws], OTr[ob][half]], wr=[k.psr[b]])
                P.add("dve", lambda e, b=b, oc=oc, half=half: e.tensor_tensor(
                    X[:, oc, half * 512:(half + 1) * 512], X[:, oc, half * 512:(half + 1) * 512],
                    k.ps[b][:, :], ALU.add),
                    rd=[k.psr[b], Xr[oc][half]], wr=[Xr[oc][half]])
    k.barrier()
    k.slot_rr = 0
    k.ACTB = [OT, VVt.rearrange("p a b -> p (a b)").rearrange("p (a b) -> p a b", a=4)]
    k.ACTBr = P.Rs("ACTB", 2, 4, 2)
    gf1 = C[:, CB["GF1"]:CB["GF1"] + 16]
    k.norm_fm(X, Xr, KC, [(0, 512, 0), (512, 512, 1)], gf1, H, Hr, 0)
    k.ffn(w_gate, w_up, w_down, 1)
    k.barrier()
    gfin = C[:, CB["GFIN"]:CB["GFIN"] + 16]
    YF = H[:, :, :].rearrange("p a b -> p (a b)").bitcast(F32).rearrange("p (a b) -> p a b", a=8)
    YFr = P.Rs("YF", 8, 2)
    yr = yT.rearrange("(kc p) t -> p kc t", p=128)
    outr = P.R("out")
    ones = k.ones
    rst = []
    for (c0, n, pi) in [(0, 512, 0), (512, 512, 1)]:
        b = k.bank()
        for kc in range(KC):
            qi = k.sq_rr % 2
            k.sq_rr += 1
            P.add("act", lambda e, qi=qi, kc=kc, c0=c0, n=n: e.activation(
                k.sq[qi][:, 0:n], X[:, kc, c0:c0 + n], AF.Square),
                rd=[Xr[kc][pi]], wr=[k.sqr[qi]])
            P.add("pe", lambda e, qi=qi, kc=kc, n=n, b=b: e.matmul(
                k.ps[b][:, 0:n], ones[:, 0, :], k.sq[qi][:, 0:n], start=(kc == 0), stop=(kc == KC - 1)),
                rd=[k.sqr[qi], k.onesr], wr=[k.psr[b]])
        P.add("act", lambda e, b=b, pi=pi: e.activation(k.rstd[pi][:, :], k.ps[b][:, :], AF.Sqrt, bias=EPS, scale=1.0),
              rd=[k.psr[b]], wr=[k.rstdr[pi]])
        P.add("dve", lambda e, pi=pi: e.reciprocal(k.rstd[pi][:, :], k.rstd[pi][:, :]),
              rd=[k.rstdr[pi]], wr=[k.rstdr[pi]])
    for pi in range(2):
        for cg in range(2):
            for kc in range(8 * cg, 8 * cg + 8):
                P.add("dve", lambda e, kc=kc, pi=pi: e.scalar_tensor_tensor(
                    X[:, kc, pi * 512:(pi + 1) * 512], X[:, kc, pi * 512:(pi + 1) * 512], gfin[:, kc:kc + 1],
                    k.rstd[pi][:, :], ALU.mult, ALU.mult),
                    rd=[Xr[kc][pi], k.rstdr[pi], k.constr], wr=[Xr[kc][pi]])
            P.add("sp", lambda e, pi=pi, cg=cg: e.dma_start(
                out=yr[:, 8 * cg:8 * cg + 8, pi * 512:(pi + 1) * 512],
                in_=X[:, 8 * cg:8 * cg + 8, pi * 512:(pi + 1) * 512]),
                rd=[Xr[kc][pi] for kc in range(8 * cg, 8 * cg + 8)], wr=[outr], dma=outr)
    P.finish("sp", [outr])
    P.emit(nc, k.es)
    k.es.close()
    return nc


def _gcols(g):
    return np.ascontiguousarray(np.asarray(g, np.float32).reshape(-1, 128).T)


def _core_tiles(c):
    b, r = c // 4, c % 4
    return b, r, [4 * i + r for i in range(NT)]


def _rope_tables(c, rank=None):
    b, r, tiles = _core_tiles(c)
    if rank is not None:
        tiles = [4 * i + rank for i in range(NT)]
    pos = np.concatenate([np.arange(g * 128, (g + 1) * 128) for g in tiles]).astype(np.float32)
    inv_freq = (10000.0 ** (-np.arange(0, 64, 2, dtype=np.float32) / 64)).astype(np.float32)
    ang = pos[:, None] * inv_freq[None, :]
    cos = np.cos(ang).astype(np.float32).T
    sin = np.sin(ang).astype(np.float32).T
    return np.tile(cos, (4, 1)), np.tile(sin, (4, 1))


def _consts_A(c, inp, rank=None):
    b, r, tiles = _core_tiles(c)
    if rank is not None:
        r = rank
    C = np.zeros((128, CA["N"]), np.float32)
    C[:, CA["GM0"]:CA["GM0"] + 16] = _gcols(inp["g_mix"][0])
    C[:, CA["GF0"]:CA["GF0"] + 16] = _gcols(inp["g_ffn"][0])
    C[:, CA["GM1"]:CA["GM1"] + 16] = _gcols(inp["g_mix"][1])
    C[:, CA["PSC"]:CA["PSC"] + 8] = _gcols(inp["pool_scale"][0])
    C[:, CA["GCKV"]:CA["GCKV"] + 4] = _gcols(inp["g_ckv"][0])
    for g, win in enumerate((2, 4, 8, 16)):
        if r == 0:
            cnt = np.minimum(np.arange(1, 17), win).astype(np.float32)
        else:
            cnt = np.full(16, win, np.float32)
        C[:, CA["INVC"] + g * 16:CA["INVC"] + (g + 1) * 16] = (1.0 / cnt)[None, :]
    return C


def _consts_B(c, inp, order=(0, 1, 2, 3)):
    b, r, tiles = _core_tiles(c)
    C = np.zeros((128, CB["N"]), np.float32)
    C[:, CB["GM1"]:CB["GM1"] + 16] = _gcols(inp["g_mix"][1])
    C[:, CB["GF1"]:CB["GF1"] + 16] = _gcols(inp["g_ffn"][1])
    C[:, CB["GFIN"]:CB["GFIN"] + 16] = _gcols(inp["g_final"])
    C[:, CB["GCQ"]:CB["GCQ"] + 4] = _gcols(inp["g_cq"][0])
    cos, sin = _rope_tables(c)
    C[:, CB["COS"]:CB["COS"] + T] = cos
    C[:, CB["SIN"]:CB["SIN"] + T] = sin
    diag = np.ones((128, 128), np.float32)
    diag[64:, :64] = 0.0
    for p_, rp in enumerate(order):
        m = np.ones((128, 128), np.float32) if rp < r else (diag if rp == r else np.zeros((128, 128), np.float32))
        C[:, CB["MASK"] + p_ * 128:CB["MASK"] + (p_ + 1) * 128] = m
    return C


_CACHE = {}


def _get(name, fn):
    if name not in _CACHE:
        _CACHE[name] = fn()
    return _CACHE[name]


def _prep(inp):
    inp = {k_: np.asarray(v) for k_, v in inp.items()}
    return inp


def maps_A(inp, cores):
    x = inp["x"].astype(np.float32, copy=False)
    w_in_c = inp["w_in_c"][0]
    kr = w_in_c[:, 1024:1088]
    w_kr = np.ascontiguousarray(np.concatenate([kr, kr[:, 32:64], kr[:, 0:32]], axis=1))
    shared_A = dict(
        w_in_ab=np.ascontiguousarray(inp["w_in_ab"][0]),
        w_sT=np.ascontiguousarray(np.transpose(inp["w_s"][0], (0, 2, 1))),
        w_pool=np.ascontiguousarray(inp["w_pool"][0]),
        w_out_ab=np.ascontiguousarray(inp["w_out_ab"][0]),
        w_gate=np.ascontiguousarray(inp["w_gate"][0:1]),
        w_up=np.ascontiguousarray(inp["w_up"][0:1]),
        w_down=np.ascontiguousarray(inp["w_down"][0:1]),
        w_ckv=np.ascontiguousarray(w_in_c[:, 512:1024]),
        w_kr=w_kr,
    )
    gvbs = np.ascontiguousarray(np.broadcast_to(np.concatenate(
        [np.asarray(inp["g_v"][0], np.float32), np.asarray(inp["b_s"][0], np.float32).reshape(1024)])[None, :],
        (128, 2048)))
    maps = []
    for c in cores:
        b, r, tiles = _core_tiles(c)
        xs = np.concatenate([x[b, g * 128:(g + 1) * 128, :] for g in tiles], axis=0)
        halo = np.zeros((128, D), np.float32)
        for i, g in enumerate(tiles):
            if g > 0:
                halo[i * 16:(i + 1) * 16] = x[b, g * 128 - 16:g * 128, :]
        m = dict(xT=np.ascontiguousarray(xs.T), xhT=np.ascontiguousarray(halo.T), consts=_consts_A(c, inp),
                 gvbs=gvbs, cossin=np.ascontiguousarray(np.concatenate(_rope_tables(c), axis=1)))
        m.update(shared_A)
        maps.append(m)
    return maps


def maps_B(inp, cores, lat):
    w_in_c = inp["w_in_c"][0]
    w_uq = inp["w_uq"][0].reshape(512, 16, 192)
    w_ukv = inp["w_ukv"][0].reshape(512, 16, 256)
    w_att = np.ascontiguousarray(np.concatenate(
        [w_uq[:, :, 0:128], w_uq[:, :, 128:192], w_uq[:, :, 160:192], w_uq[:, :, 128:160], w_ukv], axis=2))
    shared_B = dict(
        w_cq=np.ascontiguousarray(w_in_c[:, 0:512]),
        w_att=w_att,
        w_out_c=np.ascontiguousarray(inp["w_out_c"][0]),
        w_gate=np.ascontiguousarray(inp["w_gate"][1:2]),
        w_up=np.ascontiguousarray(inp["w_up"][1:2]),
        w_down=np.ascontiguousarray(inp["w_down"][1:2]),
    )
    maps = []
    for c in cores:
        b = c // 4
        ckv = np.concatenate([lat[b * 4 + rp]["latc"] for rp in range(4)], axis=1)
        krr = np.concatenate([lat[b * 4 + rp]["latr"] for rp in range(4)], axis=1)
        m = dict(x1T=np.ascontiguousarray(lat[c]["x1T"]), consts=_consts_B(c, inp),
                 ckv_all=np.ascontiguousarray(ckv), kr_all=np.ascontiguousarray(krr))
        m.update(shared_B)
        maps.append(m)
    return maps


def maps_F(inp, cores):
    x = inp["x"].astype(np.float32, copy=False)
    w_in_c = inp["w_in_c"][0]
    kr = w_in_c[:, 1024:1088]
    w_kr = np.ascontiguousarray(np.concatenate([kr, kr[:, 32:64], kr[:, 0:32]], axis=1))
    w_uq = inp["w_uq"][0].reshape(512, 16, 192)
    w_ukv = inp["w_ukv"][0].reshape(512, 16, 256)
    w_att = np.ascontiguousarray(np.concatenate(
        [w_uq[:, :, 0:128], w_uq[:, :, 128:192], w_uq[:, :, 160:192], w_uq[:, :, 128:160], w_ukv], axis=2))
    shared = dict(
        w_in_ab=np.ascontiguousarray(inp["w_in_ab"][0]),
        w_sT=np.ascontiguousarray(np.transpose(inp["w_s"][0], (0, 2, 1))),
        w_pool=np.ascontiguousarray(inp["w_pool"][0]),
        w_out_ab=np.ascontiguousarray(inp["w_out_ab"][0]),
        w_gate=np.ascontiguousarray(inp["w_gate"]),
        w_up=np.ascontiguousarray(inp["w_up"]),
        w_down=np.ascontiguousarray(inp["w_down"]),
        w_ckv=np.ascontiguousarray(w_in_c[:, 512:1024]),
        w_kr=w_kr,
        w_cq=np.ascontiguousarray(w_in_c[:, 0:512]),
        w_att=w_att,
        w_out_c=np.ascontiguousarray(inp["w_out_c"][0]),
        gvbs=np.ascontiguousarray(np.broadcast_to(np.concatenate(
            [np.asarray(inp["g_v"][0], np.float32), np.asarray(inp["b_s"][0], np.float32).reshape(1024)])[None, :],
            (128, 2048))),
    )
    maps = []
    for c in cores:
        b, r, _ = _core_tiles(c)
        order = [rp for rp in range(4) if rp != r] + [r]
        xs_l, xh_l, ca_l, cs_l = [], [], [], []
        for rp in order:
            tiles = [4 * i + rp for i in range(NT)]
            xs = np.concatenate([x[b, g * 128:(g + 1) * 128, :] for g in tiles], axis=0)
            halo = np.zeros((128, D), np.float32)
            for i, g in enumerate(tiles):
                if g > 0:
                    halo[i * 16:(i + 1) * 16] = x[b, g * 128 - 16:g * 128, :]
            xs_l.append(xs.T)
            xh_l.append(halo.T)
            ca_l.append(_consts_A(c, inp, rank=rp))
            cs_l.append(np.concatenate(_rope_tables(c, rank=rp), axis=1))
        m = dict(xT=np.ascontiguousarray(np.stack(xs_l)), xhT=np.ascontiguousarray(np.stack(xh_l)),
                 constsA=np.ascontiguousarray(np.stack(ca_l)), cossin=np.ascontiguousarray(np.stack(cs_l)),
                 constsB=_consts_B(c, inp, order=order))
        m.update(shared)
        maps.append(m)
    return maps


def maps_G(inp, cores):
    base = maps_A(inp, cores)
    w_in_c = inp["w_in_c"][0]
    w_uq = inp["w_uq"][0].reshape(512, 16, 192)
    w_ukv = inp["w_ukv"][0].reshape(512, 16, 256)
    w_att = np.ascontiguousarray(np.concatenate(
        [w_uq[:, :, 0:128], w_uq[:, :, 128:192], w_uq[:, :, 160:192], w_uq[:, :, 128:160], w_ukv], axis=2))
    shared = dict(
        w_gate=np.ascontiguousarray(inp["w_gate"]),
        w_up=np.ascontiguousarray(inp["w_up"]),
        w_down=np.ascontiguousarray(inp["w_down"]),
        w_cq=np.ascontiguousarray(w_in_c[:, 0:512]),
        w_att=w_att,
        w_out_c=np.ascontiguousarray(inp["w_out_c"][0]),
    )
    maps = []
    for c, m in zip(cores, base):
        m = dict(m)
        m["xT"] = m["xT"][None]
        m["xhT"] = m["xhT"][None]
        m["constsA"] = m.pop("consts")[None]
        m["cossin"] = m["cossin"][None]
        m["constsB"] = _consts_B(c, inp)
        m.update(shared)
        maps.append(m)
    return maps


FUSED = True


def kernel(**inp):
    inp = _prep(inp)
    cores = list(range(NCORES))
    if FUSED:
        ncG = _get("G", build_G)
        res = run_bass_kernel_spmd(ncG, maps_G(inp, cores), core_ids=cores)
        results = res.results
    else:
        ncA = _get("A", build_A)
        resA = run_bass_kernel_spmd(ncA, maps_A(inp, cores), core_ids=cores)
        lat = {c: resA.results[c] for c in cores}
        ncB = _get("B", build_B)
        resB = run_bass_kernel_spmd(ncB, maps_B(inp, cores, lat), core_ids=cores)
        results = resB.results
    out = np.empty((2, 4096, D), np.float32)
    for c in cores:
        b, r, tiles = _core_tiles(c)
        y = results[c]["yT"].T
        for i, g in enumerate(tiles):
            out[b, g * 128:(g + 1) * 128, :] = y[i * 128:(i + 1) * 128]
    return out
```
